# Optimizing a Trainium2 kernel written in Bass

```python
import math
import jax, jax.numpy as jnp
from jax import lax
import numpy as np

D_MODEL = 1024
BATCH = 4
SEQ = 4096
DEPTH = 4
DEC_BATCH = 32
DEC_SEQ = 1
PAST_LEN = 8192
PAGE_SIZE = 128

N_HEADS = 8
HEAD_DIM = 64
KV_HEADS = 2
GROUP = N_HEADS // KV_HEADS
NSA_WIDTH = N_HEADS * HEAD_DIM
KV_WIDTH = KV_HEADS * HEAD_DIM
CMP_BLOCK = 32
SEL_BLOCK = 64
SEL_RATIO = SEL_BLOCK // CMP_BLOCK
TOP_K = 16
WINDOW = 512
Q_BLOCK = 128
N_NSA_BRANCH = 3
POOL_WIDTH = 512
POOL_GROUPS = 4
POOL_GROUP_WIDTH = POOL_WIDTH // POOL_GROUPS
POOL_WINDOWS = (2, 4, 8, 16)
POOL_BUF = 15
MEM_TOKENS = 256
MEM_HEADS = 4
MEM_HEAD_DIM = 128
MEM_WIDTH = MEM_HEADS * MEM_HEAD_DIM
N_BRANCH = 3
D_FF = ((8 * D_MODEL // 3 + 255) // 256) * 256
PROJ_WIDTH = NSA_WIDTH + 6 * KV_WIDTH + N_NSA_BRANCH * N_HEADS + POOL_WIDTH + MEM_WIDTH + N_BRANCH * D_MODEL
EPS = 1e-6
NEG = -1e30
BIG = 1e9

kernel_name = "nsa_pool_memory_hybrid_step"


def rmsnorm(x, g):
    xf = x.astype(jnp.float32)
    y = xf * lax.rsqrt(jnp.mean(xf * xf, axis=-1, keepdims=True) + EPS)
    return (y * g.astype(jnp.float32)).astype(x.dtype)


def masked_softmax(s, mask):
    s = jnp.where(mask, s.astype(jnp.float32), NEG)
    m = jnp.max(s, axis=-1, keepdims=True)
    p = jnp.exp(s - m) * mask
    return p / jnp.maximum(jnp.sum(p, axis=-1, keepdims=True), 1e-30)


def _split_points():
    sizes = (NSA_WIDTH,) + (KV_WIDTH,) * 6 + (N_NSA_BRANCH * N_HEADS, POOL_WIDTH, MEM_WIDTH, N_BRANCH * D_MODEL)
    pts, acc = [], 0
    for s in sizes[:-1]:
        acc += s
        pts.append(acc)
    return pts


def project_inputs(x, l, w):
    b, t = x.shape[:2]
    z = rmsnorm(x, w['attn_norm_g'][l]) @ w['w_in'][l]
    q, kc, vc, ks, vs, kw, vw, nsa_g, pool_u, mem_q, merge_g = jnp.split(z, _split_points(), axis=-1)
    kv = lambda a: a.reshape(b, t, KV_HEADS, HEAD_DIM)
    q = rmsnorm(q.reshape(b, t, KV_HEADS, GROUP, HEAD_DIM), w['nsa_q_g'][l])
    ks = rmsnorm(kv(ks), w['nsa_ks_g'][l])
    kw = rmsnorm(kv(kw), w['nsa_kw_g'][l])
    mem_q = rmsnorm(mem_q.reshape(b, t, MEM_HEADS, MEM_HEAD_DIM), w['mem_q_g'][l])
    nsa_kv = jnp.stack([kv(kc), kv(vc), ks, kv(vs)], axis=2)
    win_kv = jnp.stack([kw, kv(vw)], axis=2)
    return q, nsa_kv, win_kv, nsa_g, pool_u, mem_q, merge_g


def compress_blocks(k_raw, pe, wc):
    b, l = k_raw.shape[:2]
    nc = l // CMP_BLOCK
    kb = k_raw[:, :nc * CMP_BLOCK].reshape(b, nc, CMP_BLOCK, KV_HEADS, HEAD_DIM)
    pooled = jnp.mean(kb + pe[None, None, :, None, :], axis=2)
    return jnp.einsum('bckd,de->bcke', pooled, wc)


def nsa_cmp_sel(q, q_pos, kc_raw, vc_raw, ks, vs, l, w):
    b, t = q.shape[:2]
    L = ks.shape[1]
    scale = HEAD_DIM ** -0.5
    kc = rmsnorm(compress_blocks(kc_raw, w['cmp_pe_k'][l], w['cmp_wk'][l]), w['nsa_kc_g'][l])
    vc = compress_blocks(vc_raw, w['cmp_pe_v'][l], w['cmp_wv'][l])
    nc = kc.shape[1]
    c_end = jnp.arange(nc) * CMP_BLOCK + (CMP_BLOCK - 1)
    c_mask = c_end[None, :] <= q_pos[:, None]
    s = jnp.einsum('btkgd,bckd->bkgtc', q, kc) * scale
    p_cmp = masked_softmax(s, c_mask)
    o_cmp = jnp.einsum('bkgtc,bckd->btkgd', p_cmp.astype(vc.dtype), vc)
    ns = (L + SEL_BLOCK - 1) // SEL_BLOCK
    imp = jnp.sum(p_cmp, axis=2)
    imp = jnp.pad(imp, ((0, 0), (0, 0), (0, 0), (0, ns * SEL_RATIO - nc)))
    imp = imp.reshape(b, KV_HEADS, t, ns, SEL_RATIO).sum(-1)
    blk = jnp.arange(ns)[None, :]
    cur = q_pos[:, None] // SEL_BLOCK
    valid = blk * SEL_BLOCK <= q_pos[:, None]
    forced = (blk == 0) | (blk == cur) | (blk == cur - 1)
    score = jnp.where(valid, jnp.where(forced, BIG, imp), -BIG)
    k_sel = min(TOP_K, ns)
    _, idx = lax.top_k(score, k_sel)
    pad = ns * SEL_BLOCK - L
    to_blocks = lambda a: jnp.pad(a, ((0, 0), (0, pad), (0, 0), (0, 0))).reshape(b, ns, SEL_BLOCK, KV_HEADS, HEAD_DIM).transpose(0, 3, 1, 2, 4)
    ks_b = to_blocks(ks)
    vs_b = to_blocks(vs)
    tb = Q_BLOCK if t % Q_BLOCK == 0 else t
    nb = t // tb
    q_blocks = q.reshape(b, nb, tb, KV_HEADS, GROUP, HEAD_DIM).transpose(1, 0, 2, 3, 4, 5)
    idx_blocks = idx.reshape(b, KV_HEADS, nb, tb, k_sel).transpose(2, 0, 1, 3, 4)
    pos_blocks = q_pos.reshape(nb, tb)
    bi = jnp.arange(b)[:, None, None, None]
    hi = jnp.arange(KV_HEADS)[None, :, None, None]
    sel_offsets = jnp.arange(SEL_BLOCK)

    def sel_block(args):
        qb, ib, pb = args
        kg = ks_b[bi, hi, ib]
        vg = vs_b[bi, hi, ib]
        kpos = ib[..., None] * SEL_BLOCK + sel_offsets
        mask = (kpos <= pb[None, None, :, None, None]).reshape(b, KV_HEADS, 1, tb, k_sel * SEL_BLOCK)
        sc = jnp.einsum('btkgd,bktjsd->bkgtjs', qb, kg).reshape(b, KV_HEADS, GROUP, tb, k_sel * SEL_BLOCK) * scale
        pr = masked_softmax(sc, mask).reshape(b, KV_HEADS, GROUP, tb, k_sel, SEL_BLOCK)
        return jnp.einsum('bkgtjs,bktjsd->btkgd', pr.astype(vg.dtype), vg)

    o_sel = lax.map(sel_block, (q_blocks, idx_blocks, pos_blocks))
    o_sel = o_sel.transpose(1, 0, 2, 3, 4, 5).reshape(b, t, KV_HEADS, GROUP, HEAD_DIM)
    return o_cmp, o_sel


def gqa_window(q, k, v, q_pos, k_pos):
    mask = (k_pos[:, None, :] <= q_pos[:, :, None]) & (k_pos[:, None, :] > q_pos[:, :, None] - WINDOW)
    s = jnp.einsum('bntkgd,bnskd->bnkgts', q, k) * HEAD_DIM ** -0.5
    p = masked_softmax(s, mask[None, :, None, None])
    return jnp.einsum('bnkgts,bnskd->bntkgd', p.astype(v.dtype), v)


def window_prompt(q, kw, vw):
    b, t = q.shape[:2]
    nb = t // Q_BLOCK
    nw = WINDOW // Q_BLOCK
    band = lambda a: jnp.concatenate(
        [jnp.pad(a, ((0, 0), (WINDOW, 0), (0, 0), (0, 0))).reshape(b, nb + nw, Q_BLOCK, KV_HEADS, HEAD_DIM)[:, i:i + nb]
         for i in range(nw + 1)], axis=2)
    q_pos = jnp.arange(t).reshape(nb, Q_BLOCK)
    k_pos = jnp.arange(nb)[:, None] * Q_BLOCK - WINDOW + jnp.arange((nw + 1) * Q_BLOCK)[None, :]
    o = gqa_window(q.reshape(b, nb, Q_BLOCK, KV_HEADS, GROUP, HEAD_DIM), band(kw), band(vw), q_pos, k_pos)
    return o.reshape(b, t, KV_HEADS, GROUP, HEAD_DIM)


def pool_mix(u_ext, n_prev, pos, w_pool, pool_scale):
    b = u_ext.shape[0]
    t = pos.shape[0]
    uf = u_ext.astype(jnp.float32)
    cs = jnp.pad(jnp.cumsum(uf, axis=1), ((0, 0), (1, 0), (0, 0)))
    end = cs[:, n_prev + 1:]
    outs = []
    for gi, win in enumerate(POOL_WINDOWS):
        lo, hi = gi * POOL_GROUP_WIDTH, (gi + 1) * POOL_GROUP_WIDTH
        start = cs[:, n_prev + 1 - win:n_prev + 1 - win + t, lo:hi]
        cnt = jnp.minimum(pos + 1, win).astype(jnp.float32)[None, :, None]
        outs.append((end[..., lo:hi] - start) / cnt)
    d = (jnp.concatenate(outs, axis=-1) - uf[:, n_prev:]).astype(u_ext.dtype)
    d = d.reshape(b, t, POOL_GROUPS, POOL_GROUP_WIDTH)
    y = jnp.einsum('btgc,gce->btge', d, w_pool).reshape(b, t, POOL_WIDTH)
    return y * pool_scale


def memory_kv(mem, l, w):
    b, m = mem.shape[:2]
    kv = (rmsnorm(mem, w['mem_norm_g'][l]) @ w['w_mem_kv'][l]).reshape(b, m, 2, MEM_HEADS, MEM_HEAD_DIM)
    return jnp.stack([rmsnorm(kv[:, :, 0], w['mem_k_g'][l]), kv[:, :, 1]], axis=2)


def mem_attend(q, mk, mv):
    s = jnp.einsum('bthd,bmhd->bhtm', q, mk).astype(jnp.float32) * MEM_HEAD_DIM ** -0.5
    p = jax.nn.softmax(s, axis=-1)
    return jnp.einsum('bhtm,bmhd->bthd', p.astype(mv.dtype), mv)


def merge_and_ffn(x, l, w, o_cmp, o_sel, o_win, nsa_g, pool_y, mem_o, merge_g):
    b, t = x.shape[:2]
    g = jax.nn.sigmoid(nsa_g.astype(jnp.float32)).reshape(b, t, KV_HEADS, GROUP, N_NSA_BRANCH).astype(x.dtype)
    o_nsa = (g[..., 0, None] * o_cmp + g[..., 1, None] * o_sel + g[..., 2, None] * o_win).reshape(b, t, NSA_WIDTH)
    mg = jax.nn.sigmoid(merge_g.astype(jnp.float32)).reshape(b, t, N_BRANCH, D_MODEL).astype(x.dtype)
    h = (mg[:, :, 0] * (o_nsa @ w['w_up_nsa'][l])
         + mg[:, :, 1] * (pool_y @ w['w_up_pool'][l])
         + mg[:, :, 2] * (mem_o.reshape(b, t, MEM_WIDTH) @ w['w_up_mem'][l]))
    x = x + h @ w['w_out'][l]
    gate, up = jnp.split(rmsnorm(x, w['ffn_norm_g'][l]) @ w['w_gate_up'][l], 2, axis=-1)
    return x + (jax.nn.silu(gate) * up) @ w['w_down'][l]


def prompt_layer(x, mem, l, w):
    t = x.shape[1]
    pos = jnp.arange(t)
    q, nsa_kv, win_kv, nsa_g, pool_u, mem_q, merge_g = project_inputs(x, l, w)
    o_cmp, o_sel = nsa_cmp_sel(q, pos, nsa_kv[:, :, 0], nsa_kv[:, :, 1], nsa_kv[:, :, 2], nsa_kv[:, :, 3], l, w)
    o_win = window_prompt(q, win_kv[:, :, 0], win_kv[:, :, 1])
    u_ext = jnp.pad(pool_u, ((0, 0), (POOL_BUF, 0), (0, 0)))
    pool_y = pool_mix(u_ext, POOL_BUF, pos, w['w_pool'][l], w['pool_scale'][l])
    mkv = memory_kv(mem, l, w)
    mem_o = mem_attend(mem_q, mkv[:, :, 0], mkv[:, :, 1])
    y = merge_and_ffn(x, l, w, o_cmp, o_sel, o_win, nsa_g, pool_y, mem_o, merge_g)
    wb = min(WINDOW, t)
    return y, nsa_kv, win_kv[:, t - wb:], pool_u[:, t - POOL_BUF:], mkv


def sample_layer(x, l, w, nsa_pool_l, page_table, win_buf, pool_buf, mkv):
    b, t = x.shape[:2]
    pos = PAST_LEN + jnp.arange(t)
    q, nsa_kv, win_kv, nsa_g, pool_u, mem_q, merge_g = project_inputs(x, l, w)
    past = nsa_pool_l[page_table].reshape(b, -1, 4, KV_HEADS, HEAD_DIM)
    full = jnp.concatenate([past, nsa_kv], axis=1)
    o_cmp, o_sel = nsa_cmp_sel(q, pos, full[:, :, 0], full[:, :, 1], full[:, :, 2], full[:, :, 3], l, w)
    wkv = jnp.concatenate([win_buf, win_kv], axis=1)
    wb = win_buf.shape[1]
    k_pos = PAST_LEN - wb + jnp.arange(wb + t)
    o_win = gqa_window(q[:, None], wkv[:, None, :, 0], wkv[:, None, :, 1], pos[None], k_pos[None])[:, 0]
    u_ext = jnp.concatenate([pool_buf, pool_u], axis=1)
    pool_y = pool_mix(u_ext, POOL_BUF, pos, w['w_pool'][l], w['pool_scale'][l])
    mem_o = mem_attend(mem_q, mkv[:, :, 0], mkv[:, :, 1])
    y = merge_and_ffn(x, l, w, o_cmp, o_sel, o_win, nsa_g, pool_y, mem_o, merge_g)
    return y, nsa_kv, wkv[:, t:], u_ext[:, t:]


def setup_inputs(seed: int = 0) -> dict:
    key = jax.random.key(seed)
    ks = jax.random.split(key, 40)
    f32 = jnp.float32
    nrm = lambda k, shape, sc: jax.random.normal(k, shape, f32) * sc
    gain = lambda k, shape: 1.0 + 0.05 * jax.random.normal(k, shape, f32)
    n_pages = PAST_LEN // PAGE_SIZE
    n_used = DEC_BATCH * n_pages
    n_phys = n_used + n_used // 4
    win_buf = min(WINDOW, PAST_LEN)
    page_table = jax.random.permutation(ks[0], n_phys)[:n_used].reshape(DEC_BATCH, n_pages).astype(jnp.int32)
    return {
        'x_prompt': nrm(ks[1], (BATCH, SEQ, D_MODEL), 1.0),
        'x_sample': nrm(ks[2], (DEC_BATCH, DEC_SEQ, D_MODEL), 1.0),
        'mem_prompt': nrm(ks[3], (BATCH, MEM_TOKENS, D_MODEL), 1.0),
        'cache_nsa_kv': nrm(ks[4], (DEPTH, n_phys, PAGE_SIZE, 4, KV_HEADS, HEAD_DIM), 1.0),
        'cache_win_kv': nrm(ks[5], (DEPTH, DEC_BATCH, win_buf, 2, KV_HEADS, HEAD_DIM), 1.0),
        'state_pool': nrm(ks[6], (DEPTH, DEC_BATCH, POOL_BUF, POOL_WIDTH), 1.0),
        'cache_mem_kv': nrm(ks[7], (DEPTH, DEC_BATCH, MEM_TOKENS, 2, MEM_HEADS, MEM_HEAD_DIM), 1.0),
        'page_table': page_table,
        'attn_norm_g': gain(ks[8], (DEPTH, D_MODEL)),
        'w_in': nrm(ks[9], (DEPTH, D_MODEL, PROJ_WIDTH), D_MODEL ** -0.5),
        'nsa_q_g': gain(ks[10], (DEPTH, HEAD_DIM)),
        'nsa_kc_g': gain(ks[11], (DEPTH, HEAD_DIM)),
        'nsa_ks_g': gain(ks[12], (DEPTH, HEAD_DIM)),
        'nsa_kw_g': gain(ks[13], (DEPTH, HEAD_DIM)),
        'cmp_pe_k': nrm(ks[14], (DEPTH, CMP_BLOCK, HEAD_DIM), 0.1),
        'cmp_pe_v': nrm(ks[15], (DEPTH, CMP_BLOCK, HEAD_DIM), 0.1),
        'cmp_wk': nrm(ks[16], (DEPTH, HEAD_DIM, HEAD_DIM), HEAD_DIM ** -0.5),
        'cmp_wv': nrm(ks[17], (DEPTH, HEAD_DIM, HEAD_DIM), HEAD_DIM ** -0.5),
        'w_pool': nrm(ks[18], (DEPTH, POOL_GROUPS, POOL_GROUP_WIDTH, POOL_GROUP_WIDTH), POOL_GROUP_WIDTH ** -0.5),
        'pool_scale': gain(ks[19], (DEPTH, POOL_WIDTH)),
        'mem_norm_g': gain(ks[20], (DEPTH, D_MODEL)),
        'w_mem_kv': nrm(ks[21], (DEPTH, D_MODEL, 2 * MEM_WIDTH), D_MODEL ** -0.5),
        'mem_q_g': gain(ks[22], (DEPTH, MEM_HEAD_DIM)),
        'mem_k_g': gain(ks[23], (DEPTH, MEM_HEAD_DIM)),
        'w_up_nsa': nrm(ks[24], (DEPTH, NSA_WIDTH, D_MODEL), NSA_WIDTH ** -0.5),
        'w_up_pool': nrm(ks[25], (DEPTH, POOL_WIDTH, D_MODEL), POOL_WIDTH ** -0.5),
        'w_up_mem': nrm(ks[26], (DEPTH, MEM_WIDTH, D_MODEL), MEM_WIDTH ** -0.5),
        'w_out': nrm(ks[27], (DEPTH, D_MODEL, D_MODEL), D_MODEL ** -0.5),
        'ffn_norm_g': gain(ks[28], (DEPTH, D_MODEL)),
        'w_gate_up': nrm(ks[29], (DEPTH, D_MODEL, 2 * D_FF), D_MODEL ** -0.5),
        'w_down': nrm(ks[30], (DEPTH, D_FF, D_MODEL), D_FF ** -0.5),
    }


def reference(x_prompt, x_sample, mem_prompt, cache_nsa_kv, cache_win_kv, state_pool, cache_mem_kv, page_table,
              attn_norm_g, w_in, nsa_q_g, nsa_kc_g, nsa_ks_g, nsa_kw_g, cmp_pe_k, cmp_pe_v, cmp_wk, cmp_wv,
              w_pool, pool_scale, mem_norm_g, w_mem_kv, mem_q_g, mem_k_g, w_up_nsa, w_up_pool, w_up_mem, w_out,
              ffn_norm_g, w_gate_up, w_down):
    w = {'attn_norm_g': attn_norm_g, 'w_in': w_in, 'nsa_q_g': nsa_q_g, 'nsa_kc_g': nsa_kc_g,
         'nsa_ks_g': nsa_ks_g, 'nsa_kw_g': nsa_kw_g, 'cmp_pe_k': cmp_pe_k, 'cmp_pe_v': cmp_pe_v,
         'cmp_wk': cmp_wk, 'cmp_wv': cmp_wv, 'w_pool': w_pool, 'pool_scale': pool_scale,
         'mem_norm_g': mem_norm_g, 'w_mem_kv': w_mem_kv, 'mem_q_g': mem_q_g, 'mem_k_g': mem_k_g,
         'w_up_nsa': w_up_nsa, 'w_up_pool': w_up_pool, 'w_up_mem': w_up_mem, 'w_out': w_out,
         'ffn_norm_g': ffn_norm_g, 'w_gate_up': w_gate_up, 'w_down': w_down}
    xp, xs = x_prompt, x_sample
    nsa_p, nsa_s, win_p, win_s, pool_p, pool_s, mem_p = [], [], [], [], [], [], []
    for l in range(DEPTH):
        xp, a, bw, c, d = prompt_layer(xp, mem_prompt, l, w)
        nsa_p.append(a); win_p.append(bw); pool_p.append(c); mem_p.append(d)
        xs, a, bw, c = sample_layer(xs, l, w, cache_nsa_kv[l], page_table, cache_win_kv[l], state_pool[l], cache_mem_kv[l])
        nsa_s.append(a); win_s.append(bw); pool_s.append(c)
    return (xp, xs, jnp.stack(nsa_p), jnp.stack(nsa_s), jnp.stack(win_p), jnp.stack(win_s),
            jnp.stack(pool_p), jnp.stack(pool_s), jnp.stack(mem_p))
```

```python
import os
import numpy as np
import concourse.bass as bass
import concourse.mybir as mybir
from concourse.alu_op_type import AluOpType as ALU
from concourse.bass_utils import run_bass_kernel_spmd

F32 = mybir.dt.float32
BF16 = mybir.dt.bfloat16
I32 = mybir.dt.int32
AF = mybir.ActivationFunctionType
AX = mybir.AxisListType

ENGS = ("pe", "act", "dve", "pool", "sp")
SEM_LIMIT = 12000
N_DMA_SEMS = 8

D = 1024
PW = 5400
DFF = 2816
NSEQ = 4
EPS = 1e-6
BIGM = 30000.0


class _Stop(Exception):
    pass


KOPS = int(os.environ.get("KOPS", "0"))
_opc = [0]


def _count():
    _opc[0] += 1
    if os.environ.get("KWHO") and _opc[0] == int(os.environ["KWHO"]):
        import traceback
        print("WHO", _opc[0], [f"{f.lineno}:{f.line}" for f in traceback.extract_stack(limit=6)[:-2]])
    if KOPS and _opc[0] >= KOPS:
        raise _Stop()


class Dep:
    __slots__ = ("w", "r")

    def __init__(self):
        self.w = None
        self.r = []


class FW:
    def __init__(self, nc):
        self.nc = nc
        self.ops = {e: [] for e in ENGS}
        self.nops = {e: 0 for e in ENGS}
        self.signaled = {e: set() for e in ENGS}
        self.waited = {e: {} for e in ENGS}
        self.dma_count = {}
        self.dbg = {}
        self.dma_rr = {e: 0 for e in ENGS}

    def _need(self, e, opid):
        if opid is None:
            return None
        if opid[0] == "c":
            _, f, k = opid
            if f == e and e == "pe":
                return None
            if self.waited[e].get(("c", f), 0) >= k:
                return None
            self.waited[e][("c", f)] = k
            self.signaled[f].add(k)
            return opid
        _, q, j, m = opid
        if self.waited[e].get(("d", q, j), 0) >= m:
            return None
        self.waited[e][("d", q, j)] = m
        return opid

    def _collect(self, e, reads, writes):
        waits = []
        for d in reads:
            w = self._need(e, d.w)
            if w:
                waits.append(w)
        for d in writes:
            w = self._need(e, d.w)
            if w:
                waits.append(w)
            for r in d.r:
                w = self._need(e, r)
                if w:
                    waits.append(w)
        return waits

    def _mark(self, opid, reads, writes):
        for d in reads:
            d.r.append(opid)
            if len(d.r) > 48:
                last = {}
                for o in d.r:
                    key = o[:2] if o[0] == "c" else o[:3]
                    if key not in last or o[-1] > last[key][-1]:
                        last[key] = o
                d.r = list(last.values())
        for d in writes:
            d.w = opid
            d.r = []

    def op(self, e, fn, reads=(), writes=()):
        _count()
        for w in self._collect(e, reads, writes):
            self.ops[e].append(("wait", w))
        self.nops[e] += 1
        k = self.nops[e]
        if os.environ.get("KDBG"):
            import traceback
            st = traceback.extract_stack(limit=6)
            self.dbg[(e, k)] = " <- ".join(f"{f.lineno}" for f in st[:-1])
        self.ops[e].append(("op", fn, k))
        self._mark(("c", e, k), reads, writes)

    def dma(self, q, fn, reads=(), writes=()):
        _count()
        waits = self._collect(q, reads, writes)
        j = self.dma_rr[q]
        self.dma_rr[q] = (j + 1) % N_DMA_SEMS
        m_prev = self.dma_count.get((q, j), 0)
        if m_prev > 0:
            w = self._need(q, ("d", q, j, m_prev))
            if w:
                waits.append(w)
        for w in waits:
            self.ops[q].append(("wait", w))
        m = m_prev + 1
        self.dma_count[(q, j)] = m
        self.ops[q].append(("dma", fn, j))
        self._mark(("d", q, j, m), reads, writes)

    def barrier(self):
        ids = {}
        for e in ("act", "dve", "pool"):
            self.op(e, self.bar_ops[e])
            ids[e] = ("c", e, self.nops[e])
        if self.nops["pe"] > 0:
            ids["pe"] = ("c", "pe", self.nops["pe"])
        dmas = [("d", q, j, m) for (q, j), m in self.dma_count.items()]
        for e in ENGS:
            for f, o in ids.items():
                if f != e:
                    w = self._need(e, o)
                    if w:
                        self.ops[e].append(("wait", w))
            for o in dmas:
                w = self._need(e, o)
                if w:
                    self.ops[e].append(("wait", w))

    def emit(self):
        nc = self.nc
        sigval, sems = {}, {}
        for e in ENGS:
            epoch, cnt = 0, 0
            for k in range(1, self.nops[e] + 1):
                if k in self.signaled[e]:
                    if cnt >= SEM_LIMIT:
                        epoch += 1
                        cnt = 0
                    cnt += 1
                    if (e, epoch) not in sems:
                        sems[(e, epoch)] = nc.alloc_semaphore(f"s_{e}_{epoch}")
                    sigval[(e, k)] = (sems[(e, epoch)], cnt)
                    if os.environ.get("KDBG") and e == os.environ.get("KDBG_E", "pe"):
                        print("SIG", e, epoch, cnt, "op", k, self.dbg.get((e, k)))
        dsems = {key: nc.alloc_semaphore(f"d_{key[0]}_{key[1]}") for key in self.dma_count}

        def run(E, e):
            for item in self.ops[e]:
                if item[0] == "wait":
                    w = item[1]
                    if w[0] == "c":
                        s, v = sigval[(w[1], w[2])]
                        E.wait_ge(s, v)
                    else:
                        E.wait_ge(dsems[(w[1], w[2])], 16 * w[3])
                elif item[0] == "op":
                    ins = item[1](E)
                    if item[2] in self.signaled[e]:
                        ins.then_inc(sigval[(e, item[2])][0], 1)
                else:
                    item[1](E).then_inc(dsems[(e, item[2])], 16)

        with nc.Block() as block:
            @block.sync
            def _(E):
                run(E, "sp")

            @block.gpsimd
            def _(E):
                run(E, "pool")

            @block.vector
            def _(E):
                run(E, "dve")

            @block.scalar
            def _(E):
                run(E, "act")

            @block.tensor
            def _(E):
                run(E, "pe")


C_OFF = {}


def make_consts(NT):
    cols = []
    off = [0]

    def add(name, a):
        a = np.asarray(a, np.float32)
        assert a.shape[0] == 128
        a = a.reshape(128, -1)
        C_OFF[name] = (off[0], a.shape[1])
        off[0] += a.shape[1]
        cols.append(a)

    p = np.arange(128)
    add("ident", np.eye(128))
    add("tric", (p[:, None] <= p[None, :]) * 1.0)
    add("trib", (p[:, None] > p[None, :]) * 1.0)
    add("selall", (p[:, None, None] == np.arange(4)[None, :, None]) * np.ones((1, 1, 128)))
    add("ones", np.ones((128, 1)))
    add("pm", (p[:, None] // 32 == np.arange(4)[None, :]) / 32.0)
    u = (p + 1) // 32
    xx = np.arange(252)
    add("tcmp", (xx[None, :] < 124 + u[:, None]) * 1.0)
    xs = np.arange(126)
    rel = xs[None, :] - 62 - (p[:, None] >= 64)
    add("bsel", np.where((rel == 0) | (rel == -1), 1e9, np.where(rel > 0, -1e9, 0.0)))
    wins = (2, 4, 8, 16)
    acur = np.zeros((128, 4, 128)); aprev = np.zeros((128, 4, 128)); afirst = np.zeros((128, 4, 128))
    for g, w in enumerate(wins):
        for t in range(128):
            for s_ in range(t - w + 1, t + 1):
                if s_ >= 0:
                    acur[s_, g, t] += 1.0 / w
                    afirst[s_, g, t] += 1.0 / min(t + 1, w)
                else:
                    aprev[128 + s_, g, t] += 1.0 / w
            acur[t, g, t] -= 1.0
            afirst[t, g, t] -= 1.0
    add("acur", acur); add("aprev", aprev); add("afirst", afirst)
    ast = np.zeros((128, 4))
    for g, w in enumerate(wins):
        for j in range(15):
            if j >= 15 - (w - 1):
                ast[j, g] = 1.0 / w
    add("ast", ast)
    anew = np.zeros((128, 4, 4))
    for s_ in range(4):
        for g, w in enumerate(wins):
            anew[s_, s_, g] = 1.0 / w - 1.0
    add("anew", anew)
    add("rowsel", (p[:, None] == np.arange(4)[None, :]) * 1.0)
    add("g8", ((p[:, None] // 4) == np.arange(2)[None, :]) * (p[:, None] < 8))
    inde = np.zeros((128, 2, 2, 128))
    for kv in range(2):
        for par in range(2):
            inde[kv, kv, par, :] = (np.arange(128) // 64 == par)
    add("inde", inde)
    add("sr", np.ones((128, 1, 1)) * (np.arange(4)[None, :, None] == np.arange(128)[None, None, :]))
    add("wmask", (p[:, None] != 0) * 1.0)
    add("npad", np.maximum(0.0, 511.0 - (128.0 * np.arange(4)[None, :] + p[:, None])))
    add("iota", p[:, None] * 1.0)
    return np.concatenate(cols, axis=1)


def build(NL, NT, NPG, NPHYS, CW):
    S = NT * 128
    NTT = NT + 1
    NB = 4 * NT
    NBS = 4 * NPG
    CB = min(128, NBS)
    NCK = NBS // CB
    A1W = 4608
    nc = bass.Bass("TRN2", target_bir_lowering=False)
    fw = FW(nc)

    def din(name, shape, dt=F32):
        return nc.dram_tensor(name, shape, dt, kind="ExternalInput").ap()

    def dout(name, shape):
        return nc.dram_tensor(name, shape, F32, kind="ExternalOutput").ap()

    xin = din("xin", [NTT * 128, D]); memin = din("memin", [256, D])
    cnsa = din("cnsa", [NL, NPHYS * 128, 512]); cwin = din("cwin", [NL, NSEQ, 512, 256])
    spool = din("spool", [NL, NSEQ, 15, 512]); cmem = din("cmem", [NL, NSEQ, 256, 1024])
    ptab = din("ptab", [1, NSEQ * NPG], I32)
    consts = din("consts", [128, CW]); e64 = din("e64", [64, S])
    w_attn_g = din("attn_norm_g", [NL, D]); w_in = din("w_in", [NL, D, PW])
    w_qg = din("nsa_q_g", [NL, 64]); w_kcg = din("nsa_kc_g", [NL, 64]); w_ksg = din("nsa_ks_g", [NL, 64]); w_kwg = din("nsa_kw_g", [NL, 64])
    w_pek = din("cmp_pe_k", [NL, 32, 64]); w_pev = din("cmp_pe_v", [NL, 32, 64])
    w_cwk = din("cmp_wk", [NL, 64, 64]); w_cwv = din("cmp_wv", [NL, 64, 64])
    w_pool = din("w_pool", [NL, 4, 128, 128]); w_psc = din("pool_scale", [NL, 512])
    w_memg = din("mem_norm_g", [NL, D]); w_memkv = din("w_mem_kv", [NL, D, 1024])
    w_mqg = din("mem_q_g", [NL, 128]); w_mkg = din("mem_k_g", [NL, 128])
    w_upn = din("w_up_nsa", [NL, 512, D]); w_upp = din("w_up_pool", [NL, 512, D]); w_upm = din("w_up_mem", [NL, 512, D])
    w_out = din("w_out", [NL, D, D]); w_ffng = din("ffn_norm_g", [NL, D])
    w_gu = din("w_gate_up", [NL, D, 2 * DFF]); w_dn = din("w_down", [NL, DFF, D])

    o_y = dout("o_y", [NTT * 128, D]); o_nsa = dout("o_nsa", [NL, NTT * 128, 512])
    o_winp = dout("o_winp", [NL, 512, 256]); o_wins = dout("o_wins", [NL, NSEQ, 512, 256])
    o_poolp = dout("o_poolp", [NL, 15, 512]); o_pools = dout("o_pools", [NL, NSEQ, 15, 512])
    o_memkv = dout("o_memkv", [NL, 256, 1024])
    outdeps = []

    xs = nc.dram_tensor("xs", [NTT * 128, D], F32, kind="Internal").ap()
    a1s = nc.dram_tensor("a1s", [NTT, 128, A1W], BF16, kind="Internal").ap()
    xs_d = [Dep() for _ in range(NTT)]
    a1s_d = [Dep() for _ in range(NTT)]
    a1s_d2 = [Dep() for _ in range(NTT)]

    class Tl:
        def __init__(self, h, n=1):
            self.h = h
            self.d = Dep()
            self.ds = [Dep() for _ in range(n)]

        def __getitem__(self, k):
            return self.h[k]

    def op(e, fn, R=(), W=()):
        fw.op(e, fn, [t.d if isinstance(t, Tl) else t for t in R], [t.d if isinstance(t, Tl) else t for t in W])

    def dma(q, out, in_, R=(), W=()):
        fw.dma(q, lambda E: E.dma_start(out=out, in_=in_), [t.d if isinstance(t, Tl) else t for t in R],
               [t.d if isinstance(t, Tl) else t for t in W])

    def dma_nc(q, out, in_, R=(), W=()):
        fw.dma(q, lambda E: E.dma_start(out=out, in_=in_, allow_slow_non_contiguous=True), [t.d if isinstance(t, Tl) else t for t in R],
               [t.d if isinstance(t, Tl) else t for t in W])

    def outdma(q, out, in_, R=()):
        d = Dep()
        outdeps.append(d)
        dma(q, out, in_, R=R, W=[d])

    def mm(ps, lhsT, rhs, start, stop, R, W):
        if os.environ.get("KNOF32") and lhsT.dtype == F32:
            return
        op("pe", lambda E: E.matmul(ps, lhsT=lhsT, rhs=rhs, start=start, stop=stop), R, W)

    def tr(ps, in_, idn, R, W):
        op("pe", lambda E: E.transpose(out=ps, in_=in_, identity=idn), R, W)

    def act(e, out, in_, func=AF.Copy, scale=None, accum=None, R=(), W=()):
        kw = {}
        if scale is not None:
            kw["scale"] = scale
        if accum is not None:
            kw["accum_out"] = accum
        op("act", lambda E: E.activation(out=out, in_=in_, func=func, **kw), R, W)

    def cp(e, out, in_, R, W):
        if e == "act":
            op("act", lambda E: E.activation(out=out, in_=in_, func=AF.Copy), R, W)
        elif e == "dve" and out.dtype == in_.dtype and out.dtype == BF16 and not os.environ.get("KCPRAW"):
            op(e, lambda E: E.tensor_scalar(out=out, in0=in_, scalar1=1.0, scalar2=None, op0=ALU.mult), R, W)
        else:
            op(e, lambda E: E.tensor_copy(out=out, in_=in_), R, W)

    def ts(e, out, in0, s1, s2, op0, op1=None, R=(), W=()):
        if op1 is None:
            op(e, lambda E: E.tensor_scalar(out=out, in0=in0, scalar1=s1, scalar2=None, op0=op0), R, W)
        else:
            op(e, lambda E: E.tensor_scalar(out=out, in0=in0, scalar1=s1, scalar2=s2, op0=op0, op1=op1), R, W)

    def tt(e, out, in0, in1, o, R, W):
        op(e, lambda E: E.tensor_tensor(out=out, in0=in0, in1=in1, op=o), R, W)

    def stt(out, in0, sc, in1, op0, op1, R, W):
        op("dve", lambda E: E.scalar_tensor_tensor(out=out, in0=in0, scalar=sc, in1=in1, op0=op0, op1=op1), R, W)

    def red(out, in_, R, W, o=ALU.add):
        op("dve", lambda E: E.tensor_reduce(out=out, in_=in_, axis=AX.X, op=o), R, W)

    def memset(e, ap, v, W):
        op(e, lambda E: E.memset(ap, v), [], W)

    def recip(out, in_, R, W):
        op("dve", lambda E: E.reciprocal(out=out, in_=in_), R, W)

    def vmax(out, in_, R, W):
        op("dve", lambda E: E.max(out=out, in_=in_), R, W)

    def mrep(out, rep, vals, imm, R, W):
        op("dve", lambda E: E.match_replace(out=out, in_to_replace=rep, in_values=vals, imm_value=imm), R, W)

    def ttr(out, in0, in1, accum, R, W):
        op("dve", lambda E: E.scalar_tensor_tensor(out=out, in0=in0, scalar=1.0, in1=in1, op0=ALU.mult, op1=ALU.mult, accum_out=accum), R, W)

    def rsqrt_chain(v, R_extra=()):
        act("act", v[:], v[:], AF.Sqrt, R=[v], W=[v])
        recip(v[:], v[:], [v], [v])

    from contextlib import ExitStack
    root = ExitStack()

    _uid = [0]

    def sb(stack, name, shape, dt=F32, n=1):
        _uid[0] += 1
        return Tl(stack.enter_context(nc.sbuf_tensor(f"{name}_{_uid[0]}", shape, dt)), n)

    def pst(stack, name, shape, dt=F32):
        _uid[0] += 1
        return Tl(stack.enter_context(nc.psum_tensor(f"{name}_{_uid[0]}", shape, dt)))

    cf = sb(root, "cf", [128, CW])
    CBW = 897
    cb = sb(root, "cb", [128, CBW], BF16)
    dma("sp", cf[:], consts[:, :], W=[cf])
    cp("dve", cb[:], cf[:, 0:CBW], [cf], [cb])

    def CF(name, sub=None):
        o, n = C_OFF[name]
        return cf.h[:, o:o + n]

    def CB_(name):
        o, n = C_OFF[name]
        return cb.h[:, o:o + n]

    identf = CF("ident"); identb = CB_("ident")
    pmf = CF("pm")
    tric_b = CB_("tric"); trib_b = CB_("trib")
    tcmp = CF("tcmp"); bsel = CF("bsel")
    acur = CF("acur").rearrange("p (g t) -> p g t", g=4); aprev = CF("aprev").rearrange("p (g t) -> p g t", g=4)
    afirst = CF("afirst").rearrange("p (g t) -> p g t", g=4)
    ast = CF("ast"); anew = CF("anew").rearrange("p (s g) -> p s g", s=4)
    selall_b = CB_("selall").rearrange("p (s m) -> p s m", s=4)
    rowsel = CF("rowsel"); g8 = CF("g8")
    inde = CF("inde").rearrange("p (a b k) -> p a b k", a=2, b=2)
    sr_f = CF("sr").rearrange("p (s m) -> p s m", s=4)
    npad = CF("npad")
    wmask = CF("wmask"); iota = CF("iota"); onesf = CF("ones"); onesb = CB_("ones")

    pti = sb(root, "pti", [128, NSEQ * NPG], I32)
    ptf = sb(root, "ptf", [128, NSEQ * NPG], F32)
    pidx = sb(root, "pidx", [128, NL, NSEQ * NPG], I32)
    ptf2 = sb(root, "ptf2", [128, NSEQ * NPG], F32)
    dma("sp", pti[:], ptab[0:1, :].to_broadcast([128, NSEQ * NPG]), W=[pti])
    cp("dve", ptf[:], pti[:], [pti], [ptf])
    ts("dve", ptf[:], ptf[:], 128.0, iota[:, 0:1], ALU.mult, ALU.add, R=[ptf, cf], W=[ptf])
    for l_ in range(NL):
        ts("dve", ptf2[:], ptf[:], float(l_ * NPHYS * 128), None, ALU.add, R=[ptf], W=[ptf2])
        cp("dve", pidx[:, l_, :], ptf2[:], [ptf2], [pidx])
    cnsa_flat = cnsa.rearrange("l r c -> (l r) c")

    bscr = sb(root, "bscr", [128, 4])
    memset("dve", bscr[:], 0.0, [bscr])
    _b0, _b1, _b2 = bscr[0:1, 0:1], bscr[0:1, 1:2], bscr[0:1, 2:3]
    fw.bar_ops = {"pe": lambda E: E.nop(), "sp": lambda E: E.nop(),
                  "act": lambda E: E.activation(out=_b0, in_=_b0, func=AF.Copy),
                  "dve": lambda E: E.tensor_copy(out=_b1, in_=_b1),
                  "pool": lambda E: E.tensor_copy(out=_b2, in_=_b2)}
    zerob = sb(root, "zerob", [128, 260], BF16)
    memset("pool", zerob[:], 0.0, [zerob])
    nsag = sb(root, "nsag", [128, NTT, 24])
    sampkv = sb(root, "sampkv", [128, 512])
    rr = [0]

    def alt(a="act", b="dve"):
        rr[0] += 1
        return a if rr[0] % 2 else b

    import os
    _stop = float(os.environ.get("KSTOP", "99"))

    KCUT = int(os.environ.get("KCUT", "99"))

    def chk(n):
        if n >= _stop:
            raise _Stop()

    try:
      chk(1)
      for l in range(NL):
          xsrc = xin if l == 0 else xs
          lays = ExitStack()
          lay = ExitStack()
          gq = sb(lays, "gq", [128, 64]); gkc = sb(lays, "gkc", [128, 64]); gks = sb(lays, "gks", [128, 64]); gkw = sb(lays, "gkw", [128, 64])
          gmq = sb(lays, "gmq", [128, 128]); gmk = sb(lays, "gmk", [128, 128])
          pemT = sb(lays, "pemT", [128, 2])
          wkbd = sb(lays, "wkbd", [128, 2, 128])
          kselT = sb(lay, "kselT", [128, 2, S], BF16, NT)
          vsel = sb(lay, "vsel", [128, NT, 2, 65], BF16, NT)
          kwinT = sb(lay, "kwinT", [64, 2, S], BF16, NT)
          vwin = sb(lay, "vwin", [128, NT, 2, 65], BF16, NT)
          kcT2 = sb(lay, "kcT2", [128, 2, 128], BF16)
          vcaug = sb(lay, "vcaug", [128, 2, 65], BF16)
          mkT = sb(lay, "mkT", [128, 4, 256], BF16)
          mva = sb(lay, "mva", [128, 2, 4, 129], BF16)
          for t_, src in ((gq, w_qg), (gkc, w_kcg), (gks, w_ksg), (gkw, w_kwg)):
              dma("sp", t_[:], src[l:l + 1, :].to_broadcast([128, 64]), W=[t_])
          for t_, src in ((gmq, w_mqg), (gmk, w_mkg)):
              dma("sp", t_[:], src[l:l + 1, :].to_broadcast([128, 128]), W=[t_])
          ts("dve", gq[:], gq[:], 0.125, None, ALU.mult, R=[gq], W=[gq])
          ts("dve", gmq[:], gmq[:], 128 ** -0.5, None, ALU.mult, R=[gmq], W=[gmq])
          dma("pool", kselT[64:128, 0, :], e64[:, :], W=kselT.ds + [kselT])
          dma("pool", kselT[64:128, 1, :], e64[:, :], W=kselT.ds + [kselT])
          memset("pool", vsel[:, :, :, 64:65], 1.0, vsel.ds + [vsel])
          memset("pool", vwin[:, :, :, 64:65], 1.0, vwin.ds + [vwin])
          memset("pool", vcaug[:, :, 64:65], 1.0, [vcaug])
          memset("pool", mva[:, :, :, 128:129], 1.0, [mva])
          memset("pool", wkbd[:], 0.0, [wkbd])
          for a_, src in ((0, w_cwk), (1, w_cwv)):
              dma("sp", wkbd[0:64, a_, 0:64], src[l], W=[wkbd])
              dma("sp", wkbd[64:128, a_, 64:128], src[l], W=[wkbd])

          p1 = ExitStack()
          win_sb = sb(p1, "win_sb", [128, 8, 2328], BF16)
          gT = sb(p1, "gT", [128, 8])
          wpl = sb(p1, "wpl", [128, 4, 128], BF16)
          psc = sb(p1, "psc", [128, 4])
          pe2 = sb(p1, "pe2", [32, 2, 128])
          xt = sb(p1, "xt", [128, D]); xb = sb(p1, "xb", [128, D], BF16); hT = sb(p1, "hT", [128, 8, 128], BF16)
          junk = sb(p1, "junk", [128, D], BF16)
          z = sb(p1, "z", [128, 2328]); a1 = sb(p1, "a1", [128, 1536], BF16)
          sq = sb(p1, "sq", [128, 1280]); rs = sb(p1, "rs", [128, 16]); ss = sb(p1, "ss", [128, 1])
          ksb = sb(p1, "ksb", [128, 256], BF16)
          uprev = sb(p1, "uprev", [128, 512]); dTb = sb(p1, "dTb", [128, 4, 128], BF16)
          stt_ = sb(p1, "stt_", [15, NSEQ, 512])
          pooled = sb(p1, "pooled", [128, 2, 128]); kcf = sb(p1, "kcf", [128, 256]); kcb = sb(p1, "kcb", [128, 2, 128], BF16)
          ptb = pst(p1, "ptb", [128, 8, 128], BF16)
          pz = [pst(p1, f"pz{i}", [128, 512]) for i in range(3)]
          pcmp = pst(p1, "pcmp", [128, 2, 128])
          pd = pst(p1, "pd", [128, 4, 128])
          pk = pst(p1, "pk", [128, 4, 128], BF16)
          pmisc = pst(p1, "pmisc", [128, 512])

          for c in range(8):
              dma("pool", win_sb[:, c, :], w_in[l, c * 128:(c + 1) * 128, 0:2328], W=[win_sb])
          dma_nc("sp", gT[:], w_attn_g[l].rearrange("(c p) -> p c", p=128), W=[gT])
          dma("pool", wpl[:], w_pool[l].rearrange("g c e -> c g e"), W=[wpl])
          dma_nc("sp", psc[:], w_psc[l].rearrange("(g e) -> e g", e=128), W=[psc])
          for a_, src in ((0, w_pek), (1, w_pev)):
              dma("sp", pe2[:, a_, 0:64], src[l], W=[pe2])
              dma("sp", pe2[:, a_, 64:128], src[l], W=[pe2])
          dma("sp", stt_[:], spool[l].rearrange("s j c -> j s c"), W=[stt_])
          for a_ in range(2):
              mm(pmisc[:, a_:a_ + 1], pe2[:, a_, :], pmf[0:32, 0:1], True, True, [pe2, cf], [pmisc])
          cp("dve", pemT[:], pmisc[:, 0:2], [pmisc], [pemT])

          def norm_transpose(xt_, gT_, hT_, rstd_):
              act("act", junk[:], xt_[:], AF.Square, accum=rstd_[:], R=[xt_], W=[junk, rstd_])
              ts("dve", rstd_[:], rstd_[:], 1.0 / D, EPS, ALU.mult, ALU.add, R=[rstd_], W=[rstd_])
              rsqrt_chain(rstd_)
              cp("pool", xb[:], xt_[:], [xt_], [xb])
              for c in range(8):
                  tr(ptb[:, c, :], xb[:, c * 128:(c + 1) * 128], identb, [xb, cb], [ptb])
              for c in range(8):
                  e = alt()
                  if e == "act":
                      act("act", hT_[:, c, :], ptb[:, c, :], AF.Copy, scale=gT_[:, c:c + 1], R=[ptb, gT_], W=[hT_])
                  else:
                      ts("dve", hT_[:, c, :], ptb[:, c, :], gT_[:, c:c + 1], None, ALU.mult, R=[ptb, gT_], W=[hT_])

          chk(1.5)
          zi = [0]
          for i in range(NTT):
              samp = (i == NT)
              dma("sp", xt[:], xsrc[i * 128:(i + 1) * 128, :], R=[xs_d[i]], W=[xt])
              norm_transpose(xt, gT, hT, ss)
              if KCUT <= 1:
                  chk(1.6 + 0.01 * i); continue
              col = 0
              while col < 2328:
                  wdt = min(512, 2328 - col)
                  p_ = pz[zi[0] % 3]; zi[0] += 1
                  for c in range(8):
                      mm(p_[:, 0:wdt], hT[:, c, :], win_sb[:, c, col:col + wdt], c == 0, c == 7, [hT, win_sb], [p_])
                  if col < 2328:
                      e = alt()
                      if e == "act":
                          act("act", z[:, col:col + wdt], p_[:, 0:wdt], AF.Copy, scale=ss[:, 0:1], R=[p_, ss], W=[z])
                      else:
                          ts("dve", z[:, col:col + wdt], p_[:, 0:wdt], ss[:, 0:1], None, ALU.mult, R=[p_, ss], W=[z])
                  else:
                      a0 = 1536 + (col - 2328)
                      act("act", a1[:, a0:a0 + wdt], p_[:, 0:wdt], AF.Sigmoid, scale=ss[:, 0:1], R=[p_, ss], W=[a1])
                  col += wdt
              if KCUT <= 2:
                  chk(1.6 + 0.01 * i); continue
              tt("dve", sq[:, 0:512], z[:, 0:512], z[:, 0:512], ALU.mult, [z], [sq])
              tt("dve", sq[:, 512:640], z[:, 768:896], z[:, 768:896], ALU.mult, [z], [sq])
              tt("dve", sq[:, 640:768], z[:, 1024:1152], z[:, 1024:1152], ALU.mult, [z], [sq])
              tt("pool", sq[:, 768:1280], z[:, 1816:2328], z[:, 1816:2328], ALU.mult, [z], [sq])
              red(rs[:, 0:12], sq[:, 0:768].rearrange("p (h d) -> p h d", d=64), [sq], [rs])
              red(rs[:, 12:16], sq[:, 768:1280].rearrange("p (h d) -> p h d", d=128), [sq], [rs])
              ts("dve", rs[:, 0:12], rs[:, 0:12], 1.0 / 64, EPS, ALU.mult, ALU.add, R=[rs], W=[rs])
              ts("dve", rs[:, 12:16], rs[:, 12:16], 1.0 / 128, EPS, ALU.mult, ALU.add, R=[rs], W=[rs])
              rsqrt_chain(rs)
              for h in range(8):
                  stt(a1[:, h * 64:(h + 1) * 64], z[:, h * 64:(h + 1) * 64], rs[:, h:h + 1], gq[:], ALU.mult, ALU.mult, [z, rs, gq], [a1])
              for kv in range(2):
                  stt(z[:, 768 + kv * 64:832 + kv * 64], z[:, 768 + kv * 64:832 + kv * 64], rs[:, 8 + kv:9 + kv], gks[:], ALU.mult, ALU.mult, [z, rs, gks], [z])
                  stt(z[:, 1024 + kv * 64:1088 + kv * 64], z[:, 1024 + kv * 64:1088 + kv * 64], rs[:, 10 + kv:11 + kv], gkw[:], ALU.mult, ALU.mult, [z, rs, gkw], [z])
              for h in range(4):
                  stt(a1[:, 512 + h * 128:640 + h * 128], z[:, 1816 + h * 128:1944 + h * 128], rs[:, 12 + h:13 + h], gmq[:], ALU.mult, ALU.mult, [z, rs, gmq], [a1])
              act("act", nsag[:, i, :], z[:, 1280:1304], AF.Sigmoid, R=[z], W=[nsag])
              if KCUT <= 3:
                  chk(1.6 + 0.01 * i); continue
              outdma("sp", o_nsa[l, i * 128:(i + 1) * 128, :], z[:, 512:1024], R=[z])
              if not samp:
                  if i >= NT - 4:
                      outdma("sp", o_winp[l, (i - (NT - 4)) * 128:(i - (NT - 4) + 1) * 128, :], z[:, 1024:1280], R=[z])
                  if i == NT - 1:
                      outdma("sp", o_poolp[l, :, :], z[113:128, 1304:1816], R=[z])
                  if KCUT <= 4:
                      chk(1.6 + 0.01 * i); continue
                  K5 = int(os.environ.get("K5", "9"))
                  cp("pool", vsel[:, i, :, 0:64], z[:, 896:1024].rearrange("p (k d) -> p k d", k=2), [z], [vsel.ds[i]])
                  cp("pool", vwin[:, i, :, 0:64], z[:, 1152:1280].rearrange("p (k d) -> p k d", k=2), [z], [vwin.ds[i]])
                  if K5 >= 2:
                      cp("pool", ksb[:, 0:128], z[:, 768:896], [z], [ksb])
                      cp("pool", ksb[:, 128:256], z[:, 1024:1152], [z], [ksb])
                  if K5 >= 3:
                      for q_ in range(4):
                          tr(pk[0:64, q_, :], ksb[:, q_ * 64:(q_ + 1) * 64], identb, [ksb, cb], [pk])
                  if K5 >= 4:
                      cp("act", kselT[0:64, :, i * 128:(i + 1) * 128], pk[0:64, 0:2, :], [pk], [kselT.ds[i]])
                  if K5 >= 5:
                      cp("act", kwinT[0:64, :, i * 128:(i + 1) * 128], pk[0:64, 2:4, :], [pk], [kwinT.ds[i]])
                  if KCUT <= 5:
                      chk(1.6 + 0.01 * i); continue
                  mm(pcmp[:, 0, 4 * i:4 * i + 4], z[:, 512:640], pmf, True, True, [z, cf], [pcmp])
                  mm(pcmp[:, 1, 4 * i:4 * i + 4], z[:, 640:768], pmf, True, True, [z, cf], [pcmp])
                  for g in range(4):
                      if i == 0:
                          mm(pd[:, g, :], z[:, 1304 + g * 128:1432 + g * 128], afirst[:, g, :], True, True, [z, cf], [pd])
                      else:
                          mm(pd[:, g, :], z[:, 1304 + g * 128:1432 + g * 128], acur[:, g, :], True, False, [z, cf], [pd])
                          mm(pd[:, g, :], uprev[:, g * 128:(g + 1) * 128], aprev[:, g, :], False, True, [uprev, cf], [pd])
                  cp("act", dTb[:], pd[:], [pd], [dTb])
                  for g in range(4):
                      mm(pd[:, g, :], wpl[:, g, :], dTb[:, g, :], True, True, [wpl, dTb], [pd])
                  for g in range(4):
                      ts("dve", a1[:, 1024 + g * 128:1152 + g * 128], pd[:, g, :], psc[:, g:g + 1], None, ALU.mult, R=[pd, psc], W=[a1])
                  cp("pool", uprev[:], z[:, 1304:1816], [z], [uprev])
              else:
                  cp("dve", sampkv[:], z[:, 768:1280], [z], [sampkv])
                  outdma("sp", o_wins[l, :, 511, :], z[0:NSEQ, 1024:1280], R=[z])
                  outdma("sp", o_pools[l, :, 14, :], z[0:NSEQ, 1304:1816], R=[z])
                  outdma("sp", o_wins[l, :, 0:511, :], cwin[l, :, 1:512, :])
                  outdma("sp", o_pools[l, :, 0:14, :], spool[l, :, 1:15, :])
                  for g in range(4):
                      for s_ in range(NSEQ):
                          mm(pd[:, g, s_:s_ + 1], stt_[0:15, s_, g * 128:(g + 1) * 128], ast[0:15, g:g + 1], True, False, [stt_, cf], [pd])
                          mm(pd[:, g, s_:s_ + 1], z[0:NSEQ, 1304 + g * 128:1432 + g * 128], anew[0:NSEQ, s_, g:g + 1], False, True, [z, cf], [pd])
                  cp("act", dTb[:, :, 0:NSEQ], pd[:, :, 0:NSEQ], [pd], [dTb])
                  for g in range(4):
                      mm(pd[:, g, 0:NSEQ], wpl[:, g, :], dTb[:, g, 0:NSEQ], True, True, [wpl, dTb], [pd])
                  for g in range(4):
                      ts("dve", a1[:, 1024 + g * 128:1024 + g * 128 + NSEQ], pd[:, g, 0:NSEQ], psc[:, g:g + 1], None, ALU.mult, R=[pd, psc], W=[a1])
              dma("sp", a1s[i][:, 0:1536], a1[:], R=[a1], W=[a1s_d[i]])
              chk(1.6 + 0.01 * i)

          chk(2)
          for a_ in range(2):
              ts("dve", pooled[:, a_, 0:NB], pcmp[:, a_, 0:NB], pemT[:, a_:a_ + 1], None, ALU.add, R=[pcmp, pemT], W=[pooled])
          mm(pmisc[0:NB, 0:128], pooled[:, 0, 0:NB], wkbd[:, 0, :], True, True, [pooled, wkbd], [pmisc])
          mm(pmisc[0:NB, 128:256], pooled[:, 1, 0:NB], wkbd[:, 1, :], True, True, [pooled, wkbd], [pmisc])
          cp("act", kcf[0:NB, :], pmisc[0:NB, 0:256], [pmisc], [kcf])
          tt("dve", sq[0:NB, 0:128], kcf[0:NB, 0:128], kcf[0:NB, 0:128], ALU.mult, [kcf], [sq])
          red(rs[0:NB, 0:2], sq[0:NB, 0:128].rearrange("p (h d) -> p h d", d=64), [sq], [rs])
          ts("dve", rs[0:NB, 0:2], rs[0:NB, 0:2], 1.0 / 64, EPS, ALU.mult, ALU.add, R=[rs], W=[rs])
          act("act", rs[0:NB, 0:2], rs[0:NB, 0:2], AF.Sqrt, R=[rs], W=[rs])
          recip(rs[0:NB, 0:2], rs[0:NB, 0:2], [rs], [rs])
          memset("pool", kcb[:], 0.0, [kcb])
          for kv in range(2):
              for dup in range(2):
                  stt(kcb[0:NB, kv, dup * 64:(dup + 1) * 64], kcf[0:NB, kv * 64:(kv + 1) * 64], rs[0:NB, kv:kv + 1], gkc[0:NB, :], ALU.mult, ALU.mult, [kcf, rs, gkc], [kcb])
          for kv in range(2):
              tr(pk[:, kv, 0:NB], kcb[0:NB, kv, :], identb[0:NB, 0:NB], [kcb, cb], [pk])
          memset("pool", kcT2[:], 0.0, [kcT2])
          cp("act", kcT2[:, :, 0:NB], pk[:, 0:2, 0:NB], [pk], [kcT2])
          memset("pool", vcaug[:, :, 0:64], 0.0, [vcaug])
          cp("dve", vcaug[0:NB, :, 0:64], kcf[0:NB, 128:256].rearrange("p (k d) -> p k d", k=2), [kcf], [vcaug])

          fw.barrier()
          p1.close()

          chk(3)
          p1b = ExitStack()
          win_sb = sb(p1b, "win_sb2", [128, 8, 3072], BF16)
          gT = sb(p1b, "gT2", [128, 8])
          xt = sb(p1b, "xt1b", [128, D]); xb = sb(p1b, "xb1b", [128, D], BF16); hT = sb(p1b, "hT1b", [128, 8, 128], BF16)
          junk = sb(p1b, "junk1b", [128, D], BF16); ss = sb(p1b, "ss1b", [128, 1])
          a1m = [sb(p1b, f"a1m{k}", [128, 3072], BF16) for k in range(2)]
          ptb = pst(p1b, "ptb1b", [128, 8, 128], BF16)
          pz = [pst(p1b, f"pz1b{i}", [128, 512]) for i in range(3)]
          for c in range(8):
              dma("pool", win_sb[:, c, :], w_in[l, c * 128:(c + 1) * 128, 2328:PW], W=[win_sb])
          dma_nc("sp", gT[:], w_attn_g[l].rearrange("(c p) -> p c", p=128), W=[gT])
          for i in range(NTT):
              dma("sp", xt[:], xsrc[i * 128:(i + 1) * 128, :], R=[xs_d[i]], W=[xt])
              norm_transpose(xt, gT, hT, ss)
              am = a1m[i % 2]
              for n in range(6):
                  p_ = pz[zi[0] % 3]; zi[0] += 1
                  for c in range(8):
                      mm(p_[:], hT[:, c, :], win_sb[:, c, n * 512:(n + 1) * 512], c == 0, c == 7, [hT, win_sb], [p_])
                  act("act", am[:, n * 512:(n + 1) * 512], p_[:], AF.Sigmoid, scale=ss[:, 0:1], R=[p_, ss], W=[am])
              dma("sp", a1s[i][:, 1536:A1W], am[:], R=[am], W=[a1s_d2[i]])
          fw.barrier()
          p1b.close()

          chk(4)
          p1 = ExitStack()
          xb = sb(p1, "xbm", [128, D], BF16); hT = sb(p1, "hTm", [128, 8, 128], BF16)
          junk = sb(p1, "junkm", [128, D], BF16); ss = sb(p1, "ssm", [128, 1])
          sq = sb(p1, "sqm", [128, 512]); rs = sb(p1, "rsm", [128, 4])
          memx = sb(p1, "memx", [128, D]); wmem = sb(p1, "wmem", [128, 8, 1024], BF16); gmT = sb(p1, "gmT", [128, 8])
          mkvf = sb(p1, "mkvf", [128, 1024]); mkb = sb(p1, "mkb", [128, 512], BF16)
          ptb = pst(p1, "ptbm", [128, 8, 128], BF16)
          pz = [pst(p1, f"pzm{i}", [128, 512]) for i in range(3)]
          pk = pst(p1, "pkm", [128, 4, 128], BF16)
          for c in range(8):
              dma("pool", wmem[:, c, :], w_memkv[l, c * 128:(c + 1) * 128, :], W=[wmem])
          dma_nc("sp", gmT[:], w_memg[l].rearrange("(c p) -> p c", p=128), W=[gmT])
          for mt in range(2):
              dma("sp", memx[:], memin[mt * 128:(mt + 1) * 128, :], W=[memx])
              norm_transpose(memx, gmT, hT, ss)
              for n in range(2):
                  p_ = pz[zi[0] % 3]; zi[0] += 1
                  for c in range(8):
                      mm(p_[:], hT[:, c, :], wmem[:, c, n * 512:(n + 1) * 512], c == 0, c == 7, [hT, wmem], [p_])
                  act("act", mkvf[:, n * 512:(n + 1) * 512], p_[:], AF.Copy, scale=ss[:, 0:1], R=[p_, ss], W=[mkvf])
              tt("dve", sq[:, 0:512], mkvf[:, 0:512], mkvf[:, 0:512], ALU.mult, [mkvf], [sq])
              red(rs[:, 0:4], sq[:, 0:512].rearrange("p (h d) -> p h d", d=128), [sq], [rs])
              ts("dve", rs[:, 0:4], rs[:, 0:4], 1.0 / 128, EPS, ALU.mult, ALU.add, R=[rs], W=[rs])
              rsqrt_chain(rs)
              for h in range(4):
                  stt(mkvf[:, h * 128:(h + 1) * 128], mkvf[:, h * 128:(h + 1) * 128], rs[:, h:h + 1], gmk[:], ALU.mult, ALU.mult, [mkvf, rs, gmk], [mkvf])
              outdma("sp", o_memkv[l, mt * 128:(mt + 1) * 128, :], mkvf[:], R=[mkvf])
              cp("pool", mkb[:], mkvf[:, 0:512], [mkvf], [mkb])
              for h in range(4):
                  tr(pk[:, h, :], mkb[:, h * 128:(h + 1) * 128], identb, [mkb, cb], [pk])
              cp("act", mkT[:, :, mt * 128:(mt + 1) * 128], pk[:], [pk], [mkT])
              cp("dve", mva[:, mt, :, 0:128], mkvf[:, 512:1024].rearrange("p (h d) -> p h d", h=4), [mkvf], [mva])
          fw.barrier()
          p1.close()

          chk(5)
          for mode in ("prompt", "sample"):
              PR = (mode == "prompt")
              SM = not PR
              if SM:
                  lay.close()
              p2 = ExitStack()

              def mk(cond, name, shape, dt=F32):
                  return sb(p2, name + mode[0], shape, dt) if cond else None

              wup = sb(p2, "wup" + mode[0], [128, 3, 4, D], BF16)
              wo = sb(p2, "wo" + mode[0], [128, 8, D], BF16)
              a1 = sb(p2, "a1b" + mode[0], [128, A1W], BF16)
              xt = sb(p2, "xt2" + mode[0], [128, D])
              den = mk(True, "den", [128, 8]); rg = mk(True, "rg", [128, 8])
              onsa = mk(True, "onsa", [128, 8, 64]); otmp = mk(True, "otmp", [128, 8, 64]); onsab = mk(True, "onsab", [128, 512], BF16)
              brT = mk(True, "brT", [128, 2, 4, 128], BF16)
              memo = mk(True, "memo", [128, 4, 128], BF16)
              hsum = mk(True, "hsum", [128, D]); htmp = mk(True, "htmp", [128, 512]); hb = mk(True, "hb", [128, D], BF16)
              hT2 = mk(True, "hT2", [128, 8, 128], BF16)
              qtp = mk(PR, "qtp", [128, 8, 128], BF16)
              ec = mk(PR, "ec", [128, 8, 128]); pc = mk(PR, "pc", [128, 8, 128]); pcb = mk(PR, "pcb", [128, 8, 128], BF16)
              pcT = mk(PR, "pcT", [128, 8, 128], BF16)
              imp = mk(PR, "imp", [128, 2, 128]); sc = mk(PR, "sc", [128, 2, 64]); sc2 = mk(PR, "sc2", [128, 2, 64])
              m8 = mk(PR, "m8", [128, 2, 16]); selm = mk(PR, "selm", [128, 2, 64])
              qa = mk(PR, "qa", [128, 8, 128], BF16); qta = mk(PR, "qta", [128, 8, 128], BF16)
              pT = [mk(PR, f"pT{i}", [128, 512], BF16) for i in range(3)]
              mqT = mk(PR, "mqT", [128, 4, 128], BF16)
              pmT = mk(PR, "pmT", [128, 8, 128], BF16)
              pg = [mk(SM, f"pg{i}", [128, 512]) for i in range(3)]
              qbf = mk(SM, "qbf", [128, 512]); mqbf = mk(SM, "mqbf", [128, 512])
              prod = [mk(SM, "prod0", [128, 1024]), mk(SM, "prod1", [128, 512])]
              ssel = mk(SM, "ssel", [128, NPG, 8]); esel = mk(SM, "esel", [128, NPG, 8]); emb = mk(SM, "emb", [128, NPG, 8], BF16)
              vsall = mk(SM, "vsall", [128, NPG + 1, 2, 65], BF16)
              pooleds = mk(SM, "pooleds", [128, 2, NBS])
              kcs = mk(SM, "kcs", [128, 256]); kcsb = mk(SM, "kcsb", [128, 128], BF16); kcTs = mk(SM, "kcTs", [128, NBS], BF16)
              vcs = mk(SM, "vcs", [128, NCK, 2, 65], BF16)
              qbt = mk(SM, "qbt", [128, 8, 128], BF16); qbm = mk(SM, "qbm", [128, 8, 128], BF16)
              pe8 = mk(SM, "pe8", [8, NBS]); pn8 = mk(SM, "pn8", [8, NBS]); d8 = mk(SM, "d8", [8, 4])
              scs = mk(SM, "scs", [2, NBS // 2]); scs2 = mk(SM, "scs2", [2, NBS // 2]); sels = mk(SM, "sels", [2, NBS // 2]); m8s = mk(SM, "m8s", [2, 16])
              pTs = mk(SM, "pTs", [128, NCK, 8], BF16)
              pnew = mk(SM, "pnew", [128, 2, 8], BF16); snew = mk(SM, "snew", [128, 2, 8])
              vnew = mk(SM, "vnew", [128, 2, 2, 65], BF16)
              wc = mk(SM, "wc", [128, 4, 256]); sw = mk(SM, "sw", [128, 4, 8]); pw = mk(SM, "pw", [128, 4, 8], BF16)
              vwa = mk(SM, "vwa", [128, 4, 2, 65], BF16)
              mc = mk(SM, "mc", [128, 2, 1024]); smm = mk(SM, "smm", [128, 8]); pmm = mk(SM, "pmm", [128, 2, 4], BF16)
              vmb = mk(SM, "vmb", [128, 2, 512], BF16)
              oall = mk(SM, "oall", [8, 3, NSEQ, 64]); o8t = mk(SM, "o8t", [8, 2, 65]); o8 = mk(SM, "o8", [8, 65])
              mall = mk(SM, "mall", [4, NSEQ, 128]); m4t = mk(SM, "m4t", [4, 4, 128]); m4 = mk(SM, "m4", [4, 128])
              od = mk(SM, "od", [8, 8, 64]); odm = mk(SM, "odm", [4, 4, 128])
              ob = mk(SM, "ob", [128, 3, 512])
              d8_rs_t = mk(SM, "d8rs", [128, 2])
              d8_rs = d8_rs_t.h if SM else None
              b0 = pst(p2, "b0" + mode[0], [128, 8, 128], BF16)
              b12 = pst(p2, "b12" + mode[0], [128, 8, 128])
              b34 = [pst(p2, f"b3{i}" + mode[0], [128, 512]) for i in range(2)]
              acc = pst(p2, "acc" + mode[0], [128, 2, 512])
              b7 = pst(p2, "b7" + mode[0], [128, 8, 128], BF16)

              for bi, src in enumerate((w_upn, w_upp, w_upm)):
                  dma("pool", wup[:, bi, :, :], src[l].rearrange("(c p) n -> p c n", p=128), W=[wup])
              dma("pool", wo[:], w_out[l].rearrange("(c p) n -> p c n", p=128), W=[wo])
              if PR:
                  memset("pool", sc[:], 0.0, [sc])
              else:
                  memset("pool", vsall[:, :, :, 64:65], 1.0, [vsall])
                  memset("pool", vnew[:, :, :, 64:65], 1.0, [vnew])
                  memset("pool", vwa[:, :, :, 64:65], 1.0, [vwa])
                  memset("pool", vcs[:, :, :, 64:65], 1.0, [vcs])
                  memset("pool", qbm[:], 0.0, [qbm])

              acc4 = acc[:, :, 0:260].rearrange("p k (g e) -> p k g e", e=65)

              def branch_out(br, i, first):
                  ts("dve", den[:].rearrange("p (k g) -> p k g", k=2), acc4[:, :, :, 64], 1e-30, None, ALU.max, R=[acc], W=[den])
                  if br == 2 and i < 4:
                      ts("dve", den[:], den[:], npad[:, i:i + 1], None, ALU.add, R=[den, cf], W=[den])
                  recip(den[:], den[:], [den], [den])
                  gv = nsag[:, i, :].rearrange("p (h b) -> p h b", b=3)[:, :, br]
                  tt("dve", rg[:], den[:], gv, ALU.mult, [den, nsag], [rg])
                  if "csw"[br] in os.environ.get("KOFF", ""):
                      ts("dve", rg[:], rg[:], 0.0, None, ALU.mult, R=[rg], W=[rg])
                  rgb = rg[:].rearrange("p (k g) -> p k g", k=2).unsqueeze(3).to_broadcast([128, 2, 4, 64])
                  dst = onsa if first else otmp
                  tt("dve", dst[:].rearrange("p (k g) d -> p k g d", k=2), acc4[:, :, :, 0:64], rgb, ALU.mult, [acc, rg], [dst])
                  if not first:
                      tt("pool", onsa[:], onsa[:], otmp[:], ALU.add, [onsa, otmp], [onsa])

              pti_ = [0]

              def prompt_attention(i):
                  for h in range(8):
                      tr(b0[0:64, h, :], a1[:, h * 64:(h + 1) * 64], identb, [a1, cb], [b0])
                  cp("act", qtp[0:64, :, :], b0[0:64, :, :], [b0], [qtp])
                  for h in range(8):
                      mm(b12[:, h, :], qtp[0:64, h, :], kcT2[0:64, h // 4, :], True, True, [qtp, kcT2], [b12])
                  act("act", ec[:], b12[:], AF.Exp, R=[b12], W=[ec])
                  tm = tcmp[:, 124 - 4 * i:124 - 4 * i + 128]
                  for h in range(8):
                      ttr(pc[:, h, :], ec[:, h, :], tm, den[:, h:h + 1], [ec, cf], [pc, den])
                  ts("dve", den[:], den[:], 1e-30, None, ALU.max, R=[den], W=[den])
                  recip(den[:], den[:], [den], [den])
                  for kv in range(2):
                      for g in range(4):
                          h = kv * 4 + g
                          if g == 0:
                              ts("dve", imp[:, kv, :], pc[:, h, :], den[:, h:h + 1], None, ALU.mult, R=[pc, den], W=[imp])
                          else:
                              stt(imp[:, kv, :], pc[:, h, :], den[:, h:h + 1], imp[:, kv, :], ALU.mult, ALU.add, [pc, den, imp], [imp])
                  iv = imp[:, :, 0:NB].rearrange("p k (b r) -> p k b r", r=2)
                  tt("dve", sc[:, :, 0:NB // 2], iv[:, :, :, 0], iv[:, :, :, 1], ALU.add, [imp], [sc])
                  bs = bsel[:, 62 - 2 * i:126 - 2 * i]
                  tt("dve", sc2[:], sc[:], bs.unsqueeze(1).to_broadcast([128, 2, 64]), ALU.add, [sc, cf], [sc2])
                  memset("dve", sc2[:, :, 0:1], 1e9, [sc2])
                  for kv in range(2):
                      vmax(m8[:, kv, 0:8], sc2[:, kv, :], [sc2], [m8])
                      mrep(selm[:, kv, :], m8[:, kv, 0:8], sc2[:, kv, :], -2e9, [sc2, m8], [selm])
                      vmax(m8[:, kv, 8:16], selm[:, kv, :], [selm], [m8])
                      ts("dve", selm[:, kv, :], sc2[:, kv, :], m8[:, kv, 15:16], None, ALU.is_ge, R=[sc2, m8], W=[selm])
                      ts("dve", qa[:, 4 * kv:4 * kv + 4, 64:128], selm[:, kv, :].unsqueeze(1).to_broadcast([128, 4, 64]), 1.0, BIGM, ALU.subtract, ALU.mult, R=[selm], W=[qa])
                  cp("pool", qa[:, :, 0:64], a1[:, 0:512].rearrange("p (h d) -> p h d", h=8), [a1], [qa])
                  for h in range(8):
                      tr(b0[:, h, :], qa[:, h, :], identb, [qa, cb], [b0])
                  cp("act", qta[:], b0[:], [b0], [qta])
                  cp("pool", pcb[:], pc[:], [pc], [pcb])
                  for h in range(8):
                      tr(b7[:, h, :], pcb[:, h, :], identb, [pcb, cb], [b7])
                  cp("dve", pcT[:], b7[:], [b7], [pcT])
                  for h in range(8):
                      mm(acc[:, h // 4, (h % 4) * 65:(h % 4) * 65 + 65], pcT[:, h, :], vcaug[:, h // 4, :], True, True, [pcT, vcaug], [acc])
                  branch_out(0, i, True)
                  if i == 0:
                      chk(5.1)
                  for br in (1, 2):
                      jlo = 0 if br == 1 else max(0, i - 4)
                      for kv in range(2):
                          mm(acc[:, kv, 0:260], zerob[:, 0:128], zerob[:, 0:260], True, False, [zerob], [acc])
                          for j in range(jlo, i + 1):
                              ps_ = b34[pti_[0] % 2]
                              pt_ = pT[pti_[0] % 3]
                              pti_[0] += 1
                              if br == 1:
                                  mm(ps_[:], kselT[:, kv, j * 128:(j + 1) * 128], qta[:, 4 * kv:4 * kv + 4, :], True, True, [kselT.ds[j], kselT, qta], [ps_])
                              else:
                                  mm(ps_[:], kwinT[0:64, kv, j * 128:(j + 1) * 128], qta[0:64, 4 * kv:4 * kv + 4, :], True, True, [kwinT.ds[j], qta], [ps_])
                              act("act", pt_[:], ps_[:], AF.Exp, R=[ps_], W=[pt_])
                              if j == i or (br == 2 and j == i - 4):
                                  msk = tric_b if j == i else trib_b
                                  tt("pool", pt_[:].rearrange("p (g t) -> p g t", g=4), pt_[:].rearrange("p (g t) -> p g t", g=4),
                                     msk.unsqueeze(1).to_broadcast([128, 4, 128]), ALU.mult, [pt_, cb], [pt_])
                              vv = vsel if br == 1 else vwin
                              for g in range(4):
                                  mm(acc[:, kv, g * 65:g * 65 + 65], pt_[:, g * 128:(g + 1) * 128], vv[:, j, kv, :], False, (j == i and g == 3), [pt_, vv.ds[j], vv], [acc])
                      branch_out(br, i, False)
                      if i == 0:
                          chk(5.2 + 0.01 * br)
                  for h in range(4):
                      tr(b0[:, h, :], a1[:, 512 + h * 128:640 + h * 128], identb, [a1, cb], [b0])
                  cp("act", mqT[:], b0[:, 0:4, :], [b0], [mqT])
                  for h in range(4):
                      for mt in range(2):
                          mm(b12[:, h * 2 + mt, :], mkT[:, h, mt * 128:(mt + 1) * 128], mqT[:, h, :], True, True, [mkT, mqT], [b12])
                  act("act", pmT[:], b12[:], AF.Exp, R=[b12], W=[pmT])
                  for h in range(4):
                      for mt in range(2):
                          mm(acc[:, h // 2, (h % 2) * 256:(h % 2) * 256 + 129], pmT[:, h * 2 + mt, :], mva[:, mt, h, :], mt == 0, mt == 1, [pmT, mva], [acc])
                  accm = acc[:, :, :].rearrange("p k (g e) -> p (k g) e", e=256)
                  cp("dve", den[:, 0:4], accm[:, :, 128], [acc], [den])
                  recip(den[:, 0:4], den[:, 0:4], [den], [den])
                  tt("dve", memo[:], accm[:, :, 0:128], den[:, 0:4].unsqueeze(2).to_broadcast([128, 4, 128]), ALU.mult, [acc, den], [memo])
                  if i == 0:
                      chk(5.3)

              def sample_attention():
                  i = NT
                  for kv in range(2):
                      cp("pool", qbm[:, 4 * kv:4 * kv + 4, 64 * kv:64 * kv + 64], a1[:, 256 * kv:256 * kv + 256].rearrange("p (g d) -> p g d", g=4), [a1], [qbm])
                  for h in range(8):
                      tr(b0[:, h, :], qbm[:, h, :], identb, [qbm, cb], [b0])
                  cp("act", qbt[:], b0[:], [b0], [qbt])
                  cp("act", vnew[:, 0, :, 0:64], sampkv[:, 128:256].rearrange("p (k d) -> p k d", k=2), [sampkv], [vnew])
                  cp("act", vnew[:, 1, :, 0:64], sampkv[:, 384:512].rearrange("p (k d) -> p k d", k=2), [sampkv], [vnew])
                  pcs = b12
                  pcs_v = b12[:].rearrange("p a b -> p (a b)")[:, 0:2 * NBS].rearrange("p (a b) -> p a b", a=2)
                  for s_ in range(NSEQ):
                      mm(b34[0][:], selall_b[:, s_, :], a1[:, 0:512], True, True, [cb, a1], [b34[0]])
                      cp("act", qbf[:], b34[0][:], [b34[0]], [qbf])
                      mm(b34[1][:], selall_b[:, s_, :], a1[:, 512:1024], True, True, [cb, a1], [b34[1]])
                      cp("act", mqbf[:], b34[1][:], [b34[1]], [mqbf])
                      qv = qbf[:].rearrange("p (k g d) -> p k g d", k=2, g=4)

                      def dots(dst, ksrc, pi):
                          pr = prod[pi % 2]
                          e = "dve" if pi % 2 == 0 else "pool"
                          tt(e, pr[:, 0:512].rearrange("p (k g d) -> p k g d", k=2, g=4),
                             ksrc.rearrange("p (k d) -> p k d", k=2).unsqueeze(2).to_broadcast([128, 2, 4, 64]), qv, ALU.mult, [qbf] + dots_R, [pr])
                          red(dst, pr[:, 0:512].rearrange("p (h d) -> p h d", d=64), [pr], dots_W)

                      for j in range(NPG):
                          pgt = pg[j % 3]
                          col = s_ * NPG + j
                          fw.dma("pool", lambda E, o_=pgt[:], i_=cnsa_flat, x_=pidx[:, l, col:col + 1]: E.indirect_dma_start(
                              out=o_, out_offset=None, in_=i_,
                              in_offset=bass.IndirectOffsetOnAxis(ap=x_, axis=0)), [pidx.d], [pgt.d])
                          mm(pcs_v[:, 0, 4 * j:4 * j + 4], pgt[:, 0:128], pmf, True, True, [pgt, cf], [b12])
                          mm(pcs_v[:, 1, 4 * j:4 * j + 4], pgt[:, 128:256], pmf, True, True, [pgt, cf], [b12])
                          dots_R = [pgt]; dots_W = [ssel]
                          dots(ssel[:, j, :], pgt[:, 256:384], j)
                          cp("act", vsall[:, j, :, 0:64], pgt[:, 384:512].rearrange("p (k d) -> p k d", k=2), [pgt], [vsall])
                      act("act", esel[:], ssel[:], AF.Exp, R=[ssel], W=[esel])
                      for a_ in range(2):
                          ts("dve", pooleds[:, a_, :], pcs_v[:, a_, :], pemT[:, a_:a_ + 1], None, ALU.add, R=[b12, pemT], W=[pooleds])
                      for ck in range(NCK):
                          mm(b34[0][0:CB, 0:128], pooleds[:, 0, ck * CB:(ck + 1) * CB], wkbd[:, 0, :], True, True, [pooleds, wkbd], [b34[0]])
                          mm(b34[0][0:CB, 128:256], pooleds[:, 1, ck * CB:(ck + 1) * CB], wkbd[:, 1, :], True, True, [pooleds, wkbd], [b34[0]])
                          cp("act", kcs[0:CB, :], b34[0][0:CB, 0:256], [b34[0]], [kcs])
                          tt("dve", prod[0][0:CB, 0:128], kcs[0:CB, 0:128], kcs[0:CB, 0:128], ALU.mult, [kcs], [prod[0]])
                          red(d8_rs[0:CB, 0:2], prod[0][0:CB, 0:128].rearrange("p (h d) -> p h d", d=64), [prod[0]], [d8_rs_t])
                          ts("dve", d8_rs[0:CB, 0:2], d8_rs[0:CB, 0:2], 1.0 / 64, EPS, ALU.mult, ALU.add, R=[d8_rs_t], W=[d8_rs_t])
                          act("act", d8_rs[0:CB, 0:2], d8_rs[0:CB, 0:2], AF.Sqrt, R=[d8_rs_t], W=[d8_rs_t])
                          recip(d8_rs[0:CB, 0:2], d8_rs[0:CB, 0:2], [d8_rs_t], [d8_rs_t])
                          for kv in range(2):
                              stt(kcsb[0:CB, kv * 64:(kv + 1) * 64], kcs[0:CB, kv * 64:(kv + 1) * 64], d8_rs[0:CB, kv:kv + 1], gkc[0:CB, :], ALU.mult, ALU.mult, [kcs, d8_rs_t, gkc], [kcsb])
                          tr(b7[:, 0, 0:CB], kcsb[0:CB, :], identb[0:CB, 0:CB], [kcsb, cb], [b7])
                          cp("act", kcTs[:, ck * CB:(ck + 1) * CB], b7[:, 0, 0:CB], [b7], [kcTs])
                          cp("dve", vcs[0:CB, ck, :, 0:64], kcs[0:CB, 128:256].rearrange("p (k d) -> p k d", k=2), [kcs], [vcs])
                      mm(b34[1][0:8, 0:NBS], qbt[:, :, s_], kcTs[:], True, True, [qbt, kcTs], [b34[1]])
                      act("act", pe8[:], b34[1][0:8, 0:NBS], AF.Exp, accum=d8[:, 0:1], R=[b34[1]], W=[pe8, d8])
                      recip(d8[:, 1:2], d8[:, 0:1], [d8], [d8])
                      ts("dve", pn8[:], pe8[:], d8[:, 1:2], None, ALU.mult, R=[pe8, d8], W=[pn8])
                      mm(b34[0][0:2, 0:NBS], g8[0:8, :], pn8[:], True, True, [cf, pn8], [b34[0]])
                      iv = b34[0][0:2, 0:NBS].rearrange("p (b r) -> p b r", r=2)
                      cp("act", scs2[:], iv[:, :, 0], [b34[0]], [scs2])
                      tt("dve", scs[:], scs2[:], iv[:, :, 1], ALU.add, [scs2, b34[0]], [scs])
                      memset("dve", scs[:, 0:1], 1e9, [scs])
                      memset("dve", scs[:, NBS // 2 - 1:NBS // 2], 1e9, [scs])
                      vmax(m8s[:, 0:8], scs[:], [scs], [m8s])
                      mrep(scs2[:], m8s[:, 0:8], scs[:], -2e9, [scs, m8s], [scs2])
                      vmax(m8s[:, 8:16], scs2[:], [scs2], [m8s])
                      ts("dve", sels[:], scs[:], m8s[:, 14:15], None, ALU.is_ge, R=[scs, m8s], W=[sels])
                      for ck in range(NCK):
                          tr(b34[1][0:CB, 256 + ck * 8:264 + ck * 8], pn8[:, ck * CB:(ck + 1) * CB], identf[0:8, 0:8], [pn8, cf], [b34[1]])
                          cp("act", pTs[0:CB, ck, :], b34[1][0:CB, 256 + ck * 8:264 + ck * 8], [b34[1]], [pTs])
                      for ck in range(NCK):
                          mm(acc[0:8, 0, 0:130], pTs[0:CB, ck, :], vcs[0:CB, ck, :, :].rearrange("p k e -> p (k e)"), ck == 0, ck == NCK - 1, [pTs, vcs], [acc])
                      pick8(0, s_, False)
                      sv = sels[:].rearrange("p (j r) -> p j r", r=2)
                      for kv in range(2):
                          mm(b34[1][:, kv * NPG:(kv + 1) * NPG], inde[0:2, kv, 0, :], sv[:, :, 0], True, False, [cf, sels], [b34[1]])
                          mm(b34[1][:, kv * NPG:(kv + 1) * NPG], inde[0:2, kv, 1, :], sv[:, :, 1], False, True, [cf, sels], [b34[1]])
                      mv = b34[1][:, 0:2 * NPG].rearrange("p (k j) -> p j k", k=2).unsqueeze(3).to_broadcast([128, NPG, 2, 4])
                      tt("dve", emb[:].rearrange("p j (k g) -> p j k g", k=2), esel[:].rearrange("p j (k g) -> p j k g", k=2), mv, ALU.mult, [esel, b34[1]], [emb])
                      dots_R = [sampkv]; dots_W = [snew]
                      dots(snew[:, 0, :], sampkv[:, 0:128], 0)
                      dots(snew[:, 1, :], sampkv[:, 256:384], 1)
                      act("act", snew[:], snew[:], AF.Exp, R=[snew], W=[snew])
                      ts("dve", pnew[:], snew[:], rowsel[:, s_:s_ + 1], None, ALU.mult, R=[snew, cf], W=[pnew])
                      for j in range(NPG):
                          mm(acc[0:8, 0, 0:130], emb[:, j, :], vsall[:, j, :, :].rearrange("p k e -> p (k e)"), j == 0, False, [emb, vsall], [acc])
                      mm(acc[0:8, 0, 0:130], pnew[:, 0, :], vnew[:, 0, :, :].rearrange("p k e -> p (k e)"), False, True, [pnew, vnew], [acc])
                      pick8(1, s_, True)
                      dma("sp", wc[:], cwin[l, s_].rearrange("(kt p) c -> p kt c", p=128), W=[wc])
                      for kt in range(4):
                          dots_R = [wc]; dots_W = [sw]
                          dots(sw[:, kt, :], wc[:, kt, 0:128], kt)
                      act("act", sw[:], sw[:], AF.Exp, R=[sw], W=[sw])
                      ts("dve", pw[:, 0, :], sw[:, 0, :], wmask[:, 0:1], None, ALU.mult, R=[sw, cf], W=[pw])
                      cp("dve", pw[:, 1:4, :], sw[:, 1:4, :], [sw], [pw])
                      cp("act", vwa[:, :, :, 0:64], wc[:, :, 128:256].rearrange("p t (k d) -> p t k d", k=2), [wc], [vwa])
                      for kt in range(4):
                          mm(acc[0:8, 0, 0:130], pw[:, kt, :], vwa[:, kt, :, :].rearrange("p k e -> p (k e)"), kt == 0, False, [pw, vwa], [acc])
                      mm(acc[0:8, 0, 0:130], pnew[:, 1, :], vnew[:, 1, :, :].rearrange("p k e -> p (k e)"), False, True, [pnew, vnew], [acc])
                      pick8(2, s_, True)
                      dma("sp", mc[:], cmem[l, s_].rearrange("(mt p) c -> p mt c", p=128), W=[mc])
                      tt("dve", prod[0][:].rearrange("p (t c) -> p t c", t=2), mc[:, :, 0:512], mqbf[:].unsqueeze(1).to_broadcast([128, 2, 512]), ALU.mult, [mc, mqbf], [prod[0]])
                      red(smm[:], prod[0][:].rearrange("p (h d) -> p h d", d=128), [prod[0]], [smm])
                      act("act", pmm[:].rearrange("p t h -> p (t h)"), smm[:], AF.Exp, R=[smm], W=[pmm])
                      cp("pool", vmb[:], mc[:, :, 512:1024], [mc], [vmb])
                      for mt in range(2):
                          mm(acc[0:4, 1, :], pmm[:, mt, :], vmb[:, mt, :], mt == 0, mt == 1, [pmm, vmb], [acc])
                      for mt in range(2):
                          mm(b34[0][0:4, 0:1], pmm[:, mt, :], onesb[:, 0:1], mt == 0, mt == 1, [pmm, cb], [b34[0]])
                      tt("dve", m4t[:], acc[0:4, 1, :].rearrange("p (a d) -> p a d", a=4), identf[0:4, 0:4].unsqueeze(2).to_broadcast([4, 4, 128]), ALU.mult, [acc, cf], [m4t])
                      red(m4[:], m4t[:].rearrange("p a d -> p d a"), [m4t], [m4])
                      recip(d8[0:4, 2:3], b34[0][0:4, 0:1], [b34[0].d], [d8.d])
                      ts("dve", mall[:, s_, :], m4[:], d8[0:4, 2:3], None, ALU.mult, R=[m4, d8], W=[mall])
                  for br in range(3):
                      for s_ in range(NSEQ):
                          tt("dve", od[:], oall[:, br, s_, :].unsqueeze(1).to_broadcast([8, 8, 64]), identf[0:8, 0:8].unsqueeze(2).to_broadcast([8, 8, 64]), ALU.mult, [oall, cf], [od])
                          mm(b12[:, 0:4, :].rearrange("p a b -> p (a b)"), sr_f[0:8, s_, :], od[:].rearrange("p a b -> p (a b)"), True, True, [cf, od], [b12])
                          if s_ == 0:
                              cp("act", ob[:, br, :], b12[:, 0:4, :].rearrange("p a b -> p (a b)"), [b12], [ob])
                          else:
                              tt("dve", ob[:, br, :], ob[:, br, :], b12[:, 0:4, :].rearrange("p a b -> p (a b)"), ALU.add, [ob, b12], [ob])
                  for br in range(3):
                      gv = nsag[:, i, :].rearrange("p (h b) -> p h b", b=3)[:, :, br].unsqueeze(2).to_broadcast([128, 8, 64])
                      dst = onsa if br == 0 else otmp
                      tt("dve", dst[:], ob[:, br, :].rearrange("p (h d) -> p h d", h=8), gv, ALU.mult, [ob, nsag], [dst])
                      if br > 0:
                          tt("pool", onsa[:], onsa[:], otmp[:], ALU.add, [onsa, otmp], [onsa])
                  for s_ in range(NSEQ):
                      tt("dve", odm[:], mall[:, s_, :].unsqueeze(1).to_broadcast([4, 4, 128]), identf[0:4, 0:4].unsqueeze(2).to_broadcast([4, 4, 128]), ALU.mult, [mall, cf], [odm])
                      mm(b12[:, 4:8, :].rearrange("p a b -> p (a b)"), sr_f[0:4, s_, :], odm[:].rearrange("p a b -> p (a b)"), True, True, [cf, odm], [b12])
                      if s_ == 0:
                          cp("act", hsum[:, 0:512], b12[:, 4:8, :].rearrange("p a b -> p (a b)"), [b12], [hsum])
                      else:
                          tt("dve", hsum[:, 0:512], hsum[:, 0:512], b12[:, 4:8, :].rearrange("p a b -> p (a b)"), ALU.add, [hsum, b12], [hsum])
                  cp("dve", memo[:].rearrange("p h d -> p (h d)"), hsum[:, 0:512], [hsum], [memo])

              def pick8(br, s_, normalize):
                  tt("dve", o8t[:], acc[0:8, 0, 0:130].rearrange("p (k e) -> p k e", k=2), g8[0:8, :].unsqueeze(2).to_broadcast([8, 2, 65]), ALU.mult, [acc, cf], [o8t])
                  tt("dve", o8[:], o8t[:, 0, :], o8t[:, 1, :], ALU.add, [o8t], [o8])
                  if normalize:
                      recip(d8[:, 3:4], o8[:, 64:65], [o8.d], [d8.d])
                      ts("dve", oall[:, br, s_, :], o8[:, 0:64], d8[:, 3:4], None, ALU.mult, R=[o8, d8], W=[oall])
                  else:
                      cp("dve", oall[:, br, s_, :], o8[:, 0:64], [o8], [oall])

              for i in (range(NT) if PR else [NT]):
                  dma("sp", a1[:], a1s[i], R=[a1s_d[i], a1s_d2[i]], W=[a1])
                  dma("sp", xt[:], xsrc[i * 128:(i + 1) * 128, :], R=[xs_d[i]], W=[xt])
                  if i < NT:
                      prompt_attention(i)
                  else:
                      sample_attention()
                  cp("act", onsab[:], onsa[:].rearrange("p h d -> p (h d)"), [onsa], [onsab])
                  for c in range(4):
                      tr(b0[:, c, :], onsab[:, c * 128:(c + 1) * 128], identb, [onsab, cb], [b0])
                      tr(b0[:, 4 + c, :], memo[:, c, :], identb, [memo, cb], [b0])
                  cp("act", brT[:].rearrange("p a c t -> p (a c) t"), b0[:], [b0], [brT])
                  pyT = a1[:, 1024:1536].rearrange("p (g t) -> p g t", g=4)
                  for n in range(2):
                      for bi in range(3):
                          p_ = b34[bi % 2]
                          for c in range(4):
                              lt = brT[:, 0, c, :] if bi == 0 else (pyT[:, c, :] if bi == 1 else brT[:, 1, c, :])
                              mm(p_[:], lt, wup[:, bi, c, n * 512:(n + 1) * 512], c == 0, c == 3, [brT, a1, wup], [p_])
                          mgv = a1[:, 1536 + bi * 1024 + n * 512:1536 + bi * 1024 + (n + 1) * 512]
                          if bi == 0:
                              tt("dve", hsum[:, n * 512:(n + 1) * 512], p_[:], mgv, ALU.mult, [p_, a1], [hsum])
                          else:
                              tt("dve", htmp[:], p_[:], mgv, ALU.mult, [p_, a1], [htmp])
                              if "xpm"[bi] in os.environ.get("KOFF", ""):
                                  ts("dve", htmp[:], htmp[:], 0.0, None, ALU.mult, R=[htmp], W=[htmp])
                              if bi == 1:
                                  tt("pool", hsum[:, n * 512:(n + 1) * 512], hsum[:, n * 512:(n + 1) * 512], htmp[:], ALU.add, [hsum, htmp], [hsum])
                              else:
                                  tt("pool", hb[:, n * 512:(n + 1) * 512], hsum[:, n * 512:(n + 1) * 512], htmp[:], ALU.add, [hsum, htmp], [hb])
                  for c in range(8):
                      tr(b7[:, c, :], hb[:, c * 128:(c + 1) * 128], identb, [hb, cb], [b7])
                  cp("act", hT2[:], b7[:], [b7], [hT2])
                  for n in range(2):
                      p_ = b34[n]
                      for c in range(8):
                          mm(p_[:], hT2[:, c, :], wo[:, c, n * 512:(n + 1) * 512], c == 0, c == 7, [hT2, wo], [p_])
                      tt("dve", xt[:, n * 512:(n + 1) * 512], xt[:, n * 512:(n + 1) * 512], p_[:], ALU.add, [xt, p_], [xt])
                  dma("sp", xs[i * 128:(i + 1) * 128, :], xt[:], R=[xt], W=[xs_d[i]])
                  chk(5.4 + 0.01 * i)
              fw.barrier()
              p2.close()
          lays.close()

          chk(8)
          p3 = ExitStack()
          wgu = sb(p3, "wgu", [128, 8, 2 * DFF], BF16)
          wdn = sb(p3, "wdn", [128, 22, D], BF16)
          gfT = sb(p3, "gfT", [128, 8])
          x4 = [sb(p3, f"x4_{k}", [128, D]) for k in range(2)]
          xb3 = sb(p3, "xb3", [128, D], BF16); junk3 = sb(p3, "junk3", [128, D], BF16); ss3 = sb(p3, "ss3", [128, 1])
          hT3 = sb(p3, "hT3", [128, 8, 256], BF16)
          aT = sb(p3, "aT", [128, 22, 256], BF16)
          sg = [sb(p3, f"sg{k}", [128, 256]) for k in range(2)]
          ptb3 = pst(p3, "ptb3", [128, 8, 128], BF16)
          pgk = [pst(p3, f"pgk{k}", [128, 512]) for k in range(2)]
          puk = [pst(p3, f"puk{k}", [128, 512]) for k in range(2)]
          pdn = [pst(p3, f"pdn{k}", [128, 512]) for k in range(2)]
          for c in range(8):
              dma("pool", wgu[:, c, :], w_gu[l, c * 128:(c + 1) * 128, :], W=[wgu])
          dma("pool", wdn[:], w_dn[l].rearrange("(c p) n -> p c n", p=128), W=[wdn])
          dma_nc("sp", gfT[:], w_ffng[l].rearrange("(c p) -> p c", p=128), W=[gfT])
          st = 0
          kk = 0
          while st < NTT:
              nt_ = min(2, NT - st) if st < NT else 1
              TW = nt_ * 128
              for k in range(nt_):
                  i = st + k
                  dma("sp", x4[k][:], xs[i * 128:(i + 1) * 128, :], R=[xs_d[i]], W=[x4[k]])
                  act("act", junk3[:], x4[k][:], AF.Square, accum=ss3[:], R=[x4[k]], W=[junk3, ss3])
                  ts("dve", ss3[:], ss3[:], 1.0 / D, EPS, ALU.mult, ALU.add, R=[ss3], W=[ss3])
                  rsqrt_chain(ss3)
                  act("act", xb3[:], x4[k][:], AF.Copy, scale=ss3[:, 0:1], R=[x4[k], ss3], W=[xb3])
                  for c in range(8):
                      tr(ptb3[:, c, :], xb3[:, c * 128:(c + 1) * 128], identb, [xb3, cb], [ptb3])
                  for c in range(8):
                      ts("dve", hT3[:, c, k * 128:(k + 1) * 128], ptb3[:, c, :], gfT[:, c:c + 1], None, ALU.mult, R=[ptb3, gfT], W=[hT3])
              for fc in range(22):
                  pg_, pu_, sg_ = pgk[kk % 2], puk[kk % 2], sg[kk % 2]
                  kk += 1
                  for c in range(8):
                      mm(pg_[:, 0:TW], wgu[:, c, fc * 128:(fc + 1) * 128], hT3[:, c, 0:TW], c == 0, c == 7, [wgu, hT3], [pg_])
                  for c in range(8):
                      mm(pu_[:, 0:TW], wgu[:, c, DFF + fc * 128:DFF + (fc + 1) * 128], hT3[:, c, 0:TW], c == 0, c == 7, [wgu, hT3], [pu_])
                  act("act", sg_[:, 0:TW], pg_[:, 0:TW], AF.Silu, R=[pg_], W=[sg_])
                  tt("dve", aT[:, fc, 0:TW], pu_[:, 0:TW], sg_[:, 0:TW], ALU.mult, [pu_, sg_], [aT])
              for k in range(nt_):
                  i = st + k
                  for n in range(2):
                      p_ = pdn[n]
                      for fc in range(22):
                          mm(p_[:], aT[:, fc, k * 128:(k + 1) * 128], wdn[:, fc, n * 512:(n + 1) * 512], fc == 0, fc == 21, [aT, wdn], [p_])
                      tt("dve", x4[k][:, n * 512:(n + 1) * 512], x4[k][:, n * 512:(n + 1) * 512], p_[:], ALU.add, [x4[k], p_], [x4[k]])
                  if l == NL - 1:
                      outdma("sp", o_y[i * 128:(i + 1) * 128, :], x4[k][:], R=[x4[k]])
                  else:
                      dma("sp", xs[i * 128:(i + 1) * 128, :], x4[k][:], R=[x4[k]], W=[xs_d[i]])
              st += nt_
          fw.barrier()
          p3.close()

    except _Stop:
        pass
    print("OPCOUNT", _opc[0])
    fw.finish_deps = outdeps
    for d in outdeps:
        for o in [d.w] + list(d.r):
            w = fw._need("sp", o)
            if w:
                fw.ops["sp"].append(("wait", w))
    fw.emit()
    return nc


WNAMES = ["attn_norm_g", "w_in", "nsa_q_g", "nsa_kc_g", "nsa_ks_g", "nsa_kw_g", "cmp_pe_k", "cmp_pe_v", "cmp_wk", "cmp_wv",
          "w_pool", "pool_scale", "mem_norm_g", "w_mem_kv", "mem_q_g", "mem_k_g", "w_up_nsa", "w_up_pool", "w_up_mem",
          "w_out", "ffn_norm_g", "w_gate_up", "w_down"]


def kernel(x_prompt, x_sample, mem_prompt, cache_nsa_kv, cache_win_kv, state_pool, cache_mem_kv, page_table, **w):
    x_prompt = np.asarray(x_prompt); x_sample = np.asarray(x_sample)
    B, S, _ = x_prompt.shape
    NL = cache_nsa_kv.shape[0]
    NPHYS = cache_nsa_kv.shape[1]
    NPG = page_table.shape[1]
    NT = S // 128
    NTT = NT + 1
    ncore = 8
    consts = make_consts(NT)
    CW = consts.shape[1]
    e64 = (np.arange(64)[:, None] == (np.arange(S)[None, :] // 64) % 64).astype(np.float32)
    nc = build(NL, NT, NPG, NPHYS, CW)
    cn = np.ascontiguousarray(np.asarray(cache_nsa_kv)).reshape(NL, NPHYS * 128, 512)
    in_maps = []
    for c in range(ncore):
        b = c % B
        xin = np.zeros((NTT * 128, D), np.float32)
        xin[:S] = x_prompt[b]
        xin[S:S + NSEQ] = x_sample[NSEQ * c:NSEQ * c + NSEQ, 0]
        m = {"xin": xin, "memin": np.ascontiguousarray(mem_prompt[b]), "cnsa": cn,
             "cwin": np.ascontiguousarray(np.asarray(cache_win_kv)[:, NSEQ * c:NSEQ * c + NSEQ]).reshape(NL, NSEQ, 512, 256),
             "spool": np.ascontiguousarray(np.asarray(state_pool)[:, NSEQ * c:NSEQ * c + NSEQ]),
             "cmem": np.ascontiguousarray(np.asarray(cache_mem_kv)[:, NSEQ * c:NSEQ * c + NSEQ]).reshape(NL, NSEQ, 256, 1024),
             "ptab": np.ascontiguousarray(np.asarray(page_table)[NSEQ * c:NSEQ * c + NSEQ]).reshape(1, NSEQ * NPG).astype(np.int32),
             "consts": consts, "e64": e64}
        for n in WNAMES:
            m[n] = np.ascontiguousarray(np.asarray(w[n], dtype=np.float32))
        in_maps.append(m)
    res = run_bass_kernel_spmd(nc, in_maps, core_ids=list(range(ncore)))
    R = res.results
    DB = x_sample.shape[0]
    y_p = np.stack([R[b]["o_y"][:S] for b in range(B)])
    y_s = np.concatenate([R[c]["o_y"][S:S + NSEQ] for c in range(ncore)])[:, None, :]
    nsa_p = np.stack([R[b]["o_nsa"][:, :S] for b in range(B)], axis=1).reshape(NL, B, S, 4, 2, 64)
    nsa_s = np.concatenate([R[c]["o_nsa"][:, S:S + NSEQ] for c in range(ncore)], axis=1).reshape(NL, DB, 1, 4, 2, 64)
    win_p = np.stack([R[b]["o_winp"] for b in range(B)], axis=1).reshape(NL, B, 512, 2, 2, 64)
    win_s = np.concatenate([R[c]["o_wins"] for c in range(ncore)], axis=1).reshape(NL, DB, 512, 2, 2, 64)
    pool_p = np.stack([R[b]["o_poolp"] for b in range(B)], axis=1)
    pool_s = np.concatenate([R[c]["o_pools"] for c in range(ncore)], axis=1)
    mem_p = np.stack([R[b]["o_memkv"] for b in range(B)], axis=1).reshape(NL, B, 256, 2, 4, 128)
    return (y_p.astype(np.float32), y_s.astype(np.float32), nsa_p, nsa_s, win_p, win_s, pool_p, pool_s, mem_p)
```

```python
import os
import numpy as np
import concourse.bass as bass
import concourse.mybir as mybir
from concourse.alu_op_type import AluOpType as ALU
from concourse.bass_utils import run_bass_kernel_spmd

F32 = mybir.dt.float32
BF16 = mybir.dt.bfloat16
I32 = mybir.dt.int32
AF = mybir.ActivationFunctionType
AX = mybir.AxisListType

ENGS = ("pe", "act", "dve", "pool", "sp")
SEM_LIMIT = 12000
N_DMA_SEMS = 8

D = 1024
PW = 5400
DFF = 2816
NSEQ = 4
EPS = 1e-6
BIGM = 30000.0


class _Stop(Exception):
    pass


KOPS = int(os.environ.get("KOPS", "0"))
_opc = [0]


def _count():
    _opc[0] += 1
    if os.environ.get("KWHO") and _opc[0] == int(os.environ["KWHO"]):
        import traceback
        print("WHO", _opc[0], [f"{f.lineno}:{f.line}" for f in traceback.extract_stack(limit=6)[:-2]])
    if KOPS and _opc[0] >= KOPS:
        raise _Stop()


class Dep:
    __slots__ = ("w", "r")

    def __init__(self):
        self.w = None
        self.r = []


class FW:
    def __init__(self, nc):
        self.nc = nc
        self.ops = {e: [] for e in ENGS}
        self.nops = {e: 0 for e in ENGS}
        self.signaled = {e: set() for e in ENGS}
        self.waited = {e: {} for e in ENGS}
        self.dma_count = {}
        self.relax = False
        self.dbg = {}
        self.dma_rr = {e: 0 for e in ENGS}

    def _need(self, e, opid):
        if opid is None:
            return None
        if opid[0] == "c":
            _, f, k = opid
            if f == e and e == "pe":
                return None
            if self.waited[e].get(("c", f), 0) >= k:
                return None
            self.waited[e][("c", f)] = k
            self.signaled[f].add(k)
            return opid
        _, q, j, m = opid
        if self.waited[e].get(("d", q, j), 0) >= m:
            return None
        self.waited[e][("d", q, j)] = m
        return opid

    def _collect(self, e, reads, writes):
        waits = []
        for d in reads:
            w = self._need(e, d.w)
            if w:
                waits.append(w)
        strict = not self.relax
        for d in writes:
            if strict or not (d.w is not None and d.w[0] == "c" and d.w[1] == e):
                w = self._need(e, d.w)
                if w:
                    waits.append(w)
            for r in d.r:
                if r[0] == "c" and r[1] == e and not strict:
                    continue
                w = self._need(e, r)
                if w:
                    waits.append(w)
        return waits

    def _mark(self, opid, reads, writes):
        for d in reads:
            d.r.append(opid)
            if len(d.r) > 48:
                last = {}
                for o in d.r:
                    key = o[:2] if o[0] == "c" else o[:3]
                    if key not in last or o[-1] > last[key][-1]:
                        last[key] = o
                d.r = list(last.values())
        for d in writes:
            d.w = opid
            d.r = []

    def op(self, e, fn, reads=(), writes=()):
        _count()
        for w in self._collect(e, reads, writes):
            self.ops[e].append(("wait", w))
        self.nops[e] += 1
        k = self.nops[e]
        if os.environ.get("KDBG"):
            import traceback
            st = traceback.extract_stack(limit=6)
            self.dbg[(e, k)] = " <- ".join(f"{f.lineno}" for f in st[:-1])
        self.ops[e].append(("op", fn, k))
        self._mark(("c", e, k), reads, writes)

    def dma(self, q, fn, reads=(), writes=()):
        _count()
        waits = self._collect(q, reads, writes)
        j = self.dma_rr[q]
        self.dma_rr[q] = (j + 1) % N_DMA_SEMS
        m_prev = self.dma_count.get((q, j), 0)
        if m_prev > 0:
            w = self._need(q, ("d", q, j, m_prev))
            if w:
                waits.append(w)
        for w in waits:
            self.ops[q].append(("wait", w))
        m = m_prev + 1
        self.dma_count[(q, j)] = m
        self.ops[q].append(("dma", fn, j))
        self._mark(("d", q, j, m), reads, writes)

    def barrier(self):
        if os.environ.get("KNOBAR"):
            return
        ids = {}
        for e in ("act", "dve", "pool"):
            self.op(e, self.bar_ops[e], reads=[self.bar_dep])
            ids[e] = ("c", e, self.nops[e])
        if self.nops["pe"] > 0:
            ids["pe"] = ("c", "pe", self.nops["pe"])
        dmas = [("d", q, j, m) for (q, j), m in self.dma_count.items()]
        for e in ENGS:
            for f, o in ids.items():
                if f != e:
                    w = self._need(e, o)
                    if w:
                        self.ops[e].append(("wait", w))
            for o in dmas:
                w = self._need(e, o)
                if w:
                    self.ops[e].append(("wait", w))

    def emit(self):
        nc = self.nc
        sigval, sems = {}, {}
        for e in ENGS:
            epoch, cnt = 0, 0
            for k in range(1, self.nops[e] + 1):
                if k in self.signaled[e]:
                    if cnt >= SEM_LIMIT:
                        epoch += 1
                        cnt = 0
                    cnt += 1
                    if (e, epoch) not in sems:
                        sems[(e, epoch)] = nc.alloc_semaphore(f"s_{e}_{epoch}")
                    sigval[(e, k)] = (sems[(e, epoch)], cnt)
                    if os.environ.get("KDBG") and e == os.environ.get("KDBG_E", "pe"):
                        print("SIG", e, epoch, cnt, "op", k, self.dbg.get((e, k)))
        dsems = {key: nc.alloc_semaphore(f"d_{key[0]}_{key[1]}") for key in self.dma_count}

        def run(E, e):
            for item in self.ops[e]:
                if item[0] == "wait":
                    w = item[1]
                    if w[0] == "c":
                        s, v = sigval[(w[1], w[2])]
                        E.wait_ge(s, v)
                    else:
                        E.wait_ge(dsems[(w[1], w[2])], 16 * w[3])
                elif item[0] == "op":
                    ins = item[1](E)
                    if item[2] in self.signaled[e]:
                        ins.then_inc(sigval[(e, item[2])][0], 1)
                else:
                    item[1](E).then_inc(dsems[(e, item[2])], 16)

        with nc.Block() as block:
            @block.sync
            def _(E):
                run(E, "sp")

            @block.gpsimd
            def _(E):
                run(E, "pool")

            @block.vector
            def _(E):
                run(E, "dve")

            @block.scalar
            def _(E):
                run(E, "act")

            @block.tensor
            def _(E):
                run(E, "pe")


C_OFF = {}


def make_consts(NT):
    cols = []
    off = [0]

    def add(name, a):
        a = np.asarray(a, np.float32)
        assert a.shape[0] == 128
        a = a.reshape(128, -1)
        C_OFF[name] = (off[0], a.shape[1])
        off[0] += a.shape[1]
        cols.append(a)

    p = np.arange(128)
    add("ident", np.eye(128))
    add("tric", (p[:, None] <= p[None, :]) * 1.0)
    add("trib", (p[:, None] > p[None, :]) * 1.0)
    add("selall", (p[:, None, None] == np.arange(4)[None, :, None]) * np.ones((1, 1, 128)))
    add("ones", np.ones((128, 1)))
    add("pm", (p[:, None] // 32 == np.arange(4)[None, :]) / 32.0)
    u = (p + 1) // 32
    xx = np.arange(252)
    add("tcmp", (xx[None, :] < 124 + u[:, None]) * 1.0)
    xs = np.arange(126)
    rel = xs[None, :] - 62 - (p[:, None] >= 64)
    add("bsel", np.where((rel == 0) | (rel == -1), 1e9, np.where(rel > 0, -1e9, 0.0)))
    wins = (2, 4, 8, 16)
    acur = np.zeros((128, 4, 128)); aprev = np.zeros((128, 4, 128)); afirst = np.zeros((128, 4, 128))
    for g, w in enumerate(wins):
        for t in range(128):
            for s_ in range(t - w + 1, t + 1):
                if s_ >= 0:
                    acur[s_, g, t] += 1.0 / w
                    afirst[s_, g, t] += 1.0 / min(t + 1, w)
                else:
                    aprev[128 + s_, g, t] += 1.0 / w
            acur[t, g, t] -= 1.0
            afirst[t, g, t] -= 1.0
    add("acur", acur); add("aprev", aprev); add("afirst", afirst)
    ast = np.zeros((128, 4))
    for g, w in enumerate(wins):
        for j in range(15):
            if j >= 15 - (w - 1):
                ast[j, g] = 1.0 / w
    add("ast", ast)
    anew = np.zeros((128, 4, 4))
    for s_ in range(4):
        for g, w in enumerate(wins):
            anew[s_, s_, g] = 1.0 / w - 1.0
    add("anew", anew)
    add("rowsel", (p[:, None] == np.arange(4)[None, :]) * 1.0)
    add("g8", ((p[:, None] // 4) == np.arange(2)[None, :]) * (p[:, None] < 8))
    inde = np.zeros((128, 2, 2, 128))
    for kv in range(2):
        for par in range(2):
            inde[kv, kv, par, :] = (np.arange(128) // 64 == par)
    add("inde", inde)
    add("sr", np.ones((128, 1, 1)) * (np.arange(4)[None, :, None] == np.arange(128)[None, None, :]))
    add("wmask", (p[:, None] != 0) * 1.0)
    add("npad", np.maximum(0.0, 511.0 - (128.0 * np.arange(4)[None, :] + p[:, None])))
    add("iota", p[:, None] * 1.0)
    return np.concatenate(cols, axis=1)


def build(NL, NT, NPG, NPHYS, CW):
    S = NT * 128
    NTT = NT + 1
    NB = 4 * NT
    NBS = 4 * NPG
    CB = min(128, NBS)
    NCK = NBS // CB
    A1W = 4608
    nc = bass.Bass("TRN2", target_bir_lowering=False)
    fw = FW(nc)

    def din(name, shape, dt=F32):
        return nc.dram_tensor(name, shape, dt, kind="ExternalInput").ap()

    def dout(name, shape):
        return nc.dram_tensor(name, shape, F32, kind="ExternalOutput").ap()

    xin = din("xin", [NTT * 128, D]); memin = din("memin", [256, D])
    cnsa = din("cnsa", [NL, NPHYS * 128, 512]); cwin = din("cwin", [NL, NSEQ, 512, 256])
    spool = din("spool", [NL, NSEQ, 15, 512]); cmem = din("cmem", [NL, NSEQ, 256, 1024])
    ptab = din("ptab", [1, NSEQ * NPG], I32)
    consts = din("consts", [128, CW]); e64 = din("e64", [64, S])
    w_attn_g = din("attn_norm_g", [NL, D]); w_in = din("w_in", [NL, D, PW])
    w_qg = din("nsa_q_g", [NL, 64]); w_kcg = din("nsa_kc_g", [NL, 64]); w_ksg = din("nsa_ks_g", [NL, 64]); w_kwg = din("nsa_kw_g", [NL, 64])
    w_pek = din("cmp_pe_k", [NL, 32, 64]); w_pev = din("cmp_pe_v", [NL, 32, 64])
    w_cwk = din("cmp_wk", [NL, 64, 64]); w_cwv = din("cmp_wv", [NL, 64, 64])
    w_pool = din("w_pool", [NL, 4, 128, 128]); w_psc = din("pool_scale", [NL, 512])
    w_memg = din("mem_norm_g", [NL, D]); w_memkv = din("w_mem_kv", [NL, D, 1024])
    w_mqg = din("mem_q_g", [NL, 128]); w_mkg = din("mem_k_g", [NL, 128])
    w_upn = din("w_up_nsa", [NL, 512, D]); w_upp = din("w_up_pool", [NL, 512, D]); w_upm = din("w_up_mem", [NL, 512, D])
    w_out = din("w_out", [NL, D, D]); w_ffng = din("ffn_norm_g", [NL, D])
    w_gu = din("w_gate_up", [NL, D, 2 * DFF]); w_dn = din("w_down", [NL, DFF, D])

    o_y = dout("o_y", [NTT * 128, D]); o_nsa = dout("o_nsa", [NL, NTT * 128, 512])
    o_winp = dout("o_winp", [NL, 512, 256]); o_wins = dout("o_wins", [NL, NSEQ, 512, 256])
    o_poolp = dout("o_poolp", [NL, 15, 512]); o_pools = dout("o_pools", [NL, NSEQ, 15, 512])
    o_memkv = dout("o_memkv", [NL, 256, 1024])
    outdeps = []

    xs = nc.dram_tensor("xs", [NTT * 128, D], F32, kind="Internal").ap()
    a1s = nc.dram_tensor("a1s", [NTT, 128, A1W], BF16, kind="Internal").ap()
    xs_d = [Dep() for _ in range(NTT)]
    a1s_d = [Dep() for _ in range(NTT)]
    a1s_d2 = [Dep() for _ in range(NTT)]

    class Tl:
        def __init__(self, h, n=1):
            self.h = h
            self.d = Dep()
            self.ds = [Dep() for _ in range(n)]

        def __getitem__(self, k):
            return self.h[k]

    def op(e, fn, R=(), W=()):
        fw.op(e, fn, [t.d if isinstance(t, Tl) else t for t in R], [t.d if isinstance(t, Tl) else t for t in W])

    def dma(q, out, in_, R=(), W=()):
        fw.dma(q, lambda E: E.dma_start(out=out, in_=in_), [t.d if isinstance(t, Tl) else t for t in R],
               [t.d if isinstance(t, Tl) else t for t in W])

    def dma_nc(q, out, in_, R=(), W=()):
        fw.dma(q, lambda E: E.dma_start(out=out, in_=in_, allow_slow_non_contiguous=True), [t.d if isinstance(t, Tl) else t for t in R],
               [t.d if isinstance(t, Tl) else t for t in W])

    def outdma(q, out, in_, R=()):
        d = Dep()
        outdeps.append(d)
        dma(q, out, in_, R=R, W=[d])

    def mm(ps, lhsT, rhs, start, stop, R, W):
        if os.environ.get("KNOF32") and lhsT.dtype == F32:
            return
        op("pe", lambda E: E.matmul(ps, lhsT=lhsT, rhs=rhs, start=start, stop=stop), R, W)

    def tr(ps, in_, idn, R, W):
        op("pe", lambda E: E.transpose(out=ps, in_=in_, identity=idn), R, W)

    def act(e, out, in_, func=AF.Copy, scale=None, accum=None, R=(), W=()):
        kw = {}
        if scale is not None:
            kw["scale"] = scale
        if accum is not None:
            kw["accum_out"] = accum
        op("act", lambda E: E.activation(out=out, in_=in_, func=func, **kw), R, W)

    def cp(e, out, in_, R, W):
        if e == "act":
            op("act", lambda E: E.activation(out=out, in_=in_, func=AF.Copy), R, W)
        elif e == "dve" and out.dtype == in_.dtype and out.dtype == BF16 and not os.environ.get("KCPRAW"):
            op(e, lambda E: E.tensor_scalar(out=out, in0=in_, scalar1=1.0, scalar2=None, op0=ALU.mult), R, W)
        else:
            op(e, lambda E: E.tensor_copy(out=out, in_=in_), R, W)

    def ts(e, out, in0, s1, s2, op0, op1=None, R=(), W=()):
        if op1 is None:
            op(e, lambda E: E.tensor_scalar(out=out, in0=in0, scalar1=s1, scalar2=None, op0=op0), R, W)
        else:
            op(e, lambda E: E.tensor_scalar(out=out, in0=in0, scalar1=s1, scalar2=s2, op0=op0, op1=op1), R, W)

    def tt(e, out, in0, in1, o, R, W):
        op(e, lambda E: E.tensor_tensor(out=out, in0=in0, in1=in1, op=o), R, W)

    def stt(out, in0, sc, in1, op0, op1, R, W):
        op("dve", lambda E: E.scalar_tensor_tensor(out=out, in0=in0, scalar=sc, in1=in1, op0=op0, op1=op1), R, W)

    def red(out, in_, R, W, o=ALU.add):
        op("dve", lambda E: E.tensor_reduce(out=out, in_=in_, axis=AX.X, op=o), R, W)

    import contextlib

    @contextlib.contextmanager
    def relaxed():
        old = fw.relax
        fw.relax = True
        try:
            yield
        finally:
            fw.relax = old

    def memset(e, ap, v, W, st=False):
        op(e, lambda E: E.memset(ap, v), [], W)

    def recip(out, in_, R, W):
        op("dve", lambda E: E.reciprocal(out=out, in_=in_), R, W)

    def vmax(out, in_, R, W):
        op("dve", lambda E: E.max(out=out, in_=in_), R, W)

    def mrep(out, rep, vals, imm, R, W):
        op("dve", lambda E: E.match_replace(out=out, in_to_replace=rep, in_values=vals, imm_value=imm), R, W)

    def ttr(out, in0, in1, accum, R, W):
        op("dve", lambda E: E.scalar_tensor_tensor(out=out, in0=in0, scalar=1.0, in1=in1, op0=ALU.mult, op1=ALU.mult, accum_out=accum), R, W)

    def rsqrt_chain(v, R_extra=()):
        act("act", v[:], v[:], AF.Sqrt, R=[v], W=[v])
        recip(v[:], v[:], [v], [v])

    from contextlib import ExitStack
    root = ExitStack()

    _uid = [0]

    def sb(stack, name, shape, dt=F32, n=1):
        _uid[0] += 1
        return Tl(stack.enter_context(nc.sbuf_tensor(f"{name}_{_uid[0]}", shape, dt)), n)

    def pst(stack, name, shape, dt=F32):
        _uid[0] += 1
        return Tl(stack.enter_context(nc.psum_tensor(f"{name}_{_uid[0]}", shape, dt)))

    cf = sb(root, "cf", [128, CW])
    CBW = 897
    cb = sb(root, "cb", [128, CBW], BF16)
    dma("sp", cf[:], consts[:, :], W=[cf])
    cp("dve", cb[:], cf[:, 0:CBW], [cf], [cb])

    def CF(name, sub=None):
        o, n = C_OFF[name]
        return cf.h[:, o:o + n]

    def CB_(name):
        o, n = C_OFF[name]
        return cb.h[:, o:o + n]

    identf = CF("ident"); identb = CB_("ident")
    pmf = CF("pm")
    tric_b = CB_("tric"); trib_b = CB_("trib")
    tcmp = CF("tcmp"); bsel = CF("bsel")
    acur = CF("acur").rearrange("p (g t) -> p g t", g=4); aprev = CF("aprev").rearrange("p (g t) -> p g t", g=4)
    afirst = CF("afirst").rearrange("p (g t) -> p g t", g=4)
    ast = CF("ast"); anew = CF("anew").rearrange("p (s g) -> p s g", s=4)
    selall_b = CB_("selall").rearrange("p (s m) -> p s m", s=4)
    rowsel = CF("rowsel"); g8 = CF("g8")
    inde = CF("inde").rearrange("p (a b k) -> p a b k", a=2, b=2)
    sr_f = CF("sr").rearrange("p (s m) -> p s m", s=4)
    npad = CF("npad")
    wmask = CF("wmask"); iota = CF("iota"); onesf = CF("ones"); onesb = CB_("ones")

    pti = sb(root, "pti", [128, NSEQ * NPG], I32)
    ptf = sb(root, "ptf", [128, NSEQ * NPG], F32)
    pidx = sb(root, "pidx", [128, NL, NSEQ * NPG], I32)
    ptf2 = sb(root, "ptf2", [128, NSEQ * NPG], F32)
    dma("sp", pti[:], ptab[0:1, :].to_broadcast([128, NSEQ * NPG]), W=[pti])
    cp("dve", ptf[:], pti[:], [pti], [ptf])
    ts("dve", ptf[:], ptf[:], 128.0, iota[:, 0:1], ALU.mult, ALU.add, R=[ptf, cf], W=[ptf])
    for l_ in range(NL):
        ts("dve", ptf2[:], ptf[:], float(l_ * NPHYS * 128), None, ALU.add, R=[ptf], W=[ptf2])
        cp("dve", pidx[:, l_, :], ptf2[:], [ptf2], [pidx])
    cnsa_flat = cnsa.rearrange("l r c -> (l r) c")

    bscr = sb(root, "bscr", [128, 4])
    memset("dve", bscr[:], 0.0, [bscr])
    _b0, _b1, _b2 = bscr[0:1, 0:1], bscr[0:1, 1:2], bscr[0:1, 2:3]
    fw.bar_dep = bscr.d
    fw.bar_ops = {"pe": lambda E: E.nop(), "sp": lambda E: E.nop(),
                  "act": lambda E: E.activation(out=_b0, in_=_b0, func=AF.Copy),
                  "dve": lambda E: E.tensor_copy(out=_b1, in_=_b1),
                  "pool": lambda E: E.tensor_copy(out=_b2, in_=_b2)}
    zerob = sb(root, "zerob", [128, 260], BF16)
    memset("pool", zerob[:], 0.0, [zerob])
    nsag = sb(root, "nsag", [128, NTT, 24])
    sampkv = sb(root, "sampkv", [128, 512])
    rr = [0]

    def alt(a="act", b="dve"):
        rr[0] += 1
        return a if rr[0] % 2 else b

    import os
    _stop = float(os.environ.get("KSTOP", "99"))

    KCUT = int(os.environ.get("KCUT", "99"))

    def chk(n):
        if n >= _stop:
            raise _Stop()

    try:
      chk(1)
      for l in range(NL):
          xsrc = xin if l == 0 else xs
          lays = ExitStack()
          lay = ExitStack()
          gq = sb(lays, "gq", [128, 64]); gkc = sb(lays, "gkc", [128, 64]); gks = sb(lays, "gks", [128, 64]); gkw = sb(lays, "gkw", [128, 64])
          gmq = sb(lays, "gmq", [128, 128]); gmk = sb(lays, "gmk", [128, 128])
          pemT = sb(lays, "pemT", [128, 2])
          wkbd = sb(lays, "wkbd", [128, 2, 128])
          kselT = sb(lay, "kselT", [128, 2, S], BF16, NT)
          vsel = sb(lay, "vsel", [128, NT, 2, 65], BF16, NT)
          kwinT = sb(lay, "kwinT", [64, 2, S], BF16, NT)
          vwin = sb(lay, "vwin", [128, NT, 2, 65], BF16, NT)
          kcT2 = sb(lay, "kcT2", [128, 2, 128], BF16)
          vcaug = sb(lay, "vcaug", [128, 2, 65], BF16)
          mkT = sb(lay, "mkT", [128, 4, 256], BF16)
          mva = sb(lay, "mva", [128, 2, 4, 129], BF16)
          for t_, src in ((gq, w_qg), (gkc, w_kcg), (gks, w_ksg), (gkw, w_kwg)):
              dma("sp", t_[:], src[l:l + 1, :].to_broadcast([128, 64]), W=[t_])
          for t_, src in ((gmq, w_mqg), (gmk, w_mkg)):
              dma("sp", t_[:], src[l:l + 1, :].to_broadcast([128, 128]), W=[t_])
          ts("dve", gq[:], gq[:], 0.125, None, ALU.mult, R=[gq], W=[gq])
          ts("dve", gmq[:], gmq[:], 128 ** -0.5, None, ALU.mult, R=[gmq], W=[gmq])
          dma("pool", kselT[64:128, 0, :], e64[:, :], W=kselT.ds + [kselT])
          dma("pool", kselT[64:128, 1, :], e64[:, :], W=kselT.ds + [kselT])
          memset("pool", vsel[:, :, :, 64:65], 1.0, vsel.ds + [vsel])
          memset("pool", vwin[:, :, :, 64:65], 1.0, vwin.ds + [vwin])
          memset("pool", vcaug[:, :, 64:65], 1.0, [vcaug])
          memset("pool", mva[:, :, :, 128:129], 1.0, [mva])
          memset("pool", wkbd[:], 0.0, [wkbd])
          for a_, src in ((0, w_cwk), (1, w_cwv)):
              dma("sp", wkbd[0:64, a_, 0:64], src[l], W=[wkbd])
              dma("sp", wkbd[64:128, a_, 64:128], src[l], W=[wkbd])

          p1 = ExitStack()
          win_sb = sb(p1, "win_sb", [128, 8, 2328], BF16)
          gT = sb(p1, "gT", [128, 8])
          wpl = sb(p1, "wpl", [128, 4, 128], BF16)
          psc = sb(p1, "psc", [128, 4])
          pe2 = sb(p1, "pe2", [32, 2, 128])
          xt = sb(p1, "xt", [128, D]); xb = sb(p1, "xb", [128, D], BF16); hT = sb(p1, "hT", [128, 8, 128], BF16)
          junk = sb(p1, "junk", [128, D], BF16)
          z = sb(p1, "z", [128, 2328]); a1 = sb(p1, "a1", [128, 1536], BF16)
          sq = sb(p1, "sq", [128, 1280]); rs = sb(p1, "rs", [128, 16]); ss = sb(p1, "ss", [128, 1])
          ksb = sb(p1, "ksb", [128, 256], BF16)
          uprev = sb(p1, "uprev", [128, 512]); dTb = sb(p1, "dTb", [128, 4, 128], BF16)
          stt_ = sb(p1, "stt_", [15, NSEQ, 512])
          pooled = sb(p1, "pooled", [128, 2, 128]); kcf = sb(p1, "kcf", [128, 256]); kcb = sb(p1, "kcb", [128, 2, 128], BF16)
          ptb = pst(p1, "ptb", [128, 8, 128], BF16)
          pz = [pst(p1, f"pz{i}", [128, 512]) for i in range(3)]
          pcmp = pst(p1, "pcmp", [128, 2, 128])
          pd = pst(p1, "pd", [128, 4, 128])
          pk = pst(p1, "pk", [128, 4, 128], BF16)
          pmisc = pst(p1, "pmisc", [128, 512])

          for c in range(8):
              dma("pool", win_sb[:, c, :], w_in[l, c * 128:(c + 1) * 128, 0:2328], W=[win_sb])
          dma_nc("sp", gT[:], w_attn_g[l].rearrange("(c p) -> p c", p=128), W=[gT])
          dma("pool", wpl[:], w_pool[l].rearrange("g c e -> c g e"), W=[wpl])
          dma_nc("sp", psc[:], w_psc[l].rearrange("(g e) -> e g", e=128), W=[psc])
          for a_, src in ((0, w_pek), (1, w_pev)):
              dma("sp", pe2[:, a_, 0:64], src[l], W=[pe2])
              dma("sp", pe2[:, a_, 64:128], src[l], W=[pe2])
          dma("sp", stt_[:], spool[l].rearrange("s j c -> j s c"), W=[stt_])
          for a_ in range(2):
              mm(pmisc[:, a_:a_ + 1], pe2[:, a_, :], pmf[0:32, 0:1], True, True, [pe2, cf], [pmisc])
          cp("dve", pemT[:], pmisc[:, 0:2], [pmisc], [pemT])

          def norm_transpose(xt_, gT_, hT_, rstd_):
              act("act", junk[:], xt_[:], AF.Square, accum=rstd_[:], R=[xt_], W=[junk, rstd_])
              ts("dve", rstd_[:], rstd_[:], 1.0 / D, EPS, ALU.mult, ALU.add, R=[rstd_], W=[rstd_])
              rsqrt_chain(rstd_)
              cp("dve", xb[:], xt_[:], [xt_], [xb])
              for c in range(8):
                  tr(ptb[:, c, :], xb[:, c * 128:(c + 1) * 128], identb, [xb, cb], [ptb])
              for c in range(8):
                  with relaxed():
                      e = alt()
                      if e == "act":
                          act("act", hT_[:, c, :], ptb[:, c, :], AF.Copy, scale=gT_[:, c:c + 1], R=[ptb, gT_], W=[hT_])
                      else:
                          ts("dve", hT_[:, c, :], ptb[:, c, :], gT_[:, c:c + 1], None, ALU.mult, R=[ptb, gT_], W=[hT_])

          chk(1.5)
          zi = [0]
          for i in range(NTT):
              samp = (i == NT)
              dma("sp", xt[:], xsrc[i * 128:(i + 1) * 128, :], R=[xs_d[i]], W=[xt])
              norm_transpose(xt, gT, hT, ss)
              if KCUT <= 1:
                  chk(1.6 + 0.01 * i); continue
              col = 0
              while col < 2328:
                  wdt = min(512, 2328 - col)
                  p_ = pz[zi[0] % 3]; zi[0] += 1
                  for c in range(8):
                      mm(p_[:, 0:wdt], hT[:, c, :], win_sb[:, c, col:col + wdt], c == 0, c == 7, [hT, win_sb], [p_])
                  if col < 2328:
                      with relaxed():
                          e = alt()
                          if e == "act":
                              act("act", z[:, col:col + wdt], p_[:, 0:wdt], AF.Copy, scale=ss[:, 0:1], R=[p_, ss], W=[z])
                          else:
                              ts("dve", z[:, col:col + wdt], p_[:, 0:wdt], ss[:, 0:1], None, ALU.mult, R=[p_, ss], W=[z])
                  else:
                      a0 = 1536 + (col - 2328)
                      act("act", a1[:, a0:a0 + wdt], p_[:, 0:wdt], AF.Sigmoid, scale=ss[:, 0:1], R=[p_, ss], W=[a1])
                  col += wdt
              if KCUT <= 2:
                  chk(1.6 + 0.01 * i); continue
              with relaxed():
                  tt("dve", sq[:, 0:512], z[:, 0:512], z[:, 0:512], ALU.mult, [z], [sq])
                  tt("dve", sq[:, 512:640], z[:, 768:896], z[:, 768:896], ALU.mult, [z], [sq])
                  tt("dve", sq[:, 640:768], z[:, 1024:1152], z[:, 1024:1152], ALU.mult, [z], [sq])
                  tt("dve", sq[:, 768:1280], z[:, 1816:2328], z[:, 1816:2328], ALU.mult, [z], [sq])
              red(rs[:, 0:12], sq[:, 0:768].rearrange("p (h d) -> p h d", d=64), [sq], [rs])
              red(rs[:, 12:16], sq[:, 768:1280].rearrange("p (h d) -> p h d", d=128), [sq], [rs])
              ts("dve", rs[:, 0:12], rs[:, 0:12], 1.0 / 64, EPS, ALU.mult, ALU.add, R=[rs], W=[rs])
              ts("dve", rs[:, 12:16], rs[:, 12:16], 1.0 / 128, EPS, ALU.mult, ALU.add, R=[rs], W=[rs])
              rsqrt_chain(rs)
              for h in range(8):
                  with relaxed():
                      stt(a1[:, h * 64:(h + 1) * 64], z[:, h * 64:(h + 1) * 64], rs[:, h:h + 1], gq[:], ALU.mult, ALU.mult, [z, rs, gq], [a1])
              for kv in range(2):
                  stt(z[:, 768 + kv * 64:832 + kv * 64], z[:, 768 + kv * 64:832 + kv * 64], rs[:, 8 + kv:9 + kv], gks[:], ALU.mult, ALU.mult, [z, rs, gks], [z])
                  stt(z[:, 1024 + kv * 64:1088 + kv * 64], z[:, 1024 + kv * 64:1088 + kv * 64], rs[:, 10 + kv:11 + kv], gkw[:], ALU.mult, ALU.mult, [z, rs, gkw], [z])
              for h in range(4):
                  with relaxed():
                      stt(a1[:, 512 + h * 128:640 + h * 128], z[:, 1816 + h * 128:1944 + h * 128], rs[:, 12 + h:13 + h], gmq[:], ALU.mult, ALU.mult, [z, rs, gmq], [a1])
              act("act", nsag[:, i, :], z[:, 1280:1304], AF.Sigmoid, R=[z], W=[nsag])
              if KCUT <= 3:
                  chk(1.6 + 0.01 * i); continue
              outdma("sp", o_nsa[l, i * 128:(i + 1) * 128, :], z[:, 512:1024], R=[z])
              if not samp:
                  if i >= NT - 4:
                      outdma("sp", o_winp[l, (i - (NT - 4)) * 128:(i - (NT - 4) + 1) * 128, :], z[:, 1024:1280], R=[z])
                  if i == NT - 1:
                      outdma("sp", o_poolp[l, :, :], z[113:128, 1304:1816], R=[z])
                  if KCUT <= 4:
                      chk(1.6 + 0.01 * i); continue
                  K5 = int(os.environ.get("K5", "9"))
                  cp("act", vsel[:, i, :, 0:64], z[:, 896:1024].rearrange("p (k d) -> p k d", k=2), [z], [vsel.ds[i]])
                  cp("act", vwin[:, i, :, 0:64], z[:, 1152:1280].rearrange("p (k d) -> p k d", k=2), [z], [vwin.ds[i]])
                  if K5 >= 2:
                      cp("act", ksb[:, 0:128], z[:, 768:896], [z], [ksb])
                      cp("act", ksb[:, 128:256], z[:, 1024:1152], [z], [ksb])
                  if K5 >= 3:
                      for q_ in range(4):
                          tr(pk[0:64, q_, :], ksb[:, q_ * 64:(q_ + 1) * 64], identb, [ksb, cb], [pk])
                  if K5 >= 4:
                      cp("act", kselT[0:64, :, i * 128:(i + 1) * 128], pk[0:64, 0:2, :], [pk], [kselT.ds[i]])
                  if K5 >= 5:
                      cp("act", kwinT[0:64, :, i * 128:(i + 1) * 128], pk[0:64, 2:4, :], [pk], [kwinT.ds[i]])
                  if KCUT <= 5:
                      chk(1.6 + 0.01 * i); continue
                  mm(pcmp[:, 0, 4 * i:4 * i + 4], z[:, 512:640], pmf, True, True, [z, cf], [pcmp])
                  mm(pcmp[:, 1, 4 * i:4 * i + 4], z[:, 640:768], pmf, True, True, [z, cf], [pcmp])
                  for g in range(4):
                      if i == 0:
                          mm(pd[:, g, :], z[:, 1304 + g * 128:1432 + g * 128], afirst[:, g, :], True, True, [z, cf], [pd])
                      else:
                          mm(pd[:, g, :], z[:, 1304 + g * 128:1432 + g * 128], acur[:, g, :], True, False, [z, cf], [pd])
                          mm(pd[:, g, :], uprev[:, g * 128:(g + 1) * 128], aprev[:, g, :], False, True, [uprev, cf], [pd])
                  cp("act", dTb[:], pd[:], [pd], [dTb])
                  for g in range(4):
                      mm(pd[:, g, :], wpl[:, g, :], dTb[:, g, :], True, True, [wpl, dTb], [pd])
                  for g in range(4):
                      ts("dve", a1[:, 1024 + g * 128:1152 + g * 128], pd[:, g, :], psc[:, g:g + 1], None, ALU.mult, R=[pd, psc], W=[a1])
                  cp("act", uprev[:], z[:, 1304:1816], [z], [uprev])
              else:
                  cp("dve", sampkv[:], z[:, 768:1280], [z], [sampkv])
                  outdma("sp", o_wins[l, :, 511, :], z[0:NSEQ, 1024:1280], R=[z])
                  outdma("sp", o_pools[l, :, 14, :], z[0:NSEQ, 1304:1816], R=[z])
                  outdma("sp", o_wins[l, :, 0:511, :], cwin[l, :, 1:512, :])
                  outdma("sp", o_pools[l, :, 0:14, :], spool[l, :, 1:15, :])
                  for g in range(4):
                      for s_ in range(NSEQ):
                          mm(pd[:, g, s_:s_ + 1], stt_[0:15, s_, g * 128:(g + 1) * 128], ast[0:15, g:g + 1], True, False, [stt_, cf], [pd])
                          mm(pd[:, g, s_:s_ + 1], z[0:NSEQ, 1304 + g * 128:1432 + g * 128], anew[0:NSEQ, s_, g:g + 1], False, True, [z, cf], [pd])
                  cp("act", dTb[:, :, 0:NSEQ], pd[:, :, 0:NSEQ], [pd], [dTb])
                  for g in range(4):
                      mm(pd[:, g, 0:NSEQ], wpl[:, g, :], dTb[:, g, 0:NSEQ], True, True, [wpl, dTb], [pd])
                  for g in range(4):
                      ts("dve", a1[:, 1024 + g * 128:1024 + g * 128 + NSEQ], pd[:, g, 0:NSEQ], psc[:, g:g + 1], None, ALU.mult, R=[pd, psc], W=[a1])
              dma("sp", a1s[i][:, 0:1536], a1[:], R=[a1], W=[a1s_d[i]])
              chk(1.6 + 0.01 * i)

          chk(2)
          for a_ in range(2):
              ts("dve", pooled[:, a_, 0:NB], pcmp[:, a_, 0:NB], pemT[:, a_:a_ + 1], None, ALU.add, R=[pcmp, pemT], W=[pooled])
          mm(pmisc[0:NB, 0:128], pooled[:, 0, 0:NB], wkbd[:, 0, :], True, True, [pooled, wkbd], [pmisc])
          mm(pmisc[0:NB, 128:256], pooled[:, 1, 0:NB], wkbd[:, 1, :], True, True, [pooled, wkbd], [pmisc])
          cp("act", kcf[0:NB, :], pmisc[0:NB, 0:256], [pmisc], [kcf])
          tt("dve", sq[0:NB, 0:128], kcf[0:NB, 0:128], kcf[0:NB, 0:128], ALU.mult, [kcf], [sq])
          red(rs[0:NB, 0:2], sq[0:NB, 0:128].rearrange("p (h d) -> p h d", d=64), [sq], [rs])
          ts("dve", rs[0:NB, 0:2], rs[0:NB, 0:2], 1.0 / 64, EPS, ALU.mult, ALU.add, R=[rs], W=[rs])
          act("act", rs[0:NB, 0:2], rs[0:NB, 0:2], AF.Sqrt, R=[rs], W=[rs])
          recip(rs[0:NB, 0:2], rs[0:NB, 0:2], [rs], [rs])
          memset("pool", kcb[:], 0.0, [kcb])
          for kv in range(2):
              for dup in range(2):
                  stt(kcb[0:NB, kv, dup * 64:(dup + 1) * 64], kcf[0:NB, kv * 64:(kv + 1) * 64], rs[0:NB, kv:kv + 1], gkc[0:NB, :], ALU.mult, ALU.mult, [kcf, rs, gkc], [kcb])
          for kv in range(2):
              tr(pk[:, kv, 0:NB], kcb[0:NB, kv, :], identb[0:NB, 0:NB], [kcb, cb], [pk])
          memset("pool", kcT2[:], 0.0, [kcT2])
          cp("act", kcT2[:, :, 0:NB], pk[:, 0:2, 0:NB], [pk], [kcT2])
          memset("pool", vcaug[:, :, 0:64], 0.0, [vcaug])
          cp("dve", vcaug[0:NB, :, 0:64], kcf[0:NB, 128:256].rearrange("p (k d) -> p k d", k=2), [kcf], [vcaug])

          fw.barrier()
          p1.close()

          chk(3)
          p1b = ExitStack()
          win_sb = sb(p1b, "win_sb2", [128, 8, 3072], BF16)
          gT = sb(p1b, "gT2", [128, 8])
          xt = sb(p1b, "xt1b", [128, D]); xb = sb(p1b, "xb1b", [128, D], BF16); hT = sb(p1b, "hT1b", [128, 8, 128], BF16)
          junk = sb(p1b, "junk1b", [128, D], BF16); ss = sb(p1b, "ss1b", [128, 1])
          a1m = [sb(p1b, f"a1m{k}", [128, 3072], BF16) for k in range(2)]
          ptb = pst(p1b, "ptb1b", [128, 8, 128], BF16)
          pz = [pst(p1b, f"pz1b{i}", [128, 512]) for i in range(3)]
          for c in range(8):
              dma("pool", win_sb[:, c, :], w_in[l, c * 128:(c + 1) * 128, 2328:PW], W=[win_sb])
          dma_nc("sp", gT[:], w_attn_g[l].rearrange("(c p) -> p c", p=128), W=[gT])
          for i in range(NTT):
              dma("sp", xt[:], xsrc[i * 128:(i + 1) * 128, :], R=[xs_d[i]], W=[xt])
              norm_transpose(xt, gT, hT, ss)
              am = a1m[i % 2]
              for n in range(6):
                  p_ = pz[zi[0] % 3]; zi[0] += 1
                  for c in range(8):
                      mm(p_[:], hT[:, c, :], win_sb[:, c, n * 512:(n + 1) * 512], c == 0, c == 7, [hT, win_sb], [p_])
                  with relaxed():
                      act("act", am[:, n * 512:(n + 1) * 512], p_[:], AF.Sigmoid, scale=ss[:, 0:1], R=[p_, ss], W=[am])
              dma("sp", a1s[i][:, 1536:A1W], am[:], R=[am], W=[a1s_d2[i]])
          fw.barrier()
          p1b.close()

          chk(4)
          p1 = ExitStack()
          xb = sb(p1, "xbm", [128, D], BF16); hT = sb(p1, "hTm", [128, 8, 128], BF16)
          junk = sb(p1, "junkm", [128, D], BF16); ss = sb(p1, "ssm", [128, 1])
          sq = sb(p1, "sqm", [128, 512]); rs = sb(p1, "rsm", [128, 4])
          memx = sb(p1, "memx", [128, D]); wmem = sb(p1, "wmem", [128, 8, 1024], BF16); gmT = sb(p1, "gmT", [128, 8])
          mkvf = sb(p1, "mkvf", [128, 1024]); mkb = sb(p1, "mkb", [128, 512], BF16)
          ptb = pst(p1, "ptbm", [128, 8, 128], BF16)
          pz = [pst(p1, f"pzm{i}", [128, 512]) for i in range(3)]
          pk = pst(p1, "pkm", [128, 4, 128], BF16)
          for c in range(8):
              dma("pool", wmem[:, c, :], w_memkv[l, c * 128:(c + 1) * 128, :], W=[wmem])
          dma_nc("sp", gmT[:], w_memg[l].rearrange("(c p) -> p c", p=128), W=[gmT])
          for mt in range(2):
              dma("sp", memx[:], memin[mt * 128:(mt + 1) * 128, :], W=[memx])
              norm_transpose(memx, gmT, hT, ss)
              for n in range(2):
                  p_ = pz[zi[0] % 3]; zi[0] += 1
                  for c in range(8):
                      mm(p_[:], hT[:, c, :], wmem[:, c, n * 512:(n + 1) * 512], c == 0, c == 7, [hT, wmem], [p_])
                  act("act", mkvf[:, n * 512:(n + 1) * 512], p_[:], AF.Copy, scale=ss[:, 0:1], R=[p_, ss], W=[mkvf])
              tt("dve", sq[:, 0:512], mkvf[:, 0:512], mkvf[:, 0:512], ALU.mult, [mkvf], [sq])
              red(rs[:, 0:4], sq[:, 0:512].rearrange("p (h d) -> p h d", d=128), [sq], [rs])
              ts("dve", rs[:, 0:4], rs[:, 0:4], 1.0 / 128, EPS, ALU.mult, ALU.add, R=[rs], W=[rs])
              rsqrt_chain(rs)
              for h in range(4):
                  stt(mkvf[:, h * 128:(h + 1) * 128], mkvf[:, h * 128:(h + 1) * 128], rs[:, h:h + 1], gmk[:], ALU.mult, ALU.mult, [mkvf, rs, gmk], [mkvf])
              outdma("sp", o_memkv[l, mt * 128:(mt + 1) * 128, :], mkvf[:], R=[mkvf])
              cp("act", mkb[:], mkvf[:, 0:512], [mkvf], [mkb])
              for h in range(4):
                  tr(pk[:, h, :], mkb[:, h * 128:(h + 1) * 128], identb, [mkb, cb], [pk])
              cp("act", mkT[:, :, mt * 128:(mt + 1) * 128], pk[:], [pk], [mkT])
              cp("dve", mva[:, mt, :, 0:128], mkvf[:, 512:1024].rearrange("p (h d) -> p h d", h=4), [mkvf], [mva])
          fw.barrier()
          p1.close()

          chk(5)
          for mode in ("prompt", "sample"):
              PR = (mode == "prompt")
              SM = not PR
              if SM:
                  lay.close()
              p2 = ExitStack()

              def mk(cond, name, shape, dt=F32):
                  return sb(p2, name + mode[0], shape, dt) if cond else None

              wup = sb(p2, "wup" + mode[0], [128, 3, 4, D], BF16)
              wo = sb(p2, "wo" + mode[0], [128, 8, D], BF16)
              a1 = sb(p2, "a1b" + mode[0], [128, A1W], BF16)
              xt = sb(p2, "xt2" + mode[0], [128, D])
              den = mk(True, "den", [128, 8]); rg = mk(True, "rg", [128, 8])
              onsa = mk(True, "onsa", [128, 8, 64]); otmp = mk(True, "otmp", [128, 8, 64]); onsab = mk(True, "onsab", [128, 512], BF16)
              brT = mk(True, "brT", [128, 2, 4, 128], BF16)
              memo = mk(True, "memo", [128, 4, 128], BF16)
              hsum = mk(True, "hsum", [128, D]); htmp = mk(True, "htmp", [128, 512]); hb = mk(True, "hb", [128, D], BF16)
              hT2 = mk(True, "hT2", [128, 8, 128], BF16)
              qtp = mk(PR, "qtp", [128, 8, 128], BF16)
              ec = mk(PR, "ec", [128, 8, 128]); pc = mk(PR, "pc", [128, 8, 128]); pcb = mk(PR, "pcb", [128, 8, 128], BF16)
              pcT = mk(PR, "pcT", [128, 8, 128], BF16)
              imp = mk(PR, "imp", [128, 2, 128]); sc = mk(PR, "sc", [128, 2, 64]); sc2 = mk(PR, "sc2", [128, 2, 64])
              m8 = mk(PR, "m8", [128, 2, 16]); selm = mk(PR, "selm", [128, 2, 64])
              qa = mk(PR, "qa", [128, 8, 128], BF16); qta = mk(PR, "qta", [128, 8, 128], BF16)
              pT = [mk(PR, f"pT{i}", [128, 512], BF16) for i in range(3)]
              mqT = mk(PR, "mqT", [128, 4, 128], BF16)
              pmT = mk(PR, "pmT", [128, 8, 128], BF16)
              pg = [mk(SM, f"pg{i}", [128, 512]) for i in range(3)]
              qbf = mk(SM, "qbf", [128, 512]); mqbf = mk(SM, "mqbf", [128, 512])
              prod = [mk(SM, "prod0", [128, 1024]), mk(SM, "prod1", [128, 512])]
              ssel = mk(SM, "ssel", [128, NPG, 8]); esel = mk(SM, "esel", [128, NPG, 8]); emb = mk(SM, "emb", [128, NPG, 8], BF16)
              vsall = mk(SM, "vsall", [128, NPG + 1, 2, 65], BF16)
              pooleds = mk(SM, "pooleds", [128, 2, NBS])
              kcs = mk(SM, "kcs", [128, 256]); kcsb = mk(SM, "kcsb", [128, 128], BF16); kcTs = mk(SM, "kcTs", [128, NBS], BF16)
              vcs = mk(SM, "vcs", [128, NCK, 2, 65], BF16)
              qbt = mk(SM, "qbt", [128, 8, 128], BF16); qbm = mk(SM, "qbm", [128, 8, 128], BF16)
              pe8 = mk(SM, "pe8", [8, NBS]); pn8 = mk(SM, "pn8", [8, NBS]); d8 = mk(SM, "d8", [8, 4])
              scs = mk(SM, "scs", [2, NBS // 2]); scs2 = mk(SM, "scs2", [2, NBS // 2]); sels = mk(SM, "sels", [2, NBS // 2]); m8s = mk(SM, "m8s", [2, 16])
              pTs = mk(SM, "pTs", [128, NCK, 8], BF16)
              pnew = mk(SM, "pnew", [128, 2, 8], BF16); snew = mk(SM, "snew", [128, 2, 8])
              vnew = mk(SM, "vnew", [128, 2, 2, 65], BF16)
              wc = mk(SM, "wc", [128, 4, 256]); sw = mk(SM, "sw", [128, 4, 8]); pw = mk(SM, "pw", [128, 4, 8], BF16)
              vwa = mk(SM, "vwa", [128, 4, 2, 65], BF16)
              mc = mk(SM, "mc", [128, 2, 1024]); smm = mk(SM, "smm", [128, 8]); pmm = mk(SM, "pmm", [128, 2, 4], BF16)
              vmb = mk(SM, "vmb", [128, 2, 512], BF16)
              oall = mk(SM, "oall", [8, 3, NSEQ, 64]); o8t = mk(SM, "o8t", [8, 2, 65]); o8 = mk(SM, "o8", [8, 65])
              mall = mk(SM, "mall", [4, NSEQ, 128]); m4t = mk(SM, "m4t", [4, 4, 128]); m4 = mk(SM, "m4", [4, 128])
              od = mk(SM, "od", [8, 8, 64]); odm = mk(SM, "odm", [4, 4, 128])
              ob = mk(SM, "ob", [128, 3, 512])
              d8_rs_t = mk(SM, "d8rs", [128, 2])
              d8_rs = d8_rs_t.h if SM else None
              b0 = pst(p2, "b0" + mode[0], [128, 8, 128], BF16)
              b12 = pst(p2, "b12" + mode[0], [128, 8, 128])
              b34 = [pst(p2, f"b3{i}" + mode[0], [128, 512]) for i in range(2)]
              acc = pst(p2, "acc" + mode[0], [128, 2, 512])
              b7 = pst(p2, "b7" + mode[0], [128, 8, 128], BF16)

              for bi, src in enumerate((w_upn, w_upp, w_upm)):
                  dma("pool", wup[:, bi, :, :], src[l].rearrange("(c p) n -> p c n", p=128), W=[wup])
              dma("pool", wo[:], w_out[l].rearrange("(c p) n -> p c n", p=128), W=[wo])
              if PR:
                  memset("pool", sc[:], 0.0, [sc])
              else:
                  memset("pool", vsall[:, :, :, 64:65], 1.0, [vsall])
                  memset("pool", vnew[:, :, :, 64:65], 1.0, [vnew])
                  memset("pool", vwa[:, :, :, 64:65], 1.0, [vwa])
                  memset("pool", vcs[:, :, :, 64:65], 1.0, [vcs])
                  memset("pool", qbm[:], 0.0, [qbm])

              acc4 = acc[:, :, 0:260].rearrange("p k (g e) -> p k g e", e=65)

              def branch_out(br, i, first):
                  ts("dve", den[:].rearrange("p (k g) -> p k g", k=2), acc4[:, :, :, 64], 1e-30, None, ALU.max, R=[acc], W=[den])
                  if br == 2 and i < 4:
                      ts("dve", den[:], den[:], npad[:, i:i + 1], None, ALU.add, R=[den, cf], W=[den])
                  recip(den[:], den[:], [den], [den])
                  gv = nsag[:, i, :].rearrange("p (h b) -> p h b", b=3)[:, :, br]
                  tt("dve", rg[:], den[:], gv, ALU.mult, [den, nsag], [rg])
                  if "csw"[br] in os.environ.get("KOFF", ""):
                      ts("dve", rg[:], rg[:], 0.0, None, ALU.mult, R=[rg], W=[rg])
                  rgb = rg[:].rearrange("p (k g) -> p k g", k=2).unsqueeze(3).to_broadcast([128, 2, 4, 64])
                  dst = onsa if first else otmp
                  tt("dve", dst[:].rearrange("p (k g) d -> p k g d", k=2), acc4[:, :, :, 0:64], rgb, ALU.mult, [acc, rg], [dst])
                  if not first:
                      tt("dve", onsa[:], onsa[:], otmp[:], ALU.add, [onsa, otmp], [onsa])

              pti_ = [0]

              def prompt_attention(i):
                  for h in range(8):
                      tr(b0[0:64, h, :], a1[:, h * 64:(h + 1) * 64], identb, [a1, cb], [b0])
                  cp("act", qtp[0:64, :, :], b0[0:64, :, :], [b0], [qtp])
                  for h in range(8):
                      mm(b12[:, h, :], qtp[0:64, h, :], kcT2[0:64, h // 4, :], True, True, [qtp, kcT2], [b12])
                  act("act", ec[:], b12[:], AF.Exp, R=[b12], W=[ec])
                  tm = tcmp[:, 124 - 4 * i:124 - 4 * i + 128]
                  for h in range(8):
                      with relaxed():
                          ttr(pc[:, h, :], ec[:, h, :], tm, den[:, h:h + 1], [ec, cf], [pc, den])
                  ts("dve", den[:], den[:], 1e-30, None, ALU.max, R=[den], W=[den])
                  recip(den[:], den[:], [den], [den])
                  for kv in range(2):
                      for g in range(4):
                          h = kv * 4 + g
                          if g == 0:
                              ts("dve", imp[:, kv, :], pc[:, h, :], den[:, h:h + 1], None, ALU.mult, R=[pc, den], W=[imp])
                          else:
                              stt(imp[:, kv, :], pc[:, h, :], den[:, h:h + 1], imp[:, kv, :], ALU.mult, ALU.add, [pc, den, imp], [imp])
                  iv = imp[:, :, 0:NB].rearrange("p k (b r) -> p k b r", r=2)
                  tt("dve", sc[:, :, 0:NB // 2], iv[:, :, :, 0], iv[:, :, :, 1], ALU.add, [imp], [sc])
                  bs = bsel[:, 62 - 2 * i:126 - 2 * i]
                  tt("dve", sc2[:], sc[:], bs.unsqueeze(1).to_broadcast([128, 2, 64]), ALU.add, [sc, cf], [sc2])
                  memset("dve", sc2[:, :, 0:1], 1e9, [sc2], st=True)
                  for kv in range(2):
                      vmax(m8[:, kv, 0:8], sc2[:, kv, :], [sc2], [m8])
                      mrep(selm[:, kv, :], m8[:, kv, 0:8], sc2[:, kv, :], -2e9, [sc2, m8], [selm])
                      vmax(m8[:, kv, 8:16], selm[:, kv, :], [selm], [m8])
                      ts("dve", selm[:, kv, :], sc2[:, kv, :], m8[:, kv, 15:16], None, ALU.is_ge, R=[sc2, m8], W=[selm])
                      ts("dve", qa[:, 4 * kv:4 * kv + 4, 64:128], selm[:, kv, :].unsqueeze(1).to_broadcast([128, 4, 64]), 1.0, BIGM, ALU.subtract, ALU.mult, R=[selm], W=[qa])
                  cp("act", qa[:, :, 0:64], a1[:, 0:512].rearrange("p (h d) -> p h d", h=8), [a1], [qa])
                  for h in range(8):
                      tr(b0[:, h, :], qa[:, h, :], identb, [qa, cb], [b0])
                  cp("act", qta[:], b0[:], [b0], [qta])
                  cp("act", pcb[:], pc[:], [pc], [pcb])
                  for h in range(8):
                      tr(b7[:, h, :], pcb[:, h, :], identb, [pcb, cb], [b7])
                  cp("dve", pcT[:], b7[:], [b7], [pcT])
                  for h in range(8):
                      mm(acc[:, h // 4, (h % 4) * 65:(h % 4) * 65 + 65], pcT[:, h, :], vcaug[:, h // 4, :], True, True, [pcT, vcaug], [acc])
                  branch_out(0, i, True)
                  if i == 0:
                      chk(5.1)
                  for br in (1, 2):
                      jlo = 0 if br == 1 else max(0, i - 4)
                      for kv in range(2):
                          mm(acc[:, kv, 0:260], zerob[:, 0:128], zerob[:, 0:260], True, False, [zerob], [acc])
                          for j in range(jlo, i + 1):
                              ps_ = b34[pti_[0] % 2]
                              pt_ = pT[pti_[0] % 3]
                              pti_[0] += 1
                              if br == 1:
                                  mm(ps_[:], kselT[:, kv, j * 128:(j + 1) * 128], qta[:, 4 * kv:4 * kv + 4, :], True, True, [kselT.ds[j], kselT, qta], [ps_])
                              else:
                                  mm(ps_[:], kwinT[0:64, kv, j * 128:(j + 1) * 128], qta[0:64, 4 * kv:4 * kv + 4, :], True, True, [kwinT.ds[j], qta], [ps_])
                              act("act", pt_[:], ps_[:], AF.Exp, R=[ps_], W=[pt_])
                              if j == i or (br == 2 and j == i - 4):
                                  msk = tric_b if j == i else trib_b
                                  tt("dve", pt_[:].rearrange("p (g t) -> p g t", g=4), pt_[:].rearrange("p (g t) -> p g t", g=4),
                                     msk.unsqueeze(1).to_broadcast([128, 4, 128]), ALU.mult, [pt_, cb], [pt_])
                              vv = vsel if br == 1 else vwin
                              for g in range(4):
                                  mm(acc[:, kv, g * 65:g * 65 + 65], pt_[:, g * 128:(g + 1) * 128], vv[:, j, kv, :], False, (j == i and g == 3), [pt_, vv.ds[j], vv], [acc])
                      branch_out(br, i, False)
                      if i == 0:
                          chk(5.2 + 0.01 * br)
                  for h in range(4):
                      tr(b0[:, h, :], a1[:, 512 + h * 128:640 + h * 128], identb, [a1, cb], [b0])
                  cp("act", mqT[:], b0[:, 0:4, :], [b0], [mqT])
                  for h in range(4):
                      for mt in range(2):
                          mm(b12[:, h * 2 + mt, :], mkT[:, h, mt * 128:(mt + 1) * 128], mqT[:, h, :], True, True, [mkT, mqT], [b12])
                  act("act", pmT[:], b12[:], AF.Exp, R=[b12], W=[pmT])
                  for h in range(4):
                      for mt in range(2):
                          mm(acc[:, h // 2, (h % 2) * 256:(h % 2) * 256 + 129], pmT[:, h * 2 + mt, :], mva[:, mt, h, :], mt == 0, mt == 1, [pmT, mva], [acc])
                  accm = acc[:, :, :].rearrange("p k (g e) -> p (k g) e", e=256)
                  cp("dve", den[:, 0:4], accm[:, :, 128], [acc], [den])
                  recip(den[:, 0:4], den[:, 0:4], [den], [den])
                  tt("dve", memo[:], accm[:, :, 0:128], den[:, 0:4].unsqueeze(2).to_broadcast([128, 4, 128]), ALU.mult, [acc, den], [memo])
                  if i == 0:
                      chk(5.3)

              def sample_attention():
                  i = NT
                  for kv in range(2):
                      cp("pool", qbm[:, 4 * kv:4 * kv + 4, 64 * kv:64 * kv + 64], a1[:, 256 * kv:256 * kv + 256].rearrange("p (g d) -> p g d", g=4), [a1], [qbm])
                  for h in range(8):
                      tr(b0[:, h, :], qbm[:, h, :], identb, [qbm, cb], [b0])
                  cp("act", qbt[:], b0[:], [b0], [qbt])
                  cp("act", vnew[:, 0, :, 0:64], sampkv[:, 128:256].rearrange("p (k d) -> p k d", k=2), [sampkv], [vnew])
                  cp("act", vnew[:, 1, :, 0:64], sampkv[:, 384:512].rearrange("p (k d) -> p k d", k=2), [sampkv], [vnew])
                  pcs = b12
                  pcs_v = b12[:].rearrange("p a b -> p (a b)")[:, 0:2 * NBS].rearrange("p (a b) -> p a b", a=2)
                  for s_ in range(NSEQ):
                      mm(b34[0][:], selall_b[:, s_, :], a1[:, 0:512], True, True, [cb, a1], [b34[0]])
                      cp("act", qbf[:], b34[0][:], [b34[0]], [qbf])
                      mm(b34[1][:], selall_b[:, s_, :], a1[:, 512:1024], True, True, [cb, a1], [b34[1]])
                      cp("act", mqbf[:], b34[1][:], [b34[1]], [mqbf])
                      qv = qbf[:].rearrange("p (k g d) -> p k g d", k=2, g=4)

                      def dots(dst, ksrc, pi):
                          pr = prod[pi % 2]
                          e = "dve" if pi % 2 == 0 else "pool"
                          tt(e, pr[:, 0:512].rearrange("p (k g d) -> p k g d", k=2, g=4),
                             ksrc.rearrange("p (k d) -> p k d", k=2).unsqueeze(2).to_broadcast([128, 2, 4, 64]), qv, ALU.mult, [qbf] + dots_R, [pr])
                          red(dst, pr[:, 0:512].rearrange("p (h d) -> p h d", d=64), [pr], dots_W)

                      for j in range(NPG):
                          pgt = pg[j % 3]
                          col = s_ * NPG + j
                          fw.dma("pool", lambda E, o_=pgt[:], i_=cnsa_flat, x_=pidx[:, l, col:col + 1]: E.indirect_dma_start(
                              out=o_, out_offset=None, in_=i_,
                              in_offset=bass.IndirectOffsetOnAxis(ap=x_, axis=0)), [pidx.d], [pgt.d])
                          mm(pcs_v[:, 0, 4 * j:4 * j + 4], pgt[:, 0:128], pmf, True, True, [pgt, cf], [b12])
                          mm(pcs_v[:, 1, 4 * j:4 * j + 4], pgt[:, 128:256], pmf, True, True, [pgt, cf], [b12])
                          dots_R = [pgt]; dots_W = [ssel]
                          dots(ssel[:, j, :], pgt[:, 256:384], j)
                          cp("act", vsall[:, j, :, 0:64], pgt[:, 384:512].rearrange("p (k d) -> p k d", k=2), [pgt], [vsall])
                      act("act", esel[:], ssel[:], AF.Exp, R=[ssel], W=[esel])
                      for a_ in range(2):
                          ts("dve", pooleds[:, a_, :], pcs_v[:, a_, :], pemT[:, a_:a_ + 1], None, ALU.add, R=[b12, pemT], W=[pooleds])
                      for ck in range(NCK):
                          mm(b34[0][0:CB, 0:128], pooleds[:, 0, ck * CB:(ck + 1) * CB], wkbd[:, 0, :], True, True, [pooleds, wkbd], [b34[0]])
                          mm(b34[0][0:CB, 128:256], pooleds[:, 1, ck * CB:(ck + 1) * CB], wkbd[:, 1, :], True, True, [pooleds, wkbd], [b34[0]])
                          cp("act", kcs[0:CB, :], b34[0][0:CB, 0:256], [b34[0]], [kcs])
                          tt("dve", prod[0][0:CB, 0:128], kcs[0:CB, 0:128], kcs[0:CB, 0:128], ALU.mult, [kcs], [prod[0]])
                          red(d8_rs[0:CB, 0:2], prod[0][0:CB, 0:128].rearrange("p (h d) -> p h d", d=64), [prod[0]], [d8_rs_t])
                          ts("dve", d8_rs[0:CB, 0:2], d8_rs[0:CB, 0:2], 1.0 / 64, EPS, ALU.mult, ALU.add, R=[d8_rs_t], W=[d8_rs_t])
                          act("act", d8_rs[0:CB, 0:2], d8_rs[0:CB, 0:2], AF.Sqrt, R=[d8_rs_t], W=[d8_rs_t])
                          recip(d8_rs[0:CB, 0:2], d8_rs[0:CB, 0:2], [d8_rs_t], [d8_rs_t])
                          for kv in range(2):
                              stt(kcsb[0:CB, kv * 64:(kv + 1) * 64], kcs[0:CB, kv * 64:(kv + 1) * 64], d8_rs[0:CB, kv:kv + 1], gkc[0:CB, :], ALU.mult, ALU.mult, [kcs, d8_rs_t, gkc], [kcsb])
                          tr(b7[:, 0, 0:CB], kcsb[0:CB, :], identb[0:CB, 0:CB], [kcsb, cb], [b7])
                          cp("act", kcTs[:, ck * CB:(ck + 1) * CB], b7[:, 0, 0:CB], [b7], [kcTs])
                          cp("dve", vcs[0:CB, ck, :, 0:64], kcs[0:CB, 128:256].rearrange("p (k d) -> p k d", k=2), [kcs], [vcs])
                      mm(b34[1][0:8, 0:NBS], qbt[:, :, s_], kcTs[:], True, True, [qbt, kcTs], [b34[1]])
                      act("act", pe8[:], b34[1][0:8, 0:NBS], AF.Exp, accum=d8[:, 0:1], R=[b34[1]], W=[pe8, d8])
                      recip(d8[:, 1:2], d8[:, 0:1], [d8], [d8])
                      ts("dve", pn8[:], pe8[:], d8[:, 1:2], None, ALU.mult, R=[pe8, d8], W=[pn8])
                      mm(b34[0][0:2, 0:NBS], g8[0:8, :], pn8[:], True, True, [cf, pn8], [b34[0]])
                      iv = b34[0][0:2, 0:NBS].rearrange("p (b r) -> p b r", r=2)
                      cp("act", scs2[:], iv[:, :, 0], [b34[0]], [scs2])
                      tt("dve", scs[:], scs2[:], iv[:, :, 1], ALU.add, [scs2, b34[0]], [scs])
                      memset("dve", scs[:, 0:1], 1e9, [scs])
                      memset("dve", scs[:, NBS // 2 - 1:NBS // 2], 1e9, [scs])
                      vmax(m8s[:, 0:8], scs[:], [scs], [m8s])
                      mrep(scs2[:], m8s[:, 0:8], scs[:], -2e9, [scs, m8s], [scs2])
                      vmax(m8s[:, 8:16], scs2[:], [scs2], [m8s])
                      ts("dve", sels[:], scs[:], m8s[:, 14:15], None, ALU.is_ge, R=[scs, m8s], W=[sels])
                      for ck in range(NCK):
                          tr(b34[1][0:CB, 256 + ck * 8:264 + ck * 8], pn8[:, ck * CB:(ck + 1) * CB], identf[0:8, 0:8], [pn8, cf], [b34[1]])
                          cp("act", pTs[0:CB, ck, :], b34[1][0:CB, 256 + ck * 8:264 + ck * 8], [b34[1]], [pTs])
                      for ck in range(NCK):
                          mm(acc[0:8, 0, 0:130], pTs[0:CB, ck, :], vcs[0:CB, ck, :, :].rearrange("p k e -> p (k e)"), ck == 0, ck == NCK - 1, [pTs, vcs], [acc])
                      pick8(0, s_, False)
                      sv = sels[:].rearrange("p (j r) -> p j r", r=2)
                      for kv in range(2):
                          mm(b34[1][:, kv * NPG:(kv + 1) * NPG], inde[0:2, kv, 0, :], sv[:, :, 0], True, False, [cf, sels], [b34[1]])
                          mm(b34[1][:, kv * NPG:(kv + 1) * NPG], inde[0:2, kv, 1, :], sv[:, :, 1], False, True, [cf, sels], [b34[1]])
                      mv = b34[1][:, 0:2 * NPG].rearrange("p (k j) -> p j k", k=2).unsqueeze(3).to_broadcast([128, NPG, 2, 4])
                      tt("dve", emb[:].rearrange("p j (k g) -> p j k g", k=2), esel[:].rearrange("p j (k g) -> p j k g", k=2), mv, ALU.mult, [esel, b34[1]], [emb])
                      dots_R = [sampkv]; dots_W = [snew]
                      dots(snew[:, 0, :], sampkv[:, 0:128], 0)
                      dots(snew[:, 1, :], sampkv[:, 256:384], 1)
                      act("act", snew[:], snew[:], AF.Exp, R=[snew], W=[snew])
                      ts("dve", pnew[:], snew[:], rowsel[:, s_:s_ + 1], None, ALU.mult, R=[snew, cf], W=[pnew])
                      for j in range(NPG):
                          mm(acc[0:8, 0, 0:130], emb[:, j, :], vsall[:, j, :, :].rearrange("p k e -> p (k e)"), j == 0, False, [emb, vsall], [acc])
                      mm(acc[0:8, 0, 0:130], pnew[:, 0, :], vnew[:, 0, :, :].rearrange("p k e -> p (k e)"), False, True, [pnew, vnew], [acc])
                      pick8(1, s_, True)
                      dma("sp", wc[:], cwin[l, s_].rearrange("(kt p) c -> p kt c", p=128), W=[wc])
                      for kt in range(4):
                          dots_R = [wc]; dots_W = [sw]
                          dots(sw[:, kt, :], wc[:, kt, 0:128], kt)
                      act("act", sw[:], sw[:], AF.Exp, R=[sw], W=[sw])
                      ts("dve", pw[:, 0, :], sw[:, 0, :], wmask[:, 0:1], None, ALU.mult, R=[sw, cf], W=[pw])
                      cp("dve", pw[:, 1:4, :], sw[:, 1:4, :], [sw], [pw])
                      cp("act", vwa[:, :, :, 0:64], wc[:, :, 128:256].rearrange("p t (k d) -> p t k d", k=2), [wc], [vwa])
                      for kt in range(4):
                          mm(acc[0:8, 0, 0:130], pw[:, kt, :], vwa[:, kt, :, :].rearrange("p k e -> p (k e)"), kt == 0, False, [pw, vwa], [acc])
                      mm(acc[0:8, 0, 0:130], pnew[:, 1, :], vnew[:, 1, :, :].rearrange("p k e -> p (k e)"), False, True, [pnew, vnew], [acc])
                      pick8(2, s_, True)
                      dma("sp", mc[:], cmem[l, s_].rearrange("(mt p) c -> p mt c", p=128), W=[mc])
                      tt("dve", prod[0][:].rearrange("p (t c) -> p t c", t=2), mc[:, :, 0:512], mqbf[:].unsqueeze(1).to_broadcast([128, 2, 512]), ALU.mult, [mc, mqbf], [prod[0]])
                      red(smm[:], prod[0][:].rearrange("p (h d) -> p h d", d=128), [prod[0]], [smm])
                      act("act", pmm[:].rearrange("p t h -> p (t h)"), smm[:], AF.Exp, R=[smm], W=[pmm])
                      cp("pool", vmb[:], mc[:, :, 512:1024], [mc], [vmb])
                      for mt in range(2):
                          mm(acc[0:4, 1, :], pmm[:, mt, :], vmb[:, mt, :], mt == 0, mt == 1, [pmm, vmb], [acc])
                      for mt in range(2):
                          mm(b34[0][0:4, 0:1], pmm[:, mt, :], onesb[:, 0:1], mt == 0, mt == 1, [pmm, cb], [b34[0]])
                      tt("dve", m4t[:], acc[0:4, 1, :].rearrange("p (a d) -> p a d", a=4), identf[0:4, 0:4].unsqueeze(2).to_broadcast([4, 4, 128]), ALU.mult, [acc, cf], [m4t])
                      red(m4[:], m4t[:].rearrange("p a d -> p d a"), [m4t], [m4])
                      recip(d8[0:4, 2:3], b34[0][0:4, 0:1], [b34[0].d], [d8.d])
                      ts("dve", mall[:, s_, :], m4[:], d8[0:4, 2:3], None, ALU.mult, R=[m4, d8], W=[mall])
                  for br in range(3):
                      for s_ in range(NSEQ):
                          tt("dve", od[:], oall[:, br, s_, :].unsqueeze(1).to_broadcast([8, 8, 64]), identf[0:8, 0:8].unsqueeze(2).to_broadcast([8, 8, 64]), ALU.mult, [oall, cf], [od])
                          mm(b12[:, 0:4, :].rearrange("p a b -> p (a b)"), sr_f[0:8, s_, :], od[:].rearrange("p a b -> p (a b)"), True, True, [cf, od], [b12])
                          if s_ == 0:
                              cp("act", ob[:, br, :], b12[:, 0:4, :].rearrange("p a b -> p (a b)"), [b12], [ob])
                          else:
                              tt("dve", ob[:, br, :], ob[:, br, :], b12[:, 0:4, :].rearrange("p a b -> p (a b)"), ALU.add, [ob, b12], [ob])
                  for br in range(3):
                      gv = nsag[:, i, :].rearrange("p (h b) -> p h b", b=3)[:, :, br].unsqueeze(2).to_broadcast([128, 8, 64])
                      dst = onsa if br == 0 else otmp
                      tt("dve", dst[:], ob[:, br, :].rearrange("p (h d) -> p h d", h=8), gv, ALU.mult, [ob, nsag], [dst])
                      if br > 0:
                          tt("dve", onsa[:], onsa[:], otmp[:], ALU.add, [onsa, otmp], [onsa])
                  for s_ in range(NSEQ):
                      tt("dve", odm[:], mall[:, s_, :].unsqueeze(1).to_broadcast([4, 4, 128]), identf[0:4, 0:4].unsqueeze(2).to_broadcast([4, 4, 128]), ALU.mult, [mall, cf], [odm])
                      mm(b12[:, 4:8, :].rearrange("p a b -> p (a b)"), sr_f[0:4, s_, :], odm[:].rearrange("p a b -> p (a b)"), True, True, [cf, odm], [b12])
                      if s_ == 0:
                          cp("act", hsum[:, 0:512], b12[:, 4:8, :].rearrange("p a b -> p (a b)"), [b12], [hsum])
                      else:
                          tt("dve", hsum[:, 0:512], hsum[:, 0:512], b12[:, 4:8, :].rearrange("p a b -> p (a b)"), ALU.add, [hsum, b12], [hsum])
                  cp("dve", memo[:].rearrange("p h d -> p (h d)"), hsum[:, 0:512], [hsum], [memo])

              def pick8(br, s_, normalize):
                  tt("dve", o8t[:], acc[0:8, 0, 0:130].rearrange("p (k e) -> p k e", k=2), g8[0:8, :].unsqueeze(2).to_broadcast([8, 2, 65]), ALU.mult, [acc, cf], [o8t])
                  tt("dve", o8[:], o8t[:, 0, :], o8t[:, 1, :], ALU.add, [o8t], [o8])
                  if normalize:
                      recip(d8[:, 3:4], o8[:, 64:65], [o8.d], [d8.d])
                      ts("dve", oall[:, br, s_, :], o8[:, 0:64], d8[:, 3:4], None, ALU.mult, R=[o8, d8], W=[oall])
                  else:
                      cp("dve", oall[:, br, s_, :], o8[:, 0:64], [o8], [oall])

              for i in (range(NT) if PR else [NT]):
                  dma("sp", a1[:], a1s[i], R=[a1s_d[i], a1s_d2[i]], W=[a1])
                  dma("sp", xt[:], xsrc[i * 128:(i + 1) * 128, :], R=[xs_d[i]], W=[xt])
                  if i < NT:
                      prompt_attention(i)
                  else:
                      sample_attention()
                  cp("act", onsab[:], onsa[:].rearrange("p h d -> p (h d)"), [onsa], [onsab])
                  for c in range(4):
                      tr(b0[:, c, :], onsab[:, c * 128:(c + 1) * 128], identb, [onsab, cb], [b0])
                      tr(b0[:, 4 + c, :], memo[:, c, :], identb, [memo, cb], [b0])
                  cp("act", brT[:].rearrange("p a c t -> p (a c) t"), b0[:], [b0], [brT])
                  pyT = a1[:, 1024:1536].rearrange("p (g t) -> p g t", g=4)
                  for n in range(2):
                      for bi in range(3):
                          p_ = b34[bi % 2]
                          for c in range(4):
                              lt = brT[:, 0, c, :] if bi == 0 else (pyT[:, c, :] if bi == 1 else brT[:, 1, c, :])
                              mm(p_[:], lt, wup[:, bi, c, n * 512:(n + 1) * 512], c == 0, c == 3, [brT, a1, wup], [p_])
                          mgv = a1[:, 1536 + bi * 1024 + n * 512:1536 + bi * 1024 + (n + 1) * 512]
                          if bi == 0:
                              tt("dve", hsum[:, n * 512:(n + 1) * 512], p_[:], mgv, ALU.mult, [p_, a1], [hsum])
                          else:
                              tt("dve", htmp[:], p_[:], mgv, ALU.mult, [p_, a1], [htmp])
                              if "xpm"[bi] in os.environ.get("KOFF", ""):
                                  ts("dve", htmp[:], htmp[:], 0.0, None, ALU.mult, R=[htmp], W=[htmp])
                              if bi == 1:
                                  tt("dve", hsum[:, n * 512:(n + 1) * 512], hsum[:, n * 512:(n + 1) * 512], htmp[:], ALU.add, [hsum, htmp], [hsum])
                              else:
                                  tt("dve", hb[:, n * 512:(n + 1) * 512], hsum[:, n * 512:(n + 1) * 512], htmp[:], ALU.add, [hsum, htmp], [hb])
                  for c in range(8):
                      tr(b7[:, c, :], hb[:, c * 128:(c + 1) * 128], identb, [hb, cb], [b7])
                  cp("act", hT2[:], b7[:], [b7], [hT2])
                  for n in range(2):
                      p_ = b34[n]
                      for c in range(8):
                          mm(p_[:], hT2[:, c, :], wo[:, c, n * 512:(n + 1) * 512], c == 0, c == 7, [hT2, wo], [p_])
                      tt("dve", xt[:, n * 512:(n + 1) * 512], xt[:, n * 512:(n + 1) * 512], p_[:], ALU.add, [xt, p_], [xt])
                  dma("sp", xs[i * 128:(i + 1) * 128, :], xt[:], R=[xt], W=[xs_d[i]])
                  chk(5.4 + 0.01 * i)
              fw.barrier()
              p2.close()
          lays.close()

          chk(8)
          p3 = ExitStack()
          wgu = sb(p3, "wgu", [128, 8, 2 * DFF], BF16)
          wdn = sb(p3, "wdn", [128, 22, D], BF16)
          gfT = sb(p3, "gfT", [128, 8])
          x4 = [sb(p3, f"x4_{k}", [128, D]) for k in range(2)]
          xb3 = sb(p3, "xb3", [128, D], BF16); junk3 = sb(p3, "junk3", [128, D], BF16); ss3 = sb(p3, "ss3", [128, 1])
          hT3 = sb(p3, "hT3", [128, 8, 256], BF16)
          aT = sb(p3, "aT", [128, 22, 256], BF16)
          sg = [sb(p3, f"sg{k}", [128, 256]) for k in range(2)]
          ptb3 = pst(p3, "ptb3", [128, 8, 128], BF16)
          pgk = [pst(p3, f"pgk{k}", [128, 512]) for k in range(2)]
          puk = [pst(p3, f"puk{k}", [128, 512]) for k in range(2)]
          pdn = [pst(p3, f"pdn{k}", [128, 512]) for k in range(2)]
          for c in range(8):
              dma("pool", wgu[:, c, :], w_gu[l, c * 128:(c + 1) * 128, :], W=[wgu])
          dma("pool", wdn[:], w_dn[l].rearrange("(c p) n -> p c n", p=128), W=[wdn])
          dma_nc("sp", gfT[:], w_ffng[l].rearrange("(c p) -> p c", p=128), W=[gfT])
          st = 0
          kk = 0
          while st < NTT:
              nt_ = min(2, NT - st) if st < NT else 1
              TW = nt_ * 128
              for k in range(nt_):
                  i = st + k
                  dma("sp", x4[k][:], xs[i * 128:(i + 1) * 128, :], R=[xs_d[i]], W=[x4[k]])
                  act("act", junk3[:], x4[k][:], AF.Square, accum=ss3[:], R=[x4[k]], W=[junk3, ss3])
                  ts("dve", ss3[:], ss3[:], 1.0 / D, EPS, ALU.mult, ALU.add, R=[ss3], W=[ss3])
                  rsqrt_chain(ss3)
                  act("act", xb3[:], x4[k][:], AF.Copy, scale=ss3[:, 0:1], R=[x4[k], ss3], W=[xb3])
                  for c in range(8):
                      tr(ptb3[:, c, :], xb3[:, c * 128:(c + 1) * 128], identb, [xb3, cb], [ptb3])
                  for c in range(8):
                      with relaxed():
                          ts("dve", hT3[:, c, k * 128:(k + 1) * 128], ptb3[:, c, :], gfT[:, c:c + 1], None, ALU.mult, R=[ptb3, gfT], W=[hT3])
              for fc in range(22):
                  pg_, pu_, sg_ = pgk[kk % 2], puk[kk % 2], sg[kk % 2]
                  kk += 1
                  for c in range(8):
                      mm(pg_[:, 0:TW], wgu[:, c, fc * 128:(fc + 1) * 128], hT3[:, c, 0:TW], c == 0, c == 7, [wgu, hT3], [pg_])
                  for c in range(8):
                      mm(pu_[:, 0:TW], wgu[:, c, DFF + fc * 128:DFF + (fc + 1) * 128], hT3[:, c, 0:TW], c == 0, c == 7, [wgu, hT3], [pu_])
                  act("act", sg_[:, 0:TW], pg_[:, 0:TW], AF.Silu, R=[pg_], W=[sg_])
                  with relaxed():
                      tt("dve", aT[:, fc, 0:TW], pu_[:, 0:TW], sg_[:, 0:TW], ALU.mult, [pu_, sg_], [aT])
              for k in range(nt_):
                  i = st + k
                  for n in range(2):
                      p_ = pdn[n]
                      for fc in range(22):
                          mm(p_[:], aT[:, fc, k * 128:(k + 1) * 128], wdn[:, fc, n * 512:(n + 1) * 512], fc == 0, fc == 21, [aT, wdn], [p_])
                      tt("dve", x4[k][:, n * 512:(n + 1) * 512], x4[k][:, n * 512:(n + 1) * 512], p_[:], ALU.add, [x4[k], p_], [x4[k]])
                  if l == NL - 1:
                      outdma("sp", o_y[i * 128:(i + 1) * 128, :], x4[k][:], R=[x4[k]])
                  else:
                      dma("sp", xs[i * 128:(i + 1) * 128, :], x4[k][:], R=[x4[k]], W=[xs_d[i]])
              st += nt_
          fw.barrier()
          p3.close()

    except _Stop:
        pass
    print("OPCOUNT", _opc[0])
    fw.finish_deps = outdeps
    for d in outdeps:
        for o in [d.w] + list(d.r):
            w = fw._need("sp", o)
            if w:
                fw.ops["sp"].append(("wait", w))
    fw.emit()
    return nc


WNAMES = ["attn_norm_g", "w_in", "nsa_q_g", "nsa_kc_g", "nsa_ks_g", "nsa_kw_g", "cmp_pe_k", "cmp_pe_v", "cmp_wk", "cmp_wv",
          "w_pool", "pool_scale", "mem_norm_g", "w_mem_kv", "mem_q_g", "mem_k_g", "w_up_nsa", "w_up_pool", "w_up_mem",
          "w_out", "ffn_norm_g", "w_gate_up", "w_down"]


def kernel(x_prompt, x_sample, mem_prompt, cache_nsa_kv, cache_win_kv, state_pool, cache_mem_kv, page_table, **w):
    x_prompt = np.asarray(x_prompt); x_sample = np.asarray(x_sample)
    B, S, _ = x_prompt.shape
    NL = cache_nsa_kv.shape[0]
    NPHYS = cache_nsa_kv.shape[1]
    NPG = page_table.shape[1]
    NT = S // 128
    NTT = NT + 1
    ncore = 8
    consts = make_consts(NT)
    CW = consts.shape[1]
    e64 = (np.arange(64)[:, None] == (np.arange(S)[None, :] // 64) % 64).astype(np.float32)
    nc = build(NL, NT, NPG, NPHYS, CW)
    cn = np.ascontiguousarray(np.asarray(cache_nsa_kv)).reshape(NL, NPHYS * 128, 512)
    in_maps = []
    for c in range(ncore):
        b = c % B
        xin = np.zeros((NTT * 128, D), np.float32)
        xin[:S] = x_prompt[b]
        xin[S:S + NSEQ] = x_sample[NSEQ * c:NSEQ * c + NSEQ, 0]
        m = {"xin": xin, "memin": np.ascontiguousarray(mem_prompt[b]), "cnsa": cn,
             "cwin": np.ascontiguousarray(np.asarray(cache_win_kv)[:, NSEQ * c:NSEQ * c + NSEQ]).reshape(NL, NSEQ, 512, 256),
             "spool": np.ascontiguousarray(np.asarray(state_pool)[:, NSEQ * c:NSEQ * c + NSEQ]),
             "cmem": np.ascontiguousarray(np.asarray(cache_mem_kv)[:, NSEQ * c:NSEQ * c + NSEQ]).reshape(NL, NSEQ, 256, 1024),
             "ptab": np.ascontiguousarray(np.asarray(page_table)[NSEQ * c:NSEQ * c + NSEQ]).reshape(1, NSEQ * NPG).astype(np.int32),
             "consts": consts, "e64": e64}
        for n in WNAMES:
            m[n] = np.ascontiguousarray(np.asarray(w[n], dtype=np.float32))
        in_maps.append(m)
    res = run_bass_kernel_spmd(nc, in_maps, core_ids=list(range(ncore)))
    R = res.results
    DB = x_sample.shape[0]
    y_p = np.stack([R[b]["o_y"][:S] for b in range(B)])
    y_s = np.concatenate([R[c]["o_y"][S:S + NSEQ] for c in range(ncore)])[:, None, :]
    nsa_p = np.stack([R[b]["o_nsa"][:, :S] for b in range(B)], axis=1).reshape(NL, B, S, 4, 2, 64)
    nsa_s = np.concatenate([R[c]["o_nsa"][:, S:S + NSEQ] for c in range(ncore)], axis=1).reshape(NL, DB, 1, 4, 2, 64)
    win_p = np.stack([R[b]["o_winp"] for b in range(B)], axis=1).reshape(NL, B, 512, 2, 2, 64)
    win_s = np.concatenate([R[c]["o_wins"] for c in range(ncore)], axis=1).reshape(NL, DB, 512, 2, 2, 64)
    pool_p = np.stack([R[b]["o_poolp"] for b in range(B)], axis=1)
    pool_s = np.concatenate([R[c]["o_pools"] for c in range(ncore)], axis=1)
    mem_p = np.stack([R[b]["o_memkv"] for b in range(B)], axis=1).reshape(NL, B, 256, 2, 4, 128)
    return (y_p.astype(np.float32), y_s.astype(np.float32), nsa_p, nsa_s, win_p, win_s, pool_p, pool_s, mem_p)
```

```python
import os
import numpy as np
import concourse.bass as bass
import concourse.mybir as mybir
from concourse.alu_op_type import AluOpType as ALU
from concourse.bass_utils import run_bass_kernel_spmd

F32 = mybir.dt.float32
BF16 = mybir.dt.bfloat16
I32 = mybir.dt.int32
AF = mybir.ActivationFunctionType
AX = mybir.AxisListType

ENGS = ("pe", "act", "dve", "pool", "sp")
SEM_LIMIT = 12000
N_DMA_SEMS = 8

D = 1024
PW = 5400
DFF = 2816
NSEQ = 4
EPS = 1e-6
BIGM = 30000.0


class _Stop(Exception):
    pass


KOPS = int(os.environ.get("KOPS", "0"))
_opc = [0]


def _count():
    _opc[0] += 1
    if os.environ.get("KWHO") and _opc[0] == int(os.environ["KWHO"]):
        import traceback
        print("WHO", _opc[0], [f"{f.lineno}:{f.line}" for f in traceback.extract_stack(limit=6)[:-2]])
    if KOPS and _opc[0] >= KOPS:
        raise _Stop()


class Dep:
    __slots__ = ("w", "r")

    def __init__(self):
        self.w = None
        self.r = []


class FW:
    def __init__(self, nc):
        self.nc = nc
        self.ops = {e: [] for e in ENGS}
        self.nops = {e: 0 for e in ENGS}
        self.signaled = {e: set() for e in ENGS}
        self.waited = {e: {} for e in ENGS}
        self.dma_count = {}
        self.relax = False
        self.dbg = {}
        self.dma_rr = {e: 0 for e in ENGS}

    def _need(self, e, opid):
        if opid is None:
            return None
        if opid[0] == "c":
            _, f, k = opid
            if f == e and e == "pe":
                return None
            if self.waited[e].get(("c", f), 0) >= k:
                return None
            self.waited[e][("c", f)] = k
            self.signaled[f].add(k)
            return opid
        _, q, j, m = opid
        if self.waited[e].get(("d", q, j), 0) >= m:
            return None
        self.waited[e][("d", q, j)] = m
        return opid

    def _collect(self, e, reads, writes):
        waits = []
        for d in reads:
            w = self._need(e, d.w)
            if w:
                waits.append(w)
        strict = not self.relax
        for d in writes:
            if strict or not (d.w is not None and d.w[0] == "c" and d.w[1] == e):
                w = self._need(e, d.w)
                if w:
                    waits.append(w)
            for r in d.r:
                if r[0] == "c" and r[1] == e and not strict:
                    continue
                w = self._need(e, r)
                if w:
                    waits.append(w)
        return waits

    def _mark(self, opid, reads, writes):
        for d in reads:
            d.r.append(opid)
            if len(d.r) > 48:
                last = {}
                for o in d.r:
                    key = o[:2] if o[0] == "c" else o[:3]
                    if key not in last or o[-1] > last[key][-1]:
                        last[key] = o
                d.r = list(last.values())
        for d in writes:
            d.w = opid
            d.r = []

    def op(self, e, fn, reads=(), writes=()):
        _count()
        for w in self._collect(e, reads, writes):
            self.ops[e].append(("wait", w))
        self.nops[e] += 1
        k = self.nops[e]
        if os.environ.get("KDBG"):
            import traceback
            st = traceback.extract_stack(limit=6)
            self.dbg[(e, k)] = " <- ".join(f"{f.lineno}" for f in st[:-1])
        self.ops[e].append(("op", fn, k))
        self._mark(("c", e, k), reads, writes)

    def dma(self, q, fn, reads=(), writes=()):
        _count()
        waits = self._collect(q, reads, writes)
        j = self.dma_rr[q]
        self.dma_rr[q] = (j + 1) % N_DMA_SEMS
        m_prev = self.dma_count.get((q, j), 0)
        if m_prev > 0:
            w = self._need(q, ("d", q, j, m_prev))
            if w:
                waits.append(w)
        for w in waits:
            self.ops[q].append(("wait", w))
        m = m_prev + 1
        self.dma_count[(q, j)] = m
        self.ops[q].append(("dma", fn, j))
        self._mark(("d", q, j, m), reads, writes)

    def barrier(self):
        if os.environ.get("KNOBAR"):
            return
        ids = {}
        for e in ("act", "dve", "pool"):
            self.op(e, self.bar_ops[e], reads=[self.bar_dep])
            ids[e] = ("c", e, self.nops[e])
        if self.nops["pe"] > 0:
            ids["pe"] = ("c", "pe", self.nops["pe"])
        dmas = [("d", q, j, m) for (q, j), m in self.dma_count.items()]
        for e in ENGS:
            for f, o in ids.items():
                if f != e:
                    w = self._need(e, o)
                    if w:
                        self.ops[e].append(("wait", w))
            for o in dmas:
                w = self._need(e, o)
                if w:
                    self.ops[e].append(("wait", w))

    def emit(self):
        nc = self.nc
        sigval, sems = {}, {}
        for e in ENGS:
            epoch, cnt = 0, 0
            for k in range(1, self.nops[e] + 1):
                if k in self.signaled[e]:
                    if cnt >= SEM_LIMIT:
                        epoch += 1
                        cnt = 0
                    cnt += 1
                    if (e, epoch) not in sems:
                        sems[(e, epoch)] = nc.alloc_semaphore(f"s_{e}_{epoch}")
                    sigval[(e, k)] = (sems[(e, epoch)], cnt)
                    if os.environ.get("KDBG") and e == os.environ.get("KDBG_E", "pe"):
                        print("SIG", e, epoch, cnt, "op", k, self.dbg.get((e, k)))
        dsems = {key: nc.alloc_semaphore(f"d_{key[0]}_{key[1]}") for key in self.dma_count}

        def run(E, e):
            for item in self.ops[e]:
                if item[0] == "wait":
                    w = item[1]
                    if w[0] == "c":
                        s, v = sigval[(w[1], w[2])]
                        E.wait_ge(s, v)
                    else:
                        E.wait_ge(dsems[(w[1], w[2])], 16 * w[3])
                elif item[0] == "op":
                    ins = item[1](E)
                    if item[2] in self.signaled[e]:
                        ins.then_inc(sigval[(e, item[2])][0], 1)
                else:
                    item[1](E).then_inc(dsems[(e, item[2])], 16)

        with nc.Block() as block:
            @block.sync
            def _(E):
                run(E, "sp")

            @block.gpsimd
            def _(E):
                run(E, "pool")

            @block.vector
            def _(E):
                run(E, "dve")

            @block.scalar
            def _(E):
                run(E, "act")

            @block.tensor
            def _(E):
                run(E, "pe")


C_OFF = {}


def make_consts(NT):
    cols = []
    off = [0]

    def add(name, a):
        a = np.asarray(a, np.float32)
        assert a.shape[0] == 128
        a = a.reshape(128, -1)
        C_OFF[name] = (off[0], a.shape[1])
        off[0] += a.shape[1]
        cols.append(a)

    p = np.arange(128)
    add("ident", np.eye(128))
    add("tric", (p[:, None] <= p[None, :]) * 1.0)
    add("trib", (p[:, None] > p[None, :]) * 1.0)
    add("selall", (p[:, None, None] == np.arange(4)[None, :, None]) * np.ones((1, 1, 128)))
    add("ones", np.ones((128, 1)))
    add("pm", (p[:, None] // 32 == np.arange(4)[None, :]) / 32.0)
    u = (p + 1) // 32
    xx = np.arange(252)
    add("tcmp", (xx[None, :] < 124 + u[:, None]) * 1.0)
    xs = np.arange(126)
    rel = xs[None, :] - 62 - (p[:, None] >= 64)
    add("bsel", np.where((rel == 0) | (rel == -1), 1e9, np.where(rel > 0, -1e9, 0.0)))
    wins = (2, 4, 8, 16)
    acur = np.zeros((128, 4, 128)); aprev = np.zeros((128, 4, 128)); afirst = np.zeros((128, 4, 128))
    for g, w in enumerate(wins):
        for t in range(128):
            for s_ in range(t - w + 1, t + 1):
                if s_ >= 0:
                    acur[s_, g, t] += 1.0 / w
                    afirst[s_, g, t] += 1.0 / min(t + 1, w)
                else:
                    aprev[128 + s_, g, t] += 1.0 / w
            acur[t, g, t] -= 1.0
            afirst[t, g, t] -= 1.0
    add("acur", acur); add("aprev", aprev); add("afirst", afirst)
    ast = np.zeros((128, 4))
    for g, w in enumerate(wins):
        for j in range(15):
            if j >= 15 - (w - 1):
                ast[j, g] = 1.0 / w
    add("ast", ast)
    anew = np.zeros((128, 4, 4))
    for s_ in range(4):
        for g, w in enumerate(wins):
            anew[s_, s_, g] = 1.0 / w - 1.0
    add("anew", anew)
    add("rowsel", (p[:, None] == np.arange(4)[None, :]) * 1.0)
    add("g8", ((p[:, None] // 4) == np.arange(2)[None, :]) * (p[:, None] < 8))
    inde = np.zeros((128, 2, 2, 128))
    for kv in range(2):
        for par in range(2):
            inde[kv, kv, par, :] = (np.arange(128) // 64 == par)
    add("inde", inde)
    add("sr", np.ones((128, 1, 1)) * (np.arange(4)[None, :, None] == np.arange(128)[None, None, :]))
    add("wmask", (p[:, None] != 0) * 1.0)
    add("npad", np.maximum(0.0, 511.0 - (128.0 * np.arange(4)[None, :] + p[:, None])))
    add("iota", p[:, None] * 1.0)
    return np.concatenate(cols, axis=1)


def build(NL, NT, NPG, NPHYS, CW):
    S = NT * 128
    NTT = NT + 1
    NB = 4 * NT
    NBS = 4 * NPG
    CB = min(128, NBS)
    NCK = NBS // CB
    A1W = 4608
    nc = bass.Bass("TRN2", target_bir_lowering=False)
    fw = FW(nc)

    def din(name, shape, dt=F32):
        return nc.dram_tensor(name, shape, dt, kind="ExternalInput").ap()

    def dout(name, shape):
        return nc.dram_tensor(name, shape, F32, kind="ExternalOutput").ap()

    xin = din("xin", [NTT * 128, D]); memin = din("memin", [256, D])
    cnsa = din("cnsa", [NL, NPHYS * 128, 512]); cwin = din("cwin", [NL, NSEQ, 512, 256])
    spool = din("spool", [NL, NSEQ, 15, 512]); cmem = din("cmem", [NL, NSEQ, 256, 1024])
    ptab = din("ptab", [1, NSEQ * NPG], I32)
    consts = din("consts", [128, CW]); e64 = din("e64", [64, S])
    w_attn_g = din("attn_norm_g", [NL, D]); w_in = din("w_in", [NL, D, PW])
    w_qg = din("nsa_q_g", [NL, 64]); w_kcg = din("nsa_kc_g", [NL, 64]); w_ksg = din("nsa_ks_g", [NL, 64]); w_kwg = din("nsa_kw_g", [NL, 64])
    w_pek = din("cmp_pe_k", [NL, 32, 64]); w_pev = din("cmp_pe_v", [NL, 32, 64])
    w_cwk = din("cmp_wk", [NL, 64, 64]); w_cwv = din("cmp_wv", [NL, 64, 64])
    w_pool = din("w_pool", [NL, 4, 128, 128]); w_psc = din("pool_scale", [NL, 512])
    w_memg = din("mem_norm_g", [NL, D]); w_memkv = din("w_mem_kv", [NL, D, 1024])
    w_mqg = din("mem_q_g", [NL, 128]); w_mkg = din("mem_k_g", [NL, 128])
    w_upn = din("w_up_nsa", [NL, 512, D]); w_upp = din("w_up_pool", [NL, 512, D]); w_upm = din("w_up_mem", [NL, 512, D])
    w_out = din("w_out", [NL, D, D]); w_ffng = din("ffn_norm_g", [NL, D])
    w_gu = din("w_gate_up", [NL, D, 2 * DFF]); w_dn = din("w_down", [NL, DFF, D])

    o_y = dout("o_y", [NTT * 128, D]); o_nsa = dout("o_nsa", [NL, NTT * 128, 512])
    o_winp = dout("o_winp", [NL, 512, 256]); o_wins = dout("o_wins", [NL, NSEQ, 512, 256])
    o_poolp = dout("o_poolp", [NL, 15, 512]); o_pools = dout("o_pools", [NL, NSEQ, 15, 512])
    o_memkv = dout("o_memkv", [NL, 256, 1024])
    outdeps = []

    xs = nc.dram_tensor("xs", [NTT * 128, D], F32, kind="Internal").ap()
    a1s = nc.dram_tensor("a1s", [NTT, 128, A1W], BF16, kind="Internal").ap()
    xs_d = [Dep() for _ in range(NTT)]
    a1s_d = [Dep() for _ in range(NTT)]
    a1s_d2 = [Dep() for _ in range(NTT)]

    class Tl:
        def __init__(self, h, n=1):
            self.h = h
            self.d = Dep()
            self.ds = [Dep() for _ in range(n)]

        def __getitem__(self, k):
            return self.h[k]

    def op(e, fn, R=(), W=()):
        fw.op(e, fn, [t.d if isinstance(t, Tl) else t for t in R], [t.d if isinstance(t, Tl) else t for t in W])

    def dma(q, out, in_, R=(), W=()):
        fw.dma(q, lambda E: E.dma_start(out=out, in_=in_), [t.d if isinstance(t, Tl) else t for t in R],
               [t.d if isinstance(t, Tl) else t for t in W])

    def dma_nc(q, out, in_, R=(), W=()):
        fw.dma(q, lambda E: E.dma_start(out=out, in_=in_, allow_slow_non_contiguous=True), [t.d if isinstance(t, Tl) else t for t in R],
               [t.d if isinstance(t, Tl) else t for t in W])

    def outdma(q, out, in_, R=()):
        d = Dep()
        outdeps.append(d)
        dma(q, out, in_, R=R, W=[d])

    def mm(ps, lhsT, rhs, start, stop, R, W):
        if os.environ.get("KNOF32") and lhsT.dtype == F32:
            return
        op("pe", lambda E: E.matmul(ps, lhsT=lhsT, rhs=rhs, start=start, stop=stop), R, W)

    def tr(ps, in_, idn, R, W):
        op("pe", lambda E: E.transpose(out=ps, in_=in_, identity=idn), R, W)

    def act(e, out, in_, func=AF.Copy, scale=None, accum=None, R=(), W=()):
        kw = {}
        if scale is not None:
            kw["scale"] = scale
        if accum is not None:
            kw["accum_out"] = accum
        op("act", lambda E: E.activation(out=out, in_=in_, func=func, **kw), R, W)

    def cp(e, out, in_, R, W):
        if e == "act":
            op("act", lambda E: E.activation(out=out, in_=in_, func=AF.Copy), R, W)
        elif e == "dve" and out.dtype == in_.dtype and out.dtype == BF16 and not os.environ.get("KCPRAW"):
            op(e, lambda E: E.tensor_scalar(out=out, in0=in_, scalar1=1.0, scalar2=None, op0=ALU.mult), R, W)
        else:
            op(e, lambda E: E.tensor_copy(out=out, in_=in_), R, W)

    def ts(e, out, in0, s1, s2, op0, op1=None, R=(), W=()):
        if op1 is None:
            op(e, lambda E: E.tensor_scalar(out=out, in0=in0, scalar1=s1, scalar2=None, op0=op0), R, W)
        else:
            op(e, lambda E: E.tensor_scalar(out=out, in0=in0, scalar1=s1, scalar2=s2, op0=op0, op1=op1), R, W)

    def tt(e, out, in0, in1, o, R, W):
        op(e, lambda E: E.tensor_tensor(out=out, in0=in0, in1=in1, op=o), R, W)

    def stt(out, in0, sc, in1, op0, op1, R, W):
        op("dve", lambda E: E.scalar_tensor_tensor(out=out, in0=in0, scalar=sc, in1=in1, op0=op0, op1=op1), R, W)

    def red(out, in_, R, W, o=ALU.add):
        op("dve", lambda E: E.tensor_reduce(out=out, in_=in_, axis=AX.X, op=o), R, W)

    import contextlib

    @contextlib.contextmanager
    def relaxed():
        old = fw.relax
        fw.relax = True
        try:
            yield
        finally:
            fw.relax = old

    def memset(e, ap, v, W, st=False):
        op(e, lambda E: E.memset(ap, v), [], W)

    def recip(out, in_, R, W):
        op("dve", lambda E: E.reciprocal(out=out, in_=in_), R, W)

    def vmax(out, in_, R, W):
        op("dve", lambda E: E.max(out=out, in_=in_), R, W)

    def mrep(out, rep, vals, imm, R, W):
        op("dve", lambda E: E.match_replace(out=out, in_to_replace=rep, in_values=vals, imm_value=imm), R, W)

    def ttr(out, in0, in1, accum, R, W):
        op("dve", lambda E: E.scalar_tensor_tensor(out=out, in0=in0, scalar=1.0, in1=in1, op0=ALU.mult, op1=ALU.mult, accum_out=accum), R, W)

    def rsqrt_chain(v, R_extra=()):
        act("act", v[:], v[:], AF.Sqrt, R=[v], W=[v])
        recip(v[:], v[:], [v], [v])

    from contextlib import ExitStack
    root = ExitStack()

    _uid = [0]

    def sb(stack, name, shape, dt=F32, n=1):
        _uid[0] += 1
        return Tl(stack.enter_context(nc.sbuf_tensor(f"{name}_{_uid[0]}", shape, dt)), n)

    def pst(stack, name, shape, dt=F32):
        _uid[0] += 1
        return Tl(stack.enter_context(nc.psum_tensor(f"{name}_{_uid[0]}", shape, dt)))

    cf = sb(root, "cf", [128, CW])
    CBW = 897
    cb = sb(root, "cb", [128, CBW], BF16)
    dma("sp", cf[:], consts[:, :], W=[cf])
    cp("dve", cb[:], cf[:, 0:CBW], [cf], [cb])

    def CF(name, sub=None):
        o, n = C_OFF[name]
        return cf.h[:, o:o + n]

    def CB_(name):
        o, n = C_OFF[name]
        return cb.h[:, o:o + n]

    identf = CF("ident"); identb = CB_("ident")
    pmf = CF("pm")
    tric_b = CB_("tric"); trib_b = CB_("trib")
    tcmp = CF("tcmp"); bsel = CF("bsel")
    acur = CF("acur").rearrange("p (g t) -> p g t", g=4); aprev = CF("aprev").rearrange("p (g t) -> p g t", g=4)
    afirst = CF("afirst").rearrange("p (g t) -> p g t", g=4)
    ast = CF("ast"); anew = CF("anew").rearrange("p (s g) -> p s g", s=4)
    selall_b = CB_("selall").rearrange("p (s m) -> p s m", s=4)
    rowsel = CF("rowsel"); g8 = CF("g8")
    inde = CF("inde").rearrange("p (a b k) -> p a b k", a=2, b=2)
    sr_f = CF("sr").rearrange("p (s m) -> p s m", s=4)
    npad = CF("npad")
    wmask = CF("wmask"); iota = CF("iota"); onesf = CF("ones"); onesb = CB_("ones")

    pti = sb(root, "pti", [128, NSEQ * NPG], I32)
    ptf = sb(root, "ptf", [128, NSEQ * NPG], F32)
    pidx = sb(root, "pidx", [128, NL, NSEQ * NPG], I32)
    ptf2 = sb(root, "ptf2", [128, NSEQ * NPG], F32)
    dma("sp", pti[:], ptab[0:1, :].to_broadcast([128, NSEQ * NPG]), W=[pti])
    cp("dve", ptf[:], pti[:], [pti], [ptf])
    ts("dve", ptf[:], ptf[:], 128.0, iota[:, 0:1], ALU.mult, ALU.add, R=[ptf, cf], W=[ptf])
    for l_ in range(NL):
        ts("dve", ptf2[:], ptf[:], float(l_ * NPHYS * 128), None, ALU.add, R=[ptf], W=[ptf2])
        cp("dve", pidx[:, l_, :], ptf2[:], [ptf2], [pidx])
    cnsa_flat = cnsa.rearrange("l r c -> (l r) c")

    bscr = sb(root, "bscr", [128, 4])
    memset("dve", bscr[:], 0.0, [bscr])
    _b0, _b1, _b2 = bscr[0:1, 0:1], bscr[0:1, 1:2], bscr[0:1, 2:3]
    fw.bar_dep = bscr.d
    fw.bar_ops = {"pe": lambda E: E.nop(), "sp": lambda E: E.nop(),
                  "act": lambda E: E.activation(out=_b0, in_=_b0, func=AF.Copy),
                  "dve": lambda E: E.tensor_copy(out=_b1, in_=_b1),
                  "pool": lambda E: E.tensor_copy(out=_b2, in_=_b2)}
    zerob = sb(root, "zerob", [128, 260], BF16)
    memset("pool", zerob[:], 0.0, [zerob])
    nsag = sb(root, "nsag", [128, NTT, 24])
    sampkv = sb(root, "sampkv", [128, 512])
    rr = [0]

    def alt(a="act", b="dve"):
        rr[0] += 1
        return a if rr[0] % 2 else b

    import os
    _stop = float(os.environ.get("KSTOP", "99"))

    KCUT = int(os.environ.get("KCUT", "99"))

    def chk(n):
        if n >= _stop:
            raise _Stop()

    try:
      chk(1)
      for l in range(NL):
          xsrc = xin if l == 0 else xs
          lays = ExitStack()
          lay = ExitStack()
          gq = sb(lays, "gq", [128, 64]); gkc = sb(lays, "gkc", [128, 64]); gks = sb(lays, "gks", [128, 64]); gkw = sb(lays, "gkw", [128, 64])
          gmq = sb(lays, "gmq", [128, 128]); gmk = sb(lays, "gmk", [128, 128])
          pemT = sb(lays, "pemT", [128, 2])
          wkbd = sb(lays, "wkbd", [128, 2, 128])
          kselT = sb(lay, "kselT", [128, 2, S], BF16, NT)
          vsel = sb(lay, "vsel", [128, NT, 2, 65], BF16, NT)
          kwinT = sb(lay, "kwinT", [64, 2, S], BF16, NT)
          vwin = sb(lay, "vwin", [128, NT, 2, 65], BF16, NT)
          kcT2 = sb(lay, "kcT2", [128, 2, 128], BF16)
          vcaug = sb(lay, "vcaug", [128, 2, 65], BF16)
          mkT = sb(lay, "mkT", [128, 4, 256], BF16)
          mva = sb(lay, "mva", [128, 2, 4, 129], BF16)
          for t_, src in ((gq, w_qg), (gkc, w_kcg), (gks, w_ksg), (gkw, w_kwg)):
              dma("sp", t_[:], src[l:l + 1, :].to_broadcast([128, 64]), W=[t_])
          for t_, src in ((gmq, w_mqg), (gmk, w_mkg)):
              dma("sp", t_[:], src[l:l + 1, :].to_broadcast([128, 128]), W=[t_])
          ts("dve", gq[:], gq[:], 0.125, None, ALU.mult, R=[gq], W=[gq])
          ts("dve", gmq[:], gmq[:], 128 ** -0.5, None, ALU.mult, R=[gmq], W=[gmq])
          dma("pool", kselT[64:128, 0, :], e64[:, :], W=kselT.ds + [kselT])
          dma("pool", kselT[64:128, 1, :], e64[:, :], W=kselT.ds + [kselT])
          memset("pool", vsel[:, :, :, 64:65], 1.0, vsel.ds + [vsel])
          memset("pool", vwin[:, :, :, 64:65], 1.0, vwin.ds + [vwin])
          memset("pool", vcaug[:, :, 64:65], 1.0, [vcaug])
          memset("pool", mva[:, :, :, 128:129], 1.0, [mva])
          memset("pool", wkbd[:], 0.0, [wkbd])
          for a_, src in ((0, w_cwk), (1, w_cwv)):
              dma("sp", wkbd[0:64, a_, 0:64], src[l], W=[wkbd])
              dma("sp", wkbd[64:128, a_, 64:128], src[l], W=[wkbd])

          p1 = ExitStack()
          win_sb = sb(p1, "win_sb", [128, 8, 2328], BF16)
          gT = sb(p1, "gT", [128, 8])
          wpl = sb(p1, "wpl", [128, 4, 128], BF16)
          psc = sb(p1, "psc", [128, 4])
          pe2 = sb(p1, "pe2", [32, 2, 128])
          xt_l = [sb(p1, "xt", [128, D]) for _ in range(2)]; xb_l = [sb(p1, "xb", [128, D], BF16) for _ in range(2)]
          hT_l = [sb(p1, "hT", [128, 8, 128], BF16) for _ in range(2)]
          junk_l = [sb(p1, "junk", [128, D], BF16) for _ in range(2)]
          z_l = [sb(p1, "z", [128, 2328]) for _ in range(2)]; a1_l = [sb(p1, "a1", [128, 1536], BF16) for _ in range(2)]
          sq_l = [sb(p1, "sq", [128, 1280]) for _ in range(2)]; rs_l = [sb(p1, "rs", [128, 16]) for _ in range(2)]
          ss_l = [sb(p1, "ss", [128, 1]) for _ in range(2)]
          xt, xb, hT, junk, z, a1, sq, rs, ss = xt_l[0], xb_l[0], hT_l[0], junk_l[0], z_l[0], a1_l[0], sq_l[0], rs_l[0], ss_l[0]
          ksb = sb(p1, "ksb", [128, 256], BF16)
          uprev = sb(p1, "uprev", [128, 512]); dTb = sb(p1, "dTb", [128, 4, 128], BF16)
          stt_ = sb(p1, "stt_", [15, NSEQ, 512])
          pooled = sb(p1, "pooled", [128, 2, 128]); kcf = sb(p1, "kcf", [128, 256]); kcb = sb(p1, "kcb", [128, 2, 128], BF16)
          ptb = pst(p1, "ptb", [128, 8, 128], BF16)
          pz = [pst(p1, f"pz{i}", [128, 512]) for i in range(3)]
          pcmp = pst(p1, "pcmp", [128, 2, 128])
          pd = pst(p1, "pd", [128, 4, 128])
          pk = pst(p1, "pk", [128, 4, 128], BF16)
          pmisc = pst(p1, "pmisc", [128, 512])

          for c in range(8):
              dma("pool", win_sb[:, c, :], w_in[l, c * 128:(c + 1) * 128, 0:2328], W=[win_sb])
          dma_nc("sp", gT[:], w_attn_g[l].rearrange("(c p) -> p c", p=128), W=[gT])
          dma("pool", wpl[:], w_pool[l].rearrange("g c e -> c g e"), W=[wpl])
          dma_nc("sp", psc[:], w_psc[l].rearrange("(g e) -> e g", e=128), W=[psc])
          for a_, src in ((0, w_pek), (1, w_pev)):
              dma("sp", pe2[:, a_, 0:64], src[l], W=[pe2])
              dma("sp", pe2[:, a_, 64:128], src[l], W=[pe2])
          dma("sp", stt_[:], spool[l].rearrange("s j c -> j s c"), W=[stt_])
          for a_ in range(2):
              mm(pmisc[:, a_:a_ + 1], pe2[:, a_, :], pmf[0:32, 0:1], True, True, [pe2, cf], [pmisc])
          cp("dve", pemT[:], pmisc[:, 0:2], [pmisc], [pemT])

          def norm_transpose(xt_, gT_, hT_, rstd_):
              act("act", junk[:], xt_[:], AF.Square, accum=rstd_[:], R=[xt_], W=[junk, rstd_])
              ts("dve", rstd_[:], rstd_[:], 1.0 / D, EPS, ALU.mult, ALU.add, R=[rstd_], W=[rstd_])
              rsqrt_chain(rstd_)
              cp("dve", xb[:], xt_[:], [xt_], [xb])
              for c in range(8):
                  tr(ptb[:, c, :], xb[:, c * 128:(c + 1) * 128], identb, [xb, cb], [ptb])
              for c in range(8):
                  with relaxed():
                      e = alt()
                      if e == "act":
                          act("act", hT_[:, c, :], ptb[:, c, :], AF.Copy, scale=gT_[:, c:c + 1], R=[ptb, gT_], W=[hT_])
                      else:
                          ts("dve", hT_[:, c, :], ptb[:, c, :], gT_[:, c:c + 1], None, ALU.mult, R=[ptb, gT_], W=[hT_])

          chk(1.5)
          zi = [0]
          for i in range(NTT):
              samp = (i == NT)
              _b = i % 2
              xt, xb, hT, junk, z, a1, sq, rs, ss = xt_l[_b], xb_l[_b], hT_l[_b], junk_l[_b], z_l[_b], a1_l[_b], sq_l[_b], rs_l[_b], ss_l[_b]
              if i == 0:
                  dma("sp", xt[:], xsrc[i * 128:(i + 1) * 128, :], R=[xs_d[i]], W=[xt])
              if i + 1 < NTT:
                  dma("sp", xt_l[1 - _b][:], xsrc[(i + 1) * 128:(i + 2) * 128, :], R=[xs_d[i + 1]], W=[xt_l[1 - _b]])
              norm_transpose(xt, gT, hT, ss)
              if KCUT <= 1:
                  chk(1.6 + 0.01 * i); continue
              col = 0
              while col < 2328:
                  wdt = min(512, 2328 - col)
                  p_ = pz[zi[0] % 3]; zi[0] += 1
                  for c in range(8):
                      mm(p_[:, 0:wdt], hT[:, c, :], win_sb[:, c, col:col + wdt], c == 0, c == 7, [hT, win_sb], [p_])
                  if col < 2328:
                      with relaxed():
                          e = alt()
                          if e == "act":
                              act("act", z[:, col:col + wdt], p_[:, 0:wdt], AF.Copy, scale=ss[:, 0:1], R=[p_, ss], W=[z])
                          else:
                              ts("dve", z[:, col:col + wdt], p_[:, 0:wdt], ss[:, 0:1], None, ALU.mult, R=[p_, ss], W=[z])
                  else:
                      a0 = 1536 + (col - 2328)
                      act("act", a1[:, a0:a0 + wdt], p_[:, 0:wdt], AF.Sigmoid, scale=ss[:, 0:1], R=[p_, ss], W=[a1])
                  col += wdt
              if KCUT <= 2:
                  chk(1.6 + 0.01 * i); continue
              with relaxed():
                  tt("dve", sq[:, 0:512], z[:, 0:512], z[:, 0:512], ALU.mult, [z], [sq])
                  tt("dve", sq[:, 512:640], z[:, 768:896], z[:, 768:896], ALU.mult, [z], [sq])
                  tt("dve", sq[:, 640:768], z[:, 1024:1152], z[:, 1024:1152], ALU.mult, [z], [sq])
                  tt("dve", sq[:, 768:1280], z[:, 1816:2328], z[:, 1816:2328], ALU.mult, [z], [sq])
              red(rs[:, 0:12], sq[:, 0:768].rearrange("p (h d) -> p h d", d=64), [sq], [rs])
              red(rs[:, 12:16], sq[:, 768:1280].rearrange("p (h d) -> p h d", d=128), [sq], [rs])
              ts("dve", rs[:, 0:12], rs[:, 0:12], 1.0 / 64, EPS, ALU.mult, ALU.add, R=[rs], W=[rs])
              ts("dve", rs[:, 12:16], rs[:, 12:16], 1.0 / 128, EPS, ALU.mult, ALU.add, R=[rs], W=[rs])
              rsqrt_chain(rs)
              for h in range(8):
                  with relaxed():
                      stt(a1[:, h * 64:(h + 1) * 64], z[:, h * 64:(h + 1) * 64], rs[:, h:h + 1], gq[:], ALU.mult, ALU.mult, [z, rs, gq], [a1])
              for kv in range(2):
                  stt(z[:, 768 + kv * 64:832 + kv * 64], z[:, 768 + kv * 64:832 + kv * 64], rs[:, 8 + kv:9 + kv], gks[:], ALU.mult, ALU.mult, [z, rs, gks], [z])
                  stt(z[:, 1024 + kv * 64:1088 + kv * 64], z[:, 1024 + kv * 64:1088 + kv * 64], rs[:, 10 + kv:11 + kv], gkw[:], ALU.mult, ALU.mult, [z, rs, gkw], [z])
              for h in range(4):
                  with relaxed():
                      stt(a1[:, 512 + h * 128:640 + h * 128], z[:, 1816 + h * 128:1944 + h * 128], rs[:, 12 + h:13 + h], gmq[:], ALU.mult, ALU.mult, [z, rs, gmq], [a1])
              act("act", nsag[:, i, :], z[:, 1280:1304], AF.Sigmoid, R=[z], W=[nsag])
              if KCUT <= 3:
                  chk(1.6 + 0.01 * i); continue
              outdma("sp", o_nsa[l, i * 128:(i + 1) * 128, :], z[:, 512:1024], R=[z])
              if not samp:
                  if i >= NT - 4:
                      outdma("sp", o_winp[l, (i - (NT - 4)) * 128:(i - (NT - 4) + 1) * 128, :], z[:, 1024:1280], R=[z])
                  if i == NT - 1:
                      outdma("sp", o_poolp[l, :, :], z[113:128, 1304:1816], R=[z])
                  if KCUT <= 4:
                      chk(1.6 + 0.01 * i); continue
                  K5 = int(os.environ.get("K5", "9"))
                  cp("act", vsel[:, i, :, 0:64], z[:, 896:1024].rearrange("p (k d) -> p k d", k=2), [z], [vsel.ds[i]])
                  cp("act", vwin[:, i, :, 0:64], z[:, 1152:1280].rearrange("p (k d) -> p k d", k=2), [z], [vwin.ds[i]])
                  if K5 >= 2:
                      cp("act", ksb[:, 0:128], z[:, 768:896], [z], [ksb])
                      cp("act", ksb[:, 128:256], z[:, 1024:1152], [z], [ksb])
                  if K5 >= 3:
                      for q_ in range(4):
                          tr(pk[0:64, q_, :], ksb[:, q_ * 64:(q_ + 1) * 64], identb, [ksb, cb], [pk])
                  if K5 >= 4:
                      cp("act", kselT[0:64, :, i * 128:(i + 1) * 128], pk[0:64, 0:2, :], [pk], [kselT.ds[i]])
                  if K5 >= 5:
                      cp("act", kwinT[0:64, :, i * 128:(i + 1) * 128], pk[0:64, 2:4, :], [pk], [kwinT.ds[i]])
                  if KCUT <= 5:
                      chk(1.6 + 0.01 * i); continue
                  mm(pcmp[:, 0, 4 * i:4 * i + 4], z[:, 512:640], pmf, True, True, [z, cf], [pcmp])
                  mm(pcmp[:, 1, 4 * i:4 * i + 4], z[:, 640:768], pmf, True, True, [z, cf], [pcmp])
                  for g in range(4):
                      if i == 0:
                          mm(pd[:, g, :], z[:, 1304 + g * 128:1432 + g * 128], afirst[:, g, :], True, True, [z, cf], [pd])
                      else:
                          mm(pd[:, g, :], z[:, 1304 + g * 128:1432 + g * 128], acur[:, g, :], True, False, [z, cf], [pd])
                          mm(pd[:, g, :], uprev[:, g * 128:(g + 1) * 128], aprev[:, g, :], False, True, [uprev, cf], [pd])
                  cp("act", dTb[:], pd[:], [pd], [dTb])
                  for g in range(4):
                      mm(pd[:, g, :], wpl[:, g, :], dTb[:, g, :], True, True, [wpl, dTb], [pd])
                  for g in range(4):
                      ts("dve", a1[:, 1024 + g * 128:1152 + g * 128], pd[:, g, :], psc[:, g:g + 1], None, ALU.mult, R=[pd, psc], W=[a1])
                  cp("act", uprev[:], z[:, 1304:1816], [z], [uprev])
              else:
                  cp("dve", sampkv[:], z[:, 768:1280], [z], [sampkv])
                  outdma("sp", o_wins[l, :, 511, :], z[0:NSEQ, 1024:1280], R=[z])
                  outdma("sp", o_pools[l, :, 14, :], z[0:NSEQ, 1304:1816], R=[z])
                  outdma("sp", o_wins[l, :, 0:511, :], cwin[l, :, 1:512, :])
                  outdma("sp", o_pools[l, :, 0:14, :], spool[l, :, 1:15, :])
                  for g in range(4):
                      for s_ in range(NSEQ):
                          mm(pd[:, g, s_:s_ + 1], stt_[0:15, s_, g * 128:(g + 1) * 128], ast[0:15, g:g + 1], True, False, [stt_, cf], [pd])
                          mm(pd[:, g, s_:s_ + 1], z[0:NSEQ, 1304 + g * 128:1432 + g * 128], anew[0:NSEQ, s_, g:g + 1], False, True, [z, cf], [pd])
                  cp("act", dTb[:, :, 0:NSEQ], pd[:, :, 0:NSEQ], [pd], [dTb])
                  for g in range(4):
                      mm(pd[:, g, 0:NSEQ], wpl[:, g, :], dTb[:, g, 0:NSEQ], True, True, [wpl, dTb], [pd])
                  for g in range(4):
                      ts("dve", a1[:, 1024 + g * 128:1024 + g * 128 + NSEQ], pd[:, g, 0:NSEQ], psc[:, g:g + 1], None, ALU.mult, R=[pd, psc], W=[a1])
              dma("sp", a1s[i][:, 0:1536], a1[:], R=[a1], W=[a1s_d[i]])
              chk(1.6 + 0.01 * i)

          chk(2)
          for a_ in range(2):
              ts("dve", pooled[:, a_, 0:NB], pcmp[:, a_, 0:NB], pemT[:, a_:a_ + 1], None, ALU.add, R=[pcmp, pemT], W=[pooled])
          mm(pmisc[0:NB, 0:128], pooled[:, 0, 0:NB], wkbd[:, 0, :], True, True, [pooled, wkbd], [pmisc])
          mm(pmisc[0:NB, 128:256], pooled[:, 1, 0:NB], wkbd[:, 1, :], True, True, [pooled, wkbd], [pmisc])
          cp("act", kcf[0:NB, :], pmisc[0:NB, 0:256], [pmisc], [kcf])
          tt("dve", sq[0:NB, 0:128], kcf[0:NB, 0:128], kcf[0:NB, 0:128], ALU.mult, [kcf], [sq])
          red(rs[0:NB, 0:2], sq[0:NB, 0:128].rearrange("p (h d) -> p h d", d=64), [sq], [rs])
          ts("dve", rs[0:NB, 0:2], rs[0:NB, 0:2], 1.0 / 64, EPS, ALU.mult, ALU.add, R=[rs], W=[rs])
          act("act", rs[0:NB, 0:2], rs[0:NB, 0:2], AF.Sqrt, R=[rs], W=[rs])
          recip(rs[0:NB, 0:2], rs[0:NB, 0:2], [rs], [rs])
          memset("pool", kcb[:], 0.0, [kcb])
          for kv in range(2):
              for dup in range(2):
                  stt(kcb[0:NB, kv, dup * 64:(dup + 1) * 64], kcf[0:NB, kv * 64:(kv + 1) * 64], rs[0:NB, kv:kv + 1], gkc[0:NB, :], ALU.mult, ALU.mult, [kcf, rs, gkc], [kcb])
          for kv in range(2):
              tr(pk[:, kv, 0:NB], kcb[0:NB, kv, :], identb[0:NB, 0:NB], [kcb, cb], [pk])
          memset("pool", kcT2[:], 0.0, [kcT2])
          cp("act", kcT2[:, :, 0:NB], pk[:, 0:2, 0:NB], [pk], [kcT2])
          memset("pool", vcaug[:, :, 0:64], 0.0, [vcaug])
          cp("dve", vcaug[0:NB, :, 0:64], kcf[0:NB, 128:256].rearrange("p (k d) -> p k d", k=2), [kcf], [vcaug])

          fw.barrier()
          p1.close()

          chk(3)
          p1b = ExitStack()
          win_sb = sb(p1b, "win_sb2", [128, 8, 3072], BF16)
          gT = sb(p1b, "gT2", [128, 8])
          xt_l = [sb(p1b, "xt1b", [128, D]) for _ in range(2)]; xb_l = [sb(p1b, "xb1b", [128, D], BF16) for _ in range(2)]
          hT_l = [sb(p1b, "hT1b", [128, 8, 128], BF16) for _ in range(2)]
          junk_l = [sb(p1b, "junk1b", [128, D], BF16) for _ in range(2)]; ss_l = [sb(p1b, "ss1b", [128, 1]) for _ in range(2)]
          a1m = [sb(p1b, f"a1m{k}", [128, 3072], BF16) for k in range(2)]
          ptb = pst(p1b, "ptb1b", [128, 8, 128], BF16)
          pz = [pst(p1b, f"pz1b{i}", [128, 512]) for i in range(3)]
          for c in range(8):
              dma("pool", win_sb[:, c, :], w_in[l, c * 128:(c + 1) * 128, 2328:PW], W=[win_sb])
          dma_nc("sp", gT[:], w_attn_g[l].rearrange("(c p) -> p c", p=128), W=[gT])
          for i in range(NTT):
              _b = i % 2
              xt, xb, hT, junk, ss = xt_l[_b], xb_l[_b], hT_l[_b], junk_l[_b], ss_l[_b]
              if i == 0:
                  dma("sp", xt[:], xsrc[i * 128:(i + 1) * 128, :], R=[xs_d[i]], W=[xt])
              if i + 1 < NTT:
                  dma("sp", xt_l[1 - _b][:], xsrc[(i + 1) * 128:(i + 2) * 128, :], R=[xs_d[i + 1]], W=[xt_l[1 - _b]])
              norm_transpose(xt, gT, hT, ss)
              am = a1m[i % 2]
              for n in range(6):
                  p_ = pz[zi[0] % 3]; zi[0] += 1
                  for c in range(8):
                      mm(p_[:], hT[:, c, :], win_sb[:, c, n * 512:(n + 1) * 512], c == 0, c == 7, [hT, win_sb], [p_])
                  with relaxed():
                      act("act", am[:, n * 512:(n + 1) * 512], p_[:], AF.Sigmoid, scale=ss[:, 0:1], R=[p_, ss], W=[am])
              dma("sp", a1s[i][:, 1536:A1W], am[:], R=[am], W=[a1s_d2[i]])
          fw.barrier()
          p1b.close()

          chk(4)
          p1 = ExitStack()
          xb = sb(p1, "xbm", [128, D], BF16); hT = sb(p1, "hTm", [128, 8, 128], BF16)
          junk = sb(p1, "junkm", [128, D], BF16); ss = sb(p1, "ssm", [128, 1])
          sq = sb(p1, "sqm", [128, 512]); rs = sb(p1, "rsm", [128, 4])
          memx = sb(p1, "memx", [128, D]); wmem = sb(p1, "wmem", [128, 8, 1024], BF16); gmT = sb(p1, "gmT", [128, 8])
          mkvf = sb(p1, "mkvf", [128, 1024]); mkb = sb(p1, "mkb", [128, 512], BF16)
          ptb = pst(p1, "ptbm", [128, 8, 128], BF16)
          pz = [pst(p1, f"pzm{i}", [128, 512]) for i in range(3)]
          pk = pst(p1, "pkm", [128, 4, 128], BF16)
          for c in range(8):
              dma("pool", wmem[:, c, :], w_memkv[l, c * 128:(c + 1) * 128, :], W=[wmem])
          dma_nc("sp", gmT[:], w_memg[l].rearrange("(c p) -> p c", p=128), W=[gmT])
          for mt in range(2):
              dma("sp", memx[:], memin[mt * 128:(mt + 1) * 128, :], W=[memx])
              norm_transpose(memx, gmT, hT, ss)
              for n in range(2):
                  p_ = pz[zi[0] % 3]; zi[0] += 1
                  for c in range(8):
                      mm(p_[:], hT[:, c, :], wmem[:, c, n * 512:(n + 1) * 512], c == 0, c == 7, [hT, wmem], [p_])
                  act("act", mkvf[:, n * 512:(n + 1) * 512], p_[:], AF.Copy, scale=ss[:, 0:1], R=[p_, ss], W=[mkvf])
              tt("dve", sq[:, 0:512], mkvf[:, 0:512], mkvf[:, 0:512], ALU.mult, [mkvf], [sq])
              red(rs[:, 0:4], sq[:, 0:512].rearrange("p (h d) -> p h d", d=128), [sq], [rs])
              ts("dve", rs[:, 0:4], rs[:, 0:4], 1.0 / 128, EPS, ALU.mult, ALU.add, R=[rs], W=[rs])
              rsqrt_chain(rs)
              for h in range(4):
                  stt(mkvf[:, h * 128:(h + 1) * 128], mkvf[:, h * 128:(h + 1) * 128], rs[:, h:h + 1], gmk[:], ALU.mult, ALU.mult, [mkvf, rs, gmk], [mkvf])
              outdma("sp", o_memkv[l, mt * 128:(mt + 1) * 128, :], mkvf[:], R=[mkvf])
              cp("act", mkb[:], mkvf[:, 0:512], [mkvf], [mkb])
              for h in range(4):
                  tr(pk[:, h, :], mkb[:, h * 128:(h + 1) * 128], identb, [mkb, cb], [pk])
              cp("act", mkT[:, :, mt * 128:(mt + 1) * 128], pk[:], [pk], [mkT])
              cp("dve", mva[:, mt, :, 0:128], mkvf[:, 512:1024].rearrange("p (h d) -> p h d", h=4), [mkvf], [mva])
          fw.barrier()
          p1.close()

          chk(5)
          for mode in ("prompt", "sample"):
              PR = (mode == "prompt")
              SM = not PR
              if SM:
                  lay.close()
              p2 = ExitStack()

              def mk(cond, name, shape, dt=F32):
                  return sb(p2, name + mode[0], shape, dt) if cond else None

              wup = sb(p2, "wup" + mode[0], [128, 3, 4, D], BF16)
              wo = sb(p2, "wo" + mode[0], [128, 8, D], BF16)
              nb2 = 2 if PR else 1
              a1_l = [sb(p2, "a1b" + mode[0], [128, A1W], BF16) for _ in range(nb2)]
              xt_l = [sb(p2, "xt2" + mode[0], [128, D]) for _ in range(nb2)]
              a1, xt = a1_l[0], xt_l[0]
              den = mk(True, "den", [128, 8]); rg = mk(True, "rg", [128, 8])
              onsa = mk(True, "onsa", [128, 8, 64]); otmp = mk(True, "otmp", [128, 8, 64]); onsab = mk(True, "onsab", [128, 512], BF16)
              brT = mk(True, "brT", [128, 2, 4, 128], BF16)
              memo = mk(True, "memo", [128, 4, 128], BF16)
              hsum = mk(True, "hsum", [128, D]); htmp = mk(True, "htmp", [128, 512]); hb = mk(True, "hb", [128, D], BF16)
              hT2 = mk(True, "hT2", [128, 8, 128], BF16)
              qtp = mk(PR, "qtp", [128, 8, 128], BF16)
              ec = mk(PR, "ec", [128, 8, 128]); pc = mk(PR, "pc", [128, 8, 128]); pcb = mk(PR, "pcb", [128, 8, 128], BF16)
              pcT = mk(PR, "pcT", [128, 8, 128], BF16)
              imp = mk(PR, "imp", [128, 2, 128]); sc = mk(PR, "sc", [128, 2, 64]); sc2 = mk(PR, "sc2", [128, 2, 64])
              m8 = mk(PR, "m8", [128, 2, 16]); selm = mk(PR, "selm", [128, 2, 64])
              qa = mk(PR, "qa", [128, 8, 128], BF16); qta = mk(PR, "qta", [128, 8, 128], BF16)
              pT = [mk(PR, f"pT{i}", [128, 512], BF16) for i in range(3)]
              mqT = mk(PR, "mqT", [128, 4, 128], BF16)
              pmT = mk(PR, "pmT", [128, 8, 128], BF16)
              pg = [mk(SM, f"pg{i}", [128, 512]) for i in range(3)]
              qbf = mk(SM, "qbf", [128, 512]); mqbf = mk(SM, "mqbf", [128, 512])
              prod = [mk(SM, "prod0", [128, 1024]), mk(SM, "prod1", [128, 512])]
              ssel = mk(SM, "ssel", [128, NPG, 8]); esel = mk(SM, "esel", [128, NPG, 8]); emb = mk(SM, "emb", [128, NPG, 8], BF16)
              vsall = mk(SM, "vsall", [128, NPG + 1, 2, 65], BF16)
              pooleds = mk(SM, "pooleds", [128, 2, NBS])
              kcs = mk(SM, "kcs", [128, 256]); kcsb = mk(SM, "kcsb", [128, 128], BF16); kcTs = mk(SM, "kcTs", [128, NBS], BF16)
              vcs = mk(SM, "vcs", [128, NCK, 2, 65], BF16)
              qbt = mk(SM, "qbt", [128, 8, 128], BF16); qbm = mk(SM, "qbm", [128, 8, 128], BF16)
              pe8 = mk(SM, "pe8", [8, NBS]); pn8 = mk(SM, "pn8", [8, NBS]); d8 = mk(SM, "d8", [8, 4])
              scs = mk(SM, "scs", [2, NBS // 2]); scs2 = mk(SM, "scs2", [2, NBS // 2]); sels = mk(SM, "sels", [2, NBS // 2]); m8s = mk(SM, "m8s", [2, 16])
              pTs = mk(SM, "pTs", [128, NCK, 8], BF16)
              pnew = mk(SM, "pnew", [128, 2, 8], BF16); snew = mk(SM, "snew", [128, 2, 8])
              vnew = mk(SM, "vnew", [128, 2, 2, 65], BF16)
              wc = mk(SM, "wc", [128, 4, 256]); sw = mk(SM, "sw", [128, 4, 8]); pw = mk(SM, "pw", [128, 4, 8], BF16)
              vwa = mk(SM, "vwa", [128, 4, 2, 65], BF16)
              mc = mk(SM, "mc", [128, 2, 1024]); smm = mk(SM, "smm", [128, 8]); pmm = mk(SM, "pmm", [128, 2, 4], BF16)
              vmb = mk(SM, "vmb", [128, 2, 512], BF16)
              oall = mk(SM, "oall", [8, 3, NSEQ, 64]); o8t = mk(SM, "o8t", [8, 2, 65]); o8 = mk(SM, "o8", [8, 65])
              mall = mk(SM, "mall", [4, NSEQ, 128]); m4t = mk(SM, "m4t", [4, 4, 128]); m4 = mk(SM, "m4", [4, 128])
              od = mk(SM, "od", [8, 8, 64]); odm = mk(SM, "odm", [4, 4, 128])
              ob = mk(SM, "ob", [128, 3, 512])
              d8_rs_t = mk(SM, "d8rs", [128, 2])
              d8_rs = d8_rs_t.h if SM else None
              b0 = pst(p2, "b0" + mode[0], [128, 8, 128], BF16)
              b12 = pst(p2, "b12" + mode[0], [128, 8, 128])
              b34 = [pst(p2, f"b3{i}" + mode[0], [128, 512]) for i in range(2)]
              acc = pst(p2, "acc" + mode[0], [128, 2, 512])
              b7 = pst(p2, "b7" + mode[0], [128, 8, 128], BF16)

              for bi, src in enumerate((w_upn, w_upp, w_upm)):
                  dma("pool", wup[:, bi, :, :], src[l].rearrange("(c p) n -> p c n", p=128), W=[wup])
              dma("pool", wo[:], w_out[l].rearrange("(c p) n -> p c n", p=128), W=[wo])
              if PR:
                  memset("pool", sc[:], 0.0, [sc])
              else:
                  memset("pool", vsall[:, :, :, 64:65], 1.0, [vsall])
                  memset("pool", vnew[:, :, :, 64:65], 1.0, [vnew])
                  memset("pool", vwa[:, :, :, 64:65], 1.0, [vwa])
                  memset("pool", vcs[:, :, :, 64:65], 1.0, [vcs])
                  memset("pool", qbm[:], 0.0, [qbm])

              acc4 = acc[:, :, 0:260].rearrange("p k (g e) -> p k g e", e=65)

              def branch_out(br, i, first):
                  ts("dve", den[:].rearrange("p (k g) -> p k g", k=2), acc4[:, :, :, 64], 1e-30, None, ALU.max, R=[acc], W=[den])
                  if br == 2 and i < 4:
                      ts("dve", den[:], den[:], npad[:, i:i + 1], None, ALU.add, R=[den, cf], W=[den])
                  recip(den[:], den[:], [den], [den])
                  gv = nsag[:, i, :].rearrange("p (h b) -> p h b", b=3)[:, :, br]
                  tt("dve", rg[:], den[:], gv, ALU.mult, [den, nsag], [rg])
                  if "csw"[br] in os.environ.get("KOFF", ""):
                      ts("dve", rg[:], rg[:], 0.0, None, ALU.mult, R=[rg], W=[rg])
                  rgb = rg[:].rearrange("p (k g) -> p k g", k=2).unsqueeze(3).to_broadcast([128, 2, 4, 64])
                  dst = onsa if first else otmp
                  tt("dve", dst[:].rearrange("p (k g) d -> p k g d", k=2), acc4[:, :, :, 0:64], rgb, ALU.mult, [acc, rg], [dst])
                  if not first:
                      tt("dve", onsa[:], onsa[:], otmp[:], ALU.add, [onsa, otmp], [onsa])

              pti_ = [0]

              def prompt_attention(i):
                  for h in range(8):
                      tr(b0[0:64, h, :], a1[:, h * 64:(h + 1) * 64], identb, [a1, cb], [b0])
                  cp("act", qtp[0:64, :, :], b0[0:64, :, :], [b0], [qtp])
                  for h in range(8):
                      mm(b12[:, h, :], qtp[0:64, h, :], kcT2[0:64, h // 4, :], True, True, [qtp, kcT2], [b12])
                  act("act", ec[:], b12[:], AF.Exp, R=[b12], W=[ec])
                  tm = tcmp[:, 124 - 4 * i:124 - 4 * i + 128]
                  for h in range(8):
                      with relaxed():
                          ttr(pc[:, h, :], ec[:, h, :], tm, den[:, h:h + 1], [ec, cf], [pc, den])
                  ts("dve", den[:], den[:], 1e-30, None, ALU.max, R=[den], W=[den])
                  recip(den[:], den[:], [den], [den])
                  for kv in range(2):
                      for g in range(4):
                          h = kv * 4 + g
                          if g == 0:
                              ts("dve", imp[:, kv, :], pc[:, h, :], den[:, h:h + 1], None, ALU.mult, R=[pc, den], W=[imp])
                          else:
                              stt(imp[:, kv, :], pc[:, h, :], den[:, h:h + 1], imp[:, kv, :], ALU.mult, ALU.add, [pc, den, imp], [imp])
                  iv = imp[:, :, 0:NB].rearrange("p k (b r) -> p k b r", r=2)
                  tt("dve", sc[:, :, 0:NB // 2], iv[:, :, :, 0], iv[:, :, :, 1], ALU.add, [imp], [sc])
                  bs = bsel[:, 62 - 2 * i:126 - 2 * i]
                  tt("dve", sc2[:], sc[:], bs.unsqueeze(1).to_broadcast([128, 2, 64]), ALU.add, [sc, cf], [sc2])
                  memset("dve", sc2[:, :, 0:1], 1e9, [sc2], st=True)
                  for kv in range(2):
                      vmax(m8[:, kv, 0:8], sc2[:, kv, :], [sc2], [m8])
                      mrep(selm[:, kv, :], m8[:, kv, 0:8], sc2[:, kv, :], -2e9, [sc2, m8], [selm])
                      vmax(m8[:, kv, 8:16], selm[:, kv, :], [selm], [m8])
                      ts("dve", selm[:, kv, :], sc2[:, kv, :], m8[:, kv, 15:16], None, ALU.is_ge, R=[sc2, m8], W=[selm])
                      ts("dve", qa[:, 4 * kv:4 * kv + 4, 64:128], selm[:, kv, :].unsqueeze(1).to_broadcast([128, 4, 64]), 1.0, BIGM, ALU.subtract, ALU.mult, R=[selm], W=[qa])
                  cp("act", qa[:, :, 0:64], a1[:, 0:512].rearrange("p (h d) -> p h d", h=8), [a1], [qa])
                  for h in range(8):
                      tr(b0[:, h, :], qa[:, h, :], identb, [qa, cb], [b0])
                  cp("act", qta[:], b0[:], [b0], [qta])
                  cp("act", pcb[:], pc[:], [pc], [pcb])
                  for h in range(8):
                      tr(b7[:, h, :], pcb[:, h, :], identb, [pcb, cb], [b7])
                  cp("dve", pcT[:], b7[:], [b7], [pcT])
                  for h in range(8):
                      mm(acc[:, h // 4, (h % 4) * 65:(h % 4) * 65 + 65], pcT[:, h, :], vcaug[:, h // 4, :], True, True, [pcT, vcaug], [acc])
                  branch_out(0, i, True)
                  if i == 0:
                      chk(5.1)
                  for br in (1, 2):
                      jlo = 0 if br == 1 else max(0, i - 4)
                      for kv in range(2):
                          mm(acc[:, kv, 0:260], zerob[:, 0:128], zerob[:, 0:260], True, False, [zerob], [acc])
                          for j in range(jlo, i + 1):
                              ps_ = b34[pti_[0] % 2]
                              pt_ = pT[pti_[0] % 3]
                              pti_[0] += 1
                              if br == 1:
                                  mm(ps_[:], kselT[:, kv, j * 128:(j + 1) * 128], qta[:, 4 * kv:4 * kv + 4, :], True, True, [kselT.ds[j], kselT, qta], [ps_])
                              else:
                                  mm(ps_[:], kwinT[0:64, kv, j * 128:(j + 1) * 128], qta[0:64, 4 * kv:4 * kv + 4, :], True, True, [kwinT.ds[j], qta], [ps_])
                              act("act", pt_[:], ps_[:], AF.Exp, R=[ps_], W=[pt_])
                              if j == i or (br == 2 and j == i - 4):
                                  msk = tric_b if j == i else trib_b
                                  tt("dve", pt_[:].rearrange("p (g t) -> p g t", g=4), pt_[:].rearrange("p (g t) -> p g t", g=4),
                                     msk.unsqueeze(1).to_broadcast([128, 4, 128]), ALU.mult, [pt_, cb], [pt_])
                              vv = vsel if br == 1 else vwin
                              for g in range(4):
                                  mm(acc[:, kv, g * 65:g * 65 + 65], pt_[:, g * 128:(g + 1) * 128], vv[:, j, kv, :], False, (j == i and g == 3), [pt_, vv.ds[j], vv], [acc])
                      branch_out(br, i, False)
                      if i == 0:
                          chk(5.2 + 0.01 * br)
                  for h in range(4):
                      tr(b0[:, h, :], a1[:, 512 + h * 128:640 + h * 128], identb, [a1, cb], [b0])
                  cp("act", mqT[:], b0[:, 0:4, :], [b0], [mqT])
                  for h in range(4):
                      for mt in range(2):
                          mm(b12[:, h * 2 + mt, :], mkT[:, h, mt * 128:(mt + 1) * 128], mqT[:, h, :], True, True, [mkT, mqT], [b12])
                  act("act", pmT[:], b12[:], AF.Exp, R=[b12], W=[pmT])
                  for h in range(4):
                      for mt in range(2):
                          mm(acc[:, h // 2, (h % 2) * 256:(h % 2) * 256 + 129], pmT[:, h * 2 + mt, :], mva[:, mt, h, :], mt == 0, mt == 1, [pmT, mva], [acc])
                  accm = acc[:, :, :].rearrange("p k (g e) -> p (k g) e", e=256)
                  cp("dve", den[:, 0:4], accm[:, :, 128], [acc], [den])
                  recip(den[:, 0:4], den[:, 0:4], [den], [den])
                  tt("dve", memo[:], accm[:, :, 0:128], den[:, 0:4].unsqueeze(2).to_broadcast([128, 4, 128]), ALU.mult, [acc, den], [memo])
                  if i == 0:
                      chk(5.3)

              def sample_attention():
                  i = NT
                  for kv in range(2):
                      cp("pool", qbm[:, 4 * kv:4 * kv + 4, 64 * kv:64 * kv + 64], a1[:, 256 * kv:256 * kv + 256].rearrange("p (g d) -> p g d", g=4), [a1], [qbm])
                  for h in range(8):
                      tr(b0[:, h, :], qbm[:, h, :], identb, [qbm, cb], [b0])
                  cp("act", qbt[:], b0[:], [b0], [qbt])
                  cp("act", vnew[:, 0, :, 0:64], sampkv[:, 128:256].rearrange("p (k d) -> p k d", k=2), [sampkv], [vnew])
                  cp("act", vnew[:, 1, :, 0:64], sampkv[:, 384:512].rearrange("p (k d) -> p k d", k=2), [sampkv], [vnew])
                  pcs = b12
                  pcs_v = b12[:].rearrange("p a b -> p (a b)")[:, 0:2 * NBS].rearrange("p (a b) -> p a b", a=2)
                  for s_ in range(NSEQ):
                      mm(b34[0][:], selall_b[:, s_, :], a1[:, 0:512], True, True, [cb, a1], [b34[0]])
                      cp("act", qbf[:], b34[0][:], [b34[0]], [qbf])
                      mm(b34[1][:], selall_b[:, s_, :], a1[:, 512:1024], True, True, [cb, a1], [b34[1]])
                      cp("act", mqbf[:], b34[1][:], [b34[1]], [mqbf])
                      qv = qbf[:].rearrange("p (k g d) -> p k g d", k=2, g=4)

                      def dots(dst, ksrc, pi):
                          pr = prod[pi % 2]
                          e = "dve" if pi % 2 == 0 else "pool"
                          tt(e, pr[:, 0:512].rearrange("p (k g d) -> p k g d", k=2, g=4),
                             ksrc.rearrange("p (k d) -> p k d", k=2).unsqueeze(2).to_broadcast([128, 2, 4, 64]), qv, ALU.mult, [qbf] + dots_R, [pr])
                          red(dst, pr[:, 0:512].rearrange("p (h d) -> p h d", d=64), [pr], dots_W)

                      for j in range(NPG):
                          pgt = pg[j % 3]
                          col = s_ * NPG + j
                          fw.dma("pool", lambda E, o_=pgt[:], i_=cnsa_flat, x_=pidx[:, l, col:col + 1]: E.indirect_dma_start(
                              out=o_, out_offset=None, in_=i_,
                              in_offset=bass.IndirectOffsetOnAxis(ap=x_, axis=0)), [pidx.d], [pgt.d])
                          mm(pcs_v[:, 0, 4 * j:4 * j + 4], pgt[:, 0:128], pmf, True, True, [pgt, cf], [b12])
                          mm(pcs_v[:, 1, 4 * j:4 * j + 4], pgt[:, 128:256], pmf, True, True, [pgt, cf], [b12])
                          dots_R = [pgt]; dots_W = [ssel]
                          dots(ssel[:, j, :], pgt[:, 256:384], j)
                          cp("act", vsall[:, j, :, 0:64], pgt[:, 384:512].rearrange("p (k d) -> p k d", k=2), [pgt], [vsall])
                      act("act", esel[:], ssel[:], AF.Exp, R=[ssel], W=[esel])
                      for a_ in range(2):
                          ts("dve", pooleds[:, a_, :], pcs_v[:, a_, :], pemT[:, a_:a_ + 1], None, ALU.add, R=[b12, pemT], W=[pooleds])
                      for ck in range(NCK):
                          mm(b34[0][0:CB, 0:128], pooleds[:, 0, ck * CB:(ck + 1) * CB], wkbd[:, 0, :], True, True, [pooleds, wkbd], [b34[0]])
                          mm(b34[0][0:CB, 128:256], pooleds[:, 1, ck * CB:(ck + 1) * CB], wkbd[:, 1, :], True, True, [pooleds, wkbd], [b34[0]])
                          cp("act", kcs[0:CB, :], b34[0][0:CB, 0:256], [b34[0]], [kcs])
                          tt("dve", prod[0][0:CB, 0:128], kcs[0:CB, 0:128], kcs[0:CB, 0:128], ALU.mult, [kcs], [prod[0]])
                          red(d8_rs[0:CB, 0:2], prod[0][0:CB, 0:128].rearrange("p (h d) -> p h d", d=64), [prod[0]], [d8_rs_t])
                          ts("dve", d8_rs[0:CB, 0:2], d8_rs[0:CB, 0:2], 1.0 / 64, EPS, ALU.mult, ALU.add, R=[d8_rs_t], W=[d8_rs_t])
                          act("act", d8_rs[0:CB, 0:2], d8_rs[0:CB, 0:2], AF.Sqrt, R=[d8_rs_t], W=[d8_rs_t])
                          recip(d8_rs[0:CB, 0:2], d8_rs[0:CB, 0:2], [d8_rs_t], [d8_rs_t])
                          for kv in range(2):
                              stt(kcsb[0:CB, kv * 64:(kv + 1) * 64], kcs[0:CB, kv * 64:(kv + 1) * 64], d8_rs[0:CB, kv:kv + 1], gkc[0:CB, :], ALU.mult, ALU.mult, [kcs, d8_rs_t, gkc], [kcsb])
                          tr(b7[:, 0, 0:CB], kcsb[0:CB, :], identb[0:CB, 0:CB], [kcsb, cb], [b7])
                          cp("act", kcTs[:, ck * CB:(ck + 1) * CB], b7[:, 0, 0:CB], [b7], [kcTs])
                          cp("dve", vcs[0:CB, ck, :, 0:64], kcs[0:CB, 128:256].rearrange("p (k d) -> p k d", k=2), [kcs], [vcs])
                      mm(b34[1][0:8, 0:NBS], qbt[:, :, s_], kcTs[:], True, True, [qbt, kcTs], [b34[1]])
                      act("act", pe8[:], b34[1][0:8, 0:NBS], AF.Exp, accum=d8[:, 0:1], R=[b34[1]], W=[pe8, d8])
                      recip(d8[:, 1:2], d8[:, 0:1], [d8], [d8])
                      ts("dve", pn8[:], pe8[:], d8[:, 1:2], None, ALU.mult, R=[pe8, d8], W=[pn8])
                      mm(b34[0][0:2, 0:NBS], g8[0:8, :], pn8[:], True, True, [cf, pn8], [b34[0]])
                      iv = b34[0][0:2, 0:NBS].rearrange("p (b r) -> p b r", r=2)
                      cp("act", scs2[:], iv[:, :, 0], [b34[0]], [scs2])
                      tt("dve", scs[:], scs2[:], iv[:, :, 1], ALU.add, [scs2, b34[0]], [scs])
                      memset("dve", scs[:, 0:1], 1e9, [scs])
                      memset("dve", scs[:, NBS // 2 - 1:NBS // 2], 1e9, [scs])
                      vmax(m8s[:, 0:8], scs[:], [scs], [m8s])
                      mrep(scs2[:], m8s[:, 0:8], scs[:], -2e9, [scs, m8s], [scs2])
                      vmax(m8s[:, 8:16], scs2[:], [scs2], [m8s])
                      ts("dve", sels[:], scs[:], m8s[:, 14:15], None, ALU.is_ge, R=[scs, m8s], W=[sels])
                      for ck in range(NCK):
                          tr(b34[1][0:CB, 256 + ck * 8:264 + ck * 8], pn8[:, ck * CB:(ck + 1) * CB], identf[0:8, 0:8], [pn8, cf], [b34[1]])
                          cp("act", pTs[0:CB, ck, :], b34[1][0:CB, 256 + ck * 8:264 + ck * 8], [b34[1]], [pTs])
                      for ck in range(NCK):
                          mm(acc[0:8, 0, 0:130], pTs[0:CB, ck, :], vcs[0:CB, ck, :, :].rearrange("p k e -> p (k e)"), ck == 0, ck == NCK - 1, [pTs, vcs], [acc])
                      pick8(0, s_, False)
                      sv = sels[:].rearrange("p (j r) -> p j r", r=2)
                      for kv in range(2):
                          mm(b34[1][:, kv * NPG:(kv + 1) * NPG], inde[0:2, kv, 0, :], sv[:, :, 0], True, False, [cf, sels], [b34[1]])
                          mm(b34[1][:, kv * NPG:(kv + 1) * NPG], inde[0:2, kv, 1, :], sv[:, :, 1], False, True, [cf, sels], [b34[1]])
                      mv = b34[1][:, 0:2 * NPG].rearrange("p (k j) -> p j k", k=2).unsqueeze(3).to_broadcast([128, NPG, 2, 4])
                      tt("dve", emb[:].rearrange("p j (k g) -> p j k g", k=2), esel[:].rearrange("p j (k g) -> p j k g", k=2), mv, ALU.mult, [esel, b34[1]], [emb])
                      dots_R = [sampkv]; dots_W = [snew]
                      dots(snew[:, 0, :], sampkv[:, 0:128], 0)
                      dots(snew[:, 1, :], sampkv[:, 256:384], 1)
                      act("act", snew[:], snew[:], AF.Exp, R=[snew], W=[snew])
                      ts("dve", pnew[:], snew[:], rowsel[:, s_:s_ + 1], None, ALU.mult, R=[snew, cf], W=[pnew])
                      for j in range(NPG):
                          mm(acc[0:8, 0, 0:130], emb[:, j, :], vsall[:, j, :, :].rearrange("p k e -> p (k e)"), j == 0, False, [emb, vsall], [acc])
                      mm(acc[0:8, 0, 0:130], pnew[:, 0, :], vnew[:, 0, :, :].rearrange("p k e -> p (k e)"), False, True, [pnew, vnew], [acc])
                      pick8(1, s_, True)
                      dma("sp", wc[:], cwin[l, s_].rearrange("(kt p) c -> p kt c", p=128), W=[wc])
                      for kt in range(4):
                          dots_R = [wc]; dots_W = [sw]
                          dots(sw[:, kt, :], wc[:, kt, 0:128], kt)
                      act("act", sw[:], sw[:], AF.Exp, R=[sw], W=[sw])
                      ts("dve", pw[:, 0, :], sw[:, 0, :], wmask[:, 0:1], None, ALU.mult, R=[sw, cf], W=[pw])
                      cp("dve", pw[:, 1:4, :], sw[:, 1:4, :], [sw], [pw])
                      cp("act", vwa[:, :, :, 0:64], wc[:, :, 128:256].rearrange("p t (k d) -> p t k d", k=2), [wc], [vwa])
                      for kt in range(4):
                          mm(acc[0:8, 0, 0:130], pw[:, kt, :], vwa[:, kt, :, :].rearrange("p k e -> p (k e)"), kt == 0, False, [pw, vwa], [acc])
                      mm(acc[0:8, 0, 0:130], pnew[:, 1, :], vnew[:, 1, :, :].rearrange("p k e -> p (k e)"), False, True, [pnew, vnew], [acc])
                      pick8(2, s_, True)
                      dma("sp", mc[:], cmem[l, s_].rearrange("(mt p) c -> p mt c", p=128), W=[mc])
                      tt("dve", prod[0][:].rearrange("p (t c) -> p t c", t=2), mc[:, :, 0:512], mqbf[:].unsqueeze(1).to_broadcast([128, 2, 512]), ALU.mult, [mc, mqbf], [prod[0]])
                      red(smm[:], prod[0][:].rearrange("p (h d) -> p h d", d=128), [prod[0]], [smm])
                      act("act", pmm[:].rearrange("p t h -> p (t h)"), smm[:], AF.Exp, R=[smm], W=[pmm])
                      cp("pool", vmb[:], mc[:, :, 512:1024], [mc], [vmb])
                      for mt in range(2):
                          mm(acc[0:4, 1, :], pmm[:, mt, :], vmb[:, mt, :], mt == 0, mt == 1, [pmm, vmb], [acc])
                      for mt in range(2):
                          mm(b34[0][0:4, 0:1], pmm[:, mt, :], onesb[:, 0:1], mt == 0, mt == 1, [pmm, cb], [b34[0]])
                      tt("dve", m4t[:], acc[0:4, 1, :].rearrange("p (a d) -> p a d", a=4), identf[0:4, 0:4].unsqueeze(2).to_broadcast([4, 4, 128]), ALU.mult, [acc, cf], [m4t])
                      red(m4[:], m4t[:].rearrange("p a d -> p d a"), [m4t], [m4])
                      recip(d8[0:4, 2:3], b34[0][0:4, 0:1], [b34[0].d], [d8.d])
                      ts("dve", mall[:, s_, :], m4[:], d8[0:4, 2:3], None, ALU.mult, R=[m4, d8], W=[mall])
                  for br in range(3):
                      for s_ in range(NSEQ):
                          tt("dve", od[:], oall[:, br, s_, :].unsqueeze(1).to_broadcast([8, 8, 64]), identf[0:8, 0:8].unsqueeze(2).to_broadcast([8, 8, 64]), ALU.mult, [oall, cf], [od])
                          mm(b12[:, 0:4, :].rearrange("p a b -> p (a b)"), sr_f[0:8, s_, :], od[:].rearrange("p a b -> p (a b)"), True, True, [cf, od], [b12])
                          if s_ == 0:
                              cp("act", ob[:, br, :], b12[:, 0:4, :].rearrange("p a b -> p (a b)"), [b12], [ob])
                          else:
                              tt("dve", ob[:, br, :], ob[:, br, :], b12[:, 0:4, :].rearrange("p a b -> p (a b)"), ALU.add, [ob, b12], [ob])
                  for br in range(3):
                      gv = nsag[:, i, :].rearrange("p (h b) -> p h b", b=3)[:, :, br].unsqueeze(2).to_broadcast([128, 8, 64])
                      dst = onsa if br == 0 else otmp
                      tt("dve", dst[:], ob[:, br, :].rearrange("p (h d) -> p h d", h=8), gv, ALU.mult, [ob, nsag], [dst])
                      if br > 0:
                          tt("dve", onsa[:], onsa[:], otmp[:], ALU.add, [onsa, otmp], [onsa])
                  for s_ in range(NSEQ):
                      tt("dve", odm[:], mall[:, s_, :].unsqueeze(1).to_broadcast([4, 4, 128]), identf[0:4, 0:4].unsqueeze(2).to_broadcast([4, 4, 128]), ALU.mult, [mall, cf], [odm])
                      mm(b12[:, 4:8, :].rearrange("p a b -> p (a b)"), sr_f[0:4, s_, :], odm[:].rearrange("p a b -> p (a b)"), True, True, [cf, odm], [b12])
                      if s_ == 0:
                          cp("act", hsum[:, 0:512], b12[:, 4:8, :].rearrange("p a b -> p (a b)"), [b12], [hsum])
                      else:
                          tt("dve", hsum[:, 0:512], hsum[:, 0:512], b12[:, 4:8, :].rearrange("p a b -> p (a b)"), ALU.add, [hsum, b12], [hsum])
                  cp("dve", memo[:].rearrange("p h d -> p (h d)"), hsum[:, 0:512], [hsum], [memo])

              def pick8(br, s_, normalize):
                  tt("dve", o8t[:], acc[0:8, 0, 0:130].rearrange("p (k e) -> p k e", k=2), g8[0:8, :].unsqueeze(2).to_broadcast([8, 2, 65]), ALU.mult, [acc, cf], [o8t])
                  tt("dve", o8[:], o8t[:, 0, :], o8t[:, 1, :], ALU.add, [o8t], [o8])
                  if normalize:
                      recip(d8[:, 3:4], o8[:, 64:65], [o8.d], [d8.d])
                      ts("dve", oall[:, br, s_, :], o8[:, 0:64], d8[:, 3:4], None, ALU.mult, R=[o8, d8], W=[oall])
                  else:
                      cp("dve", oall[:, br, s_, :], o8[:, 0:64], [o8], [oall])

              for i in (range(NT) if PR else [NT]):
                  _b = i % nb2
                  a1, xt = a1_l[_b], xt_l[_b]
                  if i == 0 or not PR:
                      dma("sp", a1[:], a1s[i], R=[a1s_d[i], a1s_d2[i]], W=[a1])
                      dma("sp", xt[:], xsrc[i * 128:(i + 1) * 128, :], R=[xs_d[i]], W=[xt])
                  if PR and i + 1 < NT:
                      dma("sp", a1_l[1 - _b][:], a1s[i + 1], R=[a1s_d[i + 1], a1s_d2[i + 1]], W=[a1_l[1 - _b]])
                      dma("sp", xt_l[1 - _b][:], xsrc[(i + 1) * 128:(i + 2) * 128, :], R=[xs_d[i + 1]], W=[xt_l[1 - _b]])
                  if i < NT:
                      prompt_attention(i)
                  else:
                      sample_attention()
                  cp("act", onsab[:], onsa[:].rearrange("p h d -> p (h d)"), [onsa], [onsab])
                  for c in range(4):
                      tr(b0[:, c, :], onsab[:, c * 128:(c + 1) * 128], identb, [onsab, cb], [b0])
                      tr(b0[:, 4 + c, :], memo[:, c, :], identb, [memo, cb], [b0])
                  cp("act", brT[:].rearrange("p a c t -> p (a c) t"), b0[:], [b0], [brT])
                  pyT = a1[:, 1024:1536].rearrange("p (g t) -> p g t", g=4)
                  for n in range(2):
                      for bi in range(3):
                          p_ = b34[bi % 2]
                          for c in range(4):
                              lt = brT[:, 0, c, :] if bi == 0 else (pyT[:, c, :] if bi == 1 else brT[:, 1, c, :])
                              mm(p_[:], lt, wup[:, bi, c, n * 512:(n + 1) * 512], c == 0, c == 3, [brT, a1, wup], [p_])
                          mgv = a1[:, 1536 + bi * 1024 + n * 512:1536 + bi * 1024 + (n + 1) * 512]
                          if bi == 0:
                              tt("dve", hsum[:, n * 512:(n + 1) * 512], p_[:], mgv, ALU.mult, [p_, a1], [hsum])
                          else:
                              tt("dve", htmp[:], p_[:], mgv, ALU.mult, [p_, a1], [htmp])
                              if "xpm"[bi] in os.environ.get("KOFF", ""):
                                  ts("dve", htmp[:], htmp[:], 0.0, None, ALU.mult, R=[htmp], W=[htmp])
                              if bi == 1:
                                  tt("dve", hsum[:, n * 512:(n + 1) * 512], hsum[:, n * 512:(n + 1) * 512], htmp[:], ALU.add, [hsum, htmp], [hsum])
                              else:
                                  tt("dve", hb[:, n * 512:(n + 1) * 512], hsum[:, n * 512:(n + 1) * 512], htmp[:], ALU.add, [hsum, htmp], [hb])
                  for c in range(8):
                      tr(b7[:, c, :], hb[:, c * 128:(c + 1) * 128], identb, [hb, cb], [b7])
                  cp("act", hT2[:], b7[:], [b7], [hT2])
                  for n in range(2):
                      p_ = b34[n]
                      for c in range(8):
                          mm(p_[:], hT2[:, c, :], wo[:, c, n * 512:(n + 1) * 512], c == 0, c == 7, [hT2, wo], [p_])
                      tt("dve", xt[:, n * 512:(n + 1) * 512], xt[:, n * 512:(n + 1) * 512], p_[:], ALU.add, [xt, p_], [xt])
                  dma("sp", xs[i * 128:(i + 1) * 128, :], xt[:], R=[xt], W=[xs_d[i]])
                  chk(5.4 + 0.01 * i)
              fw.barrier()
              p2.close()
          lays.close()

          chk(8)
          p3 = ExitStack()
          wgu = sb(p3, "wgu", [128, 8, 2 * DFF], BF16)
          wdn = sb(p3, "wdn", [128, 22, D], BF16)
          gfT = sb(p3, "gfT", [128, 8])
          x4 = [sb(p3, f"x4_{k}", [128, D]) for k in range(2)]
          xb3 = sb(p3, "xb3", [128, D], BF16); junk3 = sb(p3, "junk3", [128, D], BF16); ss3 = sb(p3, "ss3", [128, 1])
          hT3 = sb(p3, "hT3", [128, 8, 256], BF16)
          aT = sb(p3, "aT", [128, 22, 256], BF16)
          sg = [sb(p3, f"sg{k}", [128, 256]) for k in range(2)]
          ptb3 = pst(p3, "ptb3", [128, 8, 128], BF16)
          pgk = [pst(p3, f"pgk{k}", [128, 512]) for k in range(2)]
          puk = [pst(p3, f"puk{k}", [128, 512]) for k in range(2)]
          pdn = [pst(p3, f"pdn{k}", [128, 512]) for k in range(2)]
          for c in range(8):
              dma("pool", wgu[:, c, :], w_gu[l, c * 128:(c + 1) * 128, :], W=[wgu])
          dma("pool", wdn[:], w_dn[l].rearrange("(c p) n -> p c n", p=128), W=[wdn])
          dma_nc("sp", gfT[:], w_ffng[l].rearrange("(c p) -> p c", p=128), W=[gfT])
          st = 0
          kk = 0
          while st < NTT:
              nt_ = min(2, NT - st) if st < NT else 1
              TW = nt_ * 128
              for k in range(nt_):
                  i = st + k
                  dma("sp", x4[k][:], xs[i * 128:(i + 1) * 128, :], R=[xs_d[i]], W=[x4[k]])
                  act("act", junk3[:], x4[k][:], AF.Square, accum=ss3[:], R=[x4[k]], W=[junk3, ss3])
                  ts("dve", ss3[:], ss3[:], 1.0 / D, EPS, ALU.mult, ALU.add, R=[ss3], W=[ss3])
                  rsqrt_chain(ss3)
                  act("act", xb3[:], x4[k][:], AF.Copy, scale=ss3[:, 0:1], R=[x4[k], ss3], W=[xb3])
                  for c in range(8):
                      tr(ptb3[:, c, :], xb3[:, c * 128:(c + 1) * 128], identb, [xb3, cb], [ptb3])
                  for c in range(8):
                      with relaxed():
                          ts("dve", hT3[:, c, k * 128:(k + 1) * 128], ptb3[:, c, :], gfT[:, c:c + 1], None, ALU.mult, R=[ptb3, gfT], W=[hT3])
              for fc in range(22):
                  pg_, pu_, sg_ = pgk[kk % 2], puk[kk % 2], sg[kk % 2]
                  kk += 1
                  for c in range(8):
                      mm(pg_[:, 0:TW], wgu[:, c, fc * 128:(fc + 1) * 128], hT3[:, c, 0:TW], c == 0, c == 7, [wgu, hT3], [pg_])
                  for c in range(8):
                      mm(pu_[:, 0:TW], wgu[:, c, DFF + fc * 128:DFF + (fc + 1) * 128], hT3[:, c, 0:TW], c == 0, c == 7, [wgu, hT3], [pu_])
                  act("act", sg_[:, 0:TW], pg_[:, 0:TW], AF.Silu, R=[pg_], W=[sg_])
                  with relaxed():
                      tt("dve", aT[:, fc, 0:TW], pu_[:, 0:TW], sg_[:, 0:TW], ALU.mult, [pu_, sg_], [aT])
              for k in range(nt_):
                  i = st + k
                  for n in range(2):
                      p_ = pdn[n]
                      for fc in range(22):
                          mm(p_[:], aT[:, fc, k * 128:(k + 1) * 128], wdn[:, fc, n * 512:(n + 1) * 512], fc == 0, fc == 21, [aT, wdn], [p_])
                      tt("dve", x4[k][:, n * 512:(n + 1) * 512], x4[k][:, n * 512:(n + 1) * 512], p_[:], ALU.add, [x4[k], p_], [x4[k]])
                  if l == NL - 1:
                      outdma("sp", o_y[i * 128:(i + 1) * 128, :], x4[k][:], R=[x4[k]])
                  else:
                      dma("sp", xs[i * 128:(i + 1) * 128, :], x4[k][:], R=[x4[k]], W=[xs_d[i]])
              st += nt_
          fw.barrier()
          p3.close()

    except _Stop:
        pass
    print("OPCOUNT", _opc[0])
    fw.finish_deps = outdeps
    for d in outdeps:
        for o in [d.w] + list(d.r):
            w = fw._need("sp", o)
            if w:
                fw.ops["sp"].append(("wait", w))
    fw.emit()
    return nc


WNAMES = ["attn_norm_g", "w_in", "nsa_q_g", "nsa_kc_g", "nsa_ks_g", "nsa_kw_g", "cmp_pe_k", "cmp_pe_v", "cmp_wk", "cmp_wv",
          "w_pool", "pool_scale", "mem_norm_g", "w_mem_kv", "mem_q_g", "mem_k_g", "w_up_nsa", "w_up_pool", "w_up_mem",
          "w_out", "ffn_norm_g", "w_gate_up", "w_down"]


def kernel(x_prompt, x_sample, mem_prompt, cache_nsa_kv, cache_win_kv, state_pool, cache_mem_kv, page_table, **w):
    x_prompt = np.asarray(x_prompt); x_sample = np.asarray(x_sample)
    B, S, _ = x_prompt.shape
    NL = cache_nsa_kv.shape[0]
    NPHYS = cache_nsa_kv.shape[1]
    NPG = page_table.shape[1]
    NT = S // 128
    NTT = NT + 1
    ncore = 8
    consts = make_consts(NT)
    CW = consts.shape[1]
    e64 = (np.arange(64)[:, None] == (np.arange(S)[None, :] // 64) % 64).astype(np.float32)
    nc = build(NL, NT, NPG, NPHYS, CW)
    cn = np.ascontiguousarray(np.asarray(cache_nsa_kv)).reshape(NL, NPHYS * 128, 512)
    in_maps = []
    for c in range(ncore):
        b = c % B
        xin = np.zeros((NTT * 128, D), np.float32)
        xin[:S] = x_prompt[b]
        xin[S:S + NSEQ] = x_sample[NSEQ * c:NSEQ * c + NSEQ, 0]
        m = {"xin": xin, "memin": np.ascontiguousarray(mem_prompt[b]), "cnsa": cn,
             "cwin": np.ascontiguousarray(np.asarray(cache_win_kv)[:, NSEQ * c:NSEQ * c + NSEQ]).reshape(NL, NSEQ, 512, 256),
             "spool": np.ascontiguousarray(np.asarray(state_pool)[:, NSEQ * c:NSEQ * c + NSEQ]),
             "cmem": np.ascontiguousarray(np.asarray(cache_mem_kv)[:, NSEQ * c:NSEQ * c + NSEQ]).reshape(NL, NSEQ, 256, 1024),
             "ptab": np.ascontiguousarray(np.asarray(page_table)[NSEQ * c:NSEQ * c + NSEQ]).reshape(1, NSEQ * NPG).astype(np.int32),
             "consts": consts, "e64": e64}
        for n in WNAMES:
            m[n] = np.ascontiguousarray(np.asarray(w[n], dtype=np.float32))
        in_maps.append(m)
    res = run_bass_kernel_spmd(nc, in_maps, core_ids=list(range(ncore)))
    R = res.results
    DB = x_sample.shape[0]
    y_p = np.stack([R[b]["o_y"][:S] for b in range(B)])
    y_s = np.concatenate([R[c]["o_y"][S:S + NSEQ] for c in range(ncore)])[:, None, :]
    nsa_p = np.stack([R[b]["o_nsa"][:, :S] for b in range(B)], axis=1).reshape(NL, B, S, 4, 2, 64)
    nsa_s = np.concatenate([R[c]["o_nsa"][:, S:S + NSEQ] for c in range(ncore)], axis=1).reshape(NL, DB, 1, 4, 2, 64)
    win_p = np.stack([R[b]["o_winp"] for b in range(B)], axis=1).reshape(NL, B, 512, 2, 2, 64)
    win_s = np.concatenate([R[c]["o_wins"] for c in range(ncore)], axis=1).reshape(NL, DB, 512, 2, 2, 64)
    pool_p = np.stack([R[b]["o_poolp"] for b in range(B)], axis=1)
    pool_s = np.concatenate([R[c]["o_pools"] for c in range(ncore)], axis=1)
    mem_p = np.stack([R[b]["o_memkv"] for b in range(B)], axis=1).reshape(NL, B, 256, 2, 4, 128)
    return (y_p.astype(np.float32), y_s.astype(np.float32), nsa_p, nsa_s, win_p, win_s, pool_p, pool_s, mem_p)
```

```python
import os
import numpy as np
import concourse.bass as bass
import concourse.mybir as mybir
from concourse.alu_op_type import AluOpType as ALU
from concourse.bass_utils import run_bass_kernel_spmd

F32 = mybir.dt.float32
BF16 = mybir.dt.bfloat16
I32 = mybir.dt.int32
AF = mybir.ActivationFunctionType
AX = mybir.AxisListType

ENGS = ("pe", "act", "dve", "pool", "sp")
SEM_LIMIT = 12000
N_DMA_SEMS = 8

D = 1024
PW = 5400
DFF = 2816
NSEQ = 4
EPS = 1e-6
BIGM = 30000.0


class _Stop(Exception):
    pass


KOPS = int(os.environ.get("KOPS", "0"))
_opc = [0]


def _count():
    _opc[0] += 1
    if os.environ.get("KWHO") and _opc[0] == int(os.environ["KWHO"]):
        import traceback
        print("WHO", _opc[0], [f"{f.lineno}:{f.line}" for f in traceback.extract_stack(limit=6)[:-2]])
    if KOPS and _opc[0] >= KOPS:
        raise _Stop()


class Dep:
    __slots__ = ("w", "r")

    def __init__(self):
        self.w = None
        self.r = []


class FW:
    def __init__(self, nc):
        self.nc = nc
        self.ops = {e: [] for e in ENGS}
        self.nops = {e: 0 for e in ENGS}
        self.signaled = {e: set() for e in ENGS}
        self.waited = {e: {} for e in ENGS}
        self.dma_count = {}
        self.relax = False
        self.dbg = {}
        self.dma_rr = {e: 0 for e in ENGS}

    def _need(self, e, opid):
        if opid is None:
            return None
        if opid[0] == "c":
            _, f, k = opid
            if f == e and e == "pe":
                return None
            if self.waited[e].get(("c", f), 0) >= k:
                return None
            self.waited[e][("c", f)] = k
            self.signaled[f].add(k)
            return opid
        _, q, j, m = opid
        if self.waited[e].get(("d", q, j), 0) >= m:
            return None
        self.waited[e][("d", q, j)] = m
        return opid

    def _collect(self, e, reads, writes):
        waits = []
        for d in reads:
            w = self._need(e, d.w)
            if w:
                waits.append(w)
        strict = not self.relax
        for d in writes:
            if strict or not (d.w is not None and d.w[0] == "c" and d.w[1] == e):
                w = self._need(e, d.w)
                if w:
                    waits.append(w)
            for r in d.r:
                if r[0] == "c" and r[1] == e and not strict:
                    continue
                w = self._need(e, r)
                if w:
                    waits.append(w)
        return waits

    def _mark(self, opid, reads, writes):
        for d in reads:
            d.r.append(opid)
            if len(d.r) > 48:
                last = {}
                for o in d.r:
                    key = o[:2] if o[0] == "c" else o[:3]
                    if key not in last or o[-1] > last[key][-1]:
                        last[key] = o
                d.r = list(last.values())
        for d in writes:
            d.w = opid
            d.r = []

    def op(self, e, fn, reads=(), writes=()):
        _count()
        for w in self._collect(e, reads, writes):
            self.ops[e].append(("wait", w))
        self.nops[e] += 1
        k = self.nops[e]
        if os.environ.get("KDBG"):
            import traceback
            st = traceback.extract_stack(limit=6)
            self.dbg[(e, k)] = " <- ".join(f"{f.lineno}" for f in st[:-1])
        self.ops[e].append(("op", fn, k))
        self._mark(("c", e, k), reads, writes)

    def dma(self, q, fn, reads=(), writes=()):
        _count()
        waits = self._collect(q, reads, writes)
        j = self.dma_rr[q]
        self.dma_rr[q] = (j + 1) % N_DMA_SEMS
        m_prev = self.dma_count.get((q, j), 0)
        if m_prev > 0:
            w = self._need(q, ("d", q, j, m_prev))
            if w:
                waits.append(w)
        for w in waits:
            self.ops[q].append(("wait", w))
        m = m_prev + 1
        self.dma_count[(q, j)] = m
        self.ops[q].append(("dma", fn, j))
        self._mark(("d", q, j, m), reads, writes)

    def barrier(self):
        if os.environ.get("KNOBAR"):
            return
        ids = {}
        for e in ("act", "dve", "pool"):
            self.op(e, self.bar_ops[e], reads=[self.bar_dep])
            ids[e] = ("c", e, self.nops[e])
        if self.nops["pe"] > 0:
            ids["pe"] = ("c", "pe", self.nops["pe"])
        dmas = [("d", q, j, m) for (q, j), m in self.dma_count.items()]
        for e in ENGS:
            for f, o in ids.items():
                if f != e:
                    w = self._need(e, o)
                    if w:
                        self.ops[e].append(("wait", w))
            for o in dmas:
                w = self._need(e, o)
                if w:
                    self.ops[e].append(("wait", w))

    def emit(self):
        nc = self.nc
        sigval, sems = {}, {}
        for e in ENGS:
            epoch, cnt = 0, 0
            for k in range(1, self.nops[e] + 1):
                if k in self.signaled[e]:
                    if cnt >= SEM_LIMIT:
                        epoch += 1
                        cnt = 0
                    cnt += 1
                    if (e, epoch) not in sems:
                        sems[(e, epoch)] = nc.alloc_semaphore(f"s_{e}_{epoch}")
                    sigval[(e, k)] = (sems[(e, epoch)], cnt)
                    if os.environ.get("KDBG") and e == os.environ.get("KDBG_E", "pe"):
                        print("SIG", e, epoch, cnt, "op", k, self.dbg.get((e, k)))
        dsems = {key: nc.alloc_semaphore(f"d_{key[0]}_{key[1]}") for key in self.dma_count}

        def run(E, e):
            for item in self.ops[e]:
                if item[0] == "wait":
                    w = item[1]
                    if w[0] == "c":
                        s, v = sigval[(w[1], w[2])]
                        E.wait_ge(s, v)
                    else:
                        E.wait_ge(dsems[(w[1], w[2])], 16 * w[3])
                elif item[0] == "op":
                    ins = item[1](E)
                    if item[2] in self.signaled[e]:
                        ins.then_inc(sigval[(e, item[2])][0], 1)
                else:
                    item[1](E).then_inc(dsems[(e, item[2])], 16)

        with nc.Block() as block:
            @block.sync
            def _(E):
                run(E, "sp")

            @block.gpsimd
            def _(E):
                run(E, "pool")

            @block.vector
            def _(E):
                run(E, "dve")

            @block.scalar
            def _(E):
                run(E, "act")

            @block.tensor
            def _(E):
                run(E, "pe")


C_OFF = {}


def make_consts(NT):
    cols = []
    off = [0]

    def add(name, a):
        a = np.asarray(a, np.float32)
        assert a.shape[0] == 128
        a = a.reshape(128, -1)
        C_OFF[name] = (off[0], a.shape[1])
        off[0] += a.shape[1]
        cols.append(a)

    p = np.arange(128)
    add("ident", np.eye(128))
    add("tric", (p[:, None] <= p[None, :]) * 1.0)
    add("trib", (p[:, None] > p[None, :]) * 1.0)
    add("selall", (p[:, None, None] == np.arange(4)[None, :, None]) * np.ones((1, 1, 128)))
    add("ones", np.ones((128, 1)))
    add("pm", (p[:, None] // 32 == np.arange(4)[None, :]) / 32.0)
    u = (p + 1) // 32
    xx = np.arange(252)
    add("tcmp", (xx[None, :] < 124 + u[:, None]) * 1.0)
    xs = np.arange(126)
    rel = xs[None, :] - 62 - (p[:, None] >= 64)
    add("bsel", np.where((rel == 0) | (rel == -1), 1e9, np.where(rel > 0, -1e9, 0.0)))
    wins = (2, 4, 8, 16)
    acur = np.zeros((128, 4, 128)); aprev = np.zeros((128, 4, 128)); afirst = np.zeros((128, 4, 128))
    for g, w in enumerate(wins):
        for t in range(128):
            for s_ in range(t - w + 1, t + 1):
                if s_ >= 0:
                    acur[s_, g, t] += 1.0 / w
                    afirst[s_, g, t] += 1.0 / min(t + 1, w)
                else:
                    aprev[128 + s_, g, t] += 1.0 / w
            acur[t, g, t] -= 1.0
            afirst[t, g, t] -= 1.0
    add("acur", acur); add("aprev", aprev); add("afirst", afirst)
    ast = np.zeros((128, 4))
    for g, w in enumerate(wins):
        for j in range(15):
            if j >= 15 - (w - 1):
                ast[j, g] = 1.0 / w
    add("ast", ast)
    anew = np.zeros((128, 4, 4))
    for s_ in range(4):
        for g, w in enumerate(wins):
            anew[s_, s_, g] = 1.0 / w - 1.0
    add("anew", anew)
    add("rowsel", (p[:, None] == np.arange(4)[None, :]) * 1.0)
    add("g8", ((p[:, None] // 4) == np.arange(2)[None, :]) * (p[:, None] < 8))
    inde = np.zeros((128, 2, 2, 128))
    for kv in range(2):
        for par in range(2):
            inde[kv, kv, par, :] = (np.arange(128) // 64 == par)
    add("inde", inde)
    add("sr", np.ones((128, 1, 1)) * (np.arange(4)[None, :, None] == np.arange(128)[None, None, :]))
    add("wmask", (p[:, None] != 0) * 1.0)
    add("npad", np.maximum(0.0, 511.0 - (128.0 * np.arange(4)[None, :] + p[:, None])))
    add("iota", p[:, None] * 1.0)
    return np.concatenate(cols, axis=1)


def build(NL, NT, NPG, NPHYS, CW):
    S = NT * 128
    NTT = NT + 1
    NB = 4 * NT
    NBS = 4 * NPG
    CB = min(128, NBS)
    NCK = NBS // CB
    A1W = 4608
    nc = bass.Bass("TRN2", target_bir_lowering=False)
    fw = FW(nc)

    def din(name, shape, dt=F32):
        return nc.dram_tensor(name, shape, dt, kind="ExternalInput").ap()

    def dout(name, shape):
        return nc.dram_tensor(name, shape, F32, kind="ExternalOutput").ap()

    xin = din("xin", [NTT * 128, D]); memin = din("memin", [256, D])
    cnsa = din("cnsa", [NL, NPHYS * 128, 512]); cwin = din("cwin", [NL, NSEQ, 512, 256])
    spool = din("spool", [NL, NSEQ, 15, 512]); cmem = din("cmem", [NL, NSEQ, 256, 1024])
    ptab = din("ptab", [1, NSEQ * NPG], I32)
    consts = din("consts", [128, CW]); e64 = din("e64", [64, S])
    w_attn_g = din("attn_norm_g", [NL, D]); w_in = din("w_in", [NL, D, PW])
    w_qg = din("nsa_q_g", [NL, 64]); w_kcg = din("nsa_kc_g", [NL, 64]); w_ksg = din("nsa_ks_g", [NL, 64]); w_kwg = din("nsa_kw_g", [NL, 64])
    w_pek = din("cmp_pe_k", [NL, 32, 64]); w_pev = din("cmp_pe_v", [NL, 32, 64])
    w_cwk = din("cmp_wk", [NL, 64, 64]); w_cwv = din("cmp_wv", [NL, 64, 64])
    w_pool = din("w_pool", [NL, 4, 128, 128]); w_psc = din("pool_scale", [NL, 512])
    w_memg = din("mem_norm_g", [NL, D]); w_memkv = din("w_mem_kv", [NL, D, 1024])
    w_mqg = din("mem_q_g", [NL, 128]); w_mkg = din("mem_k_g", [NL, 128])
    w_upn = din("w_up_nsa", [NL, 512, D]); w_upp = din("w_up_pool", [NL, 512, D]); w_upm = din("w_up_mem", [NL, 512, D])
    w_out = din("w_out", [NL, D, D]); w_ffng = din("ffn_norm_g", [NL, D])
    w_gu = din("w_gate_up", [NL, D, 2 * DFF]); w_dn = din("w_down", [NL, DFF, D])

    o_y = dout("o_y", [NTT * 128, D]); o_nsa = dout("o_nsa", [NL, NTT * 128, 512])
    o_winp = dout("o_winp", [NL, 512, 256]); o_wins = dout("o_wins", [NL, NSEQ, 512, 256])
    o_poolp = dout("o_poolp", [NL, 15, 512]); o_pools = dout("o_pools", [NL, NSEQ, 15, 512])
    o_memkv = dout("o_memkv", [NL, 256, 1024])
    outdeps = []

    xs = nc.dram_tensor("xs", [NTT * 128, D], F32, kind="Internal").ap()
    a1s = nc.dram_tensor("a1s", [NTT, 128, A1W], BF16, kind="Internal").ap()
    xs_d = [Dep() for _ in range(NTT)]
    a1s_d = [Dep() for _ in range(NTT)]
    a1s_d2 = [Dep() for _ in range(NTT)]

    class Tl:
        def __init__(self, h, n=1):
            self.h = h
            self.d = Dep()
            self.ds = [Dep() for _ in range(n)]

        def __getitem__(self, k):
            return self.h[k]

    def op(e, fn, R=(), W=()):
        fw.op(e, fn, [t.d if isinstance(t, Tl) else t for t in R], [t.d if isinstance(t, Tl) else t for t in W])

    def dma(q, out, in_, R=(), W=()):
        fw.dma(q, lambda E: E.dma_start(out=out, in_=in_), [t.d if isinstance(t, Tl) else t for t in R],
               [t.d if isinstance(t, Tl) else t for t in W])

    def dma_nc(q, out, in_, R=(), W=()):
        fw.dma(q, lambda E: E.dma_start(out=out, in_=in_, allow_slow_non_contiguous=True), [t.d if isinstance(t, Tl) else t for t in R],
               [t.d if isinstance(t, Tl) else t for t in W])

    def outdma(q, out, in_, R=()):
        d = Dep()
        outdeps.append(d)
        dma(q, out, in_, R=R, W=[d])

    def mm(ps, lhsT, rhs, start, stop, R, W):
        if os.environ.get("KNOF32") and lhsT.dtype == F32:
            return
        op("pe", lambda E: E.matmul(ps, lhsT=lhsT, rhs=rhs, start=start, stop=stop), R, W)

    def tr(ps, in_, idn, R, W):
        op("pe", lambda E: E.transpose(out=ps, in_=in_, identity=idn), R, W)

    def act(e, out, in_, func=AF.Copy, scale=None, accum=None, R=(), W=()):
        kw = {}
        if scale is not None:
            kw["scale"] = scale
        if accum is not None:
            kw["accum_out"] = accum
        op("act", lambda E: E.activation(out=out, in_=in_, func=func, **kw), R, W)

    def cp(e, out, in_, R, W):
        if e == "act":
            op("act", lambda E: E.activation(out=out, in_=in_, func=AF.Copy), R, W)
        elif e == "dve" and out.dtype == in_.dtype and out.dtype == BF16 and not os.environ.get("KCPRAW"):
            op(e, lambda E: E.tensor_scalar(out=out, in0=in_, scalar1=1.0, scalar2=None, op0=ALU.mult), R, W)
        else:
            op(e, lambda E: E.tensor_copy(out=out, in_=in_), R, W)

    def ts(e, out, in0, s1, s2, op0, op1=None, R=(), W=()):
        if op1 is None:
            op(e, lambda E: E.tensor_scalar(out=out, in0=in0, scalar1=s1, scalar2=None, op0=op0), R, W)
        else:
            op(e, lambda E: E.tensor_scalar(out=out, in0=in0, scalar1=s1, scalar2=s2, op0=op0, op1=op1), R, W)

    def tt(e, out, in0, in1, o, R, W):
        op(e, lambda E: E.tensor_tensor(out=out, in0=in0, in1=in1, op=o), R, W)

    def stt(out, in0, sc, in1, op0, op1, R, W):
        op("dve", lambda E: E.scalar_tensor_tensor(out=out, in0=in0, scalar=sc, in1=in1, op0=op0, op1=op1), R, W)

    def red(out, in_, R, W, o=ALU.add):
        op("dve", lambda E: E.tensor_reduce(out=out, in_=in_, axis=AX.X, op=o), R, W)

    import contextlib

    @contextlib.contextmanager
    def relaxed():
        old = fw.relax
        fw.relax = True
        try:
            yield
        finally:
            fw.relax = old

    def memset(e, ap, v, W, st=False):
        op(e, lambda E: E.memset(ap, v), [], W)

    def recip(out, in_, R, W):
        op("dve", lambda E: E.reciprocal(out=out, in_=in_), R, W)

    def vmax(out, in_, R, W):
        op("dve", lambda E: E.max(out=out, in_=in_), R, W)

    def mrep(out, rep, vals, imm, R, W):
        op("dve", lambda E: E.match_replace(out=out, in_to_replace=rep, in_values=vals, imm_value=imm), R, W)

    def ttr(out, in0, in1, accum, R, W):
        op("dve", lambda E: E.scalar_tensor_tensor(out=out, in0=in0, scalar=1.0, in1=in1, op0=ALU.mult, op1=ALU.mult, accum_out=accum), R, W)

    def rsqrt_chain(v, R_extra=()):
        act("act", v[:], v[:], AF.Sqrt, R=[v], W=[v])
        recip(v[:], v[:], [v], [v])

    from contextlib import ExitStack
    root = ExitStack()

    _uid = [0]

    def sb(stack, name, shape, dt=F32, n=1):
        _uid[0] += 1
        return Tl(stack.enter_context(nc.sbuf_tensor(f"{name}_{_uid[0]}", shape, dt)), n)

    def pst(stack, name, shape, dt=F32):
        _uid[0] += 1
        return Tl(stack.enter_context(nc.psum_tensor(f"{name}_{_uid[0]}", shape, dt)))

    cf = sb(root, "cf", [128, CW])
    CBW = 897
    cb = sb(root, "cb", [128, CBW], BF16)
    dma("sp", cf[:], consts[:, :], W=[cf])
    cp("dve", cb[:], cf[:, 0:CBW], [cf], [cb])

    def CF(name, sub=None):
        o, n = C_OFF[name]
        return cf.h[:, o:o + n]

    def CB_(name):
        o, n = C_OFF[name]
        return cb.h[:, o:o + n]

    identf = CF("ident"); identb = CB_("ident")
    pmf = CF("pm")
    tric_b = CB_("tric"); trib_b = CB_("trib")
    tcmp = CF("tcmp"); bsel = CF("bsel")
    acur = CF("acur").rearrange("p (g t) -> p g t", g=4); aprev = CF("aprev").rearrange("p (g t) -> p g t", g=4)
    afirst = CF("afirst").rearrange("p (g t) -> p g t", g=4)
    ast = CF("ast"); anew = CF("anew").rearrange("p (s g) -> p s g", s=4)
    selall_b = CB_("selall").rearrange("p (s m) -> p s m", s=4)
    rowsel = CF("rowsel"); g8 = CF("g8")
    inde = CF("inde").rearrange("p (a b k) -> p a b k", a=2, b=2)
    sr_f = CF("sr").rearrange("p (s m) -> p s m", s=4)
    npad = CF("npad")
    wmask = CF("wmask"); iota = CF("iota"); onesf = CF("ones"); onesb = CB_("ones")

    pidx = sb(root, "pidx", [128, NL, NSEQ * NPG], I32)
    cnsa_flat = cnsa.rearrange("l r c -> (l r) c")

    bscr = sb(root, "bscr", [128, 4])
    memset("dve", bscr[:], 0.0, [bscr])
    _b0, _b1, _b2 = bscr[0:1, 0:1], bscr[0:1, 1:2], bscr[0:1, 2:3]
    fw.bar_dep = bscr.d
    fw.bar_ops = {"pe": lambda E: E.nop(), "sp": lambda E: E.nop(),
                  "act": lambda E: E.activation(out=_b0, in_=_b0, func=AF.Copy),
                  "dve": lambda E: E.tensor_copy(out=_b1, in_=_b1),
                  "pool": lambda E: E.tensor_copy(out=_b2, in_=_b2)}
    zerob = sb(root, "zerob", [128, 260], BF16)
    memset("pool", zerob[:], 0.0, [zerob])
    tmp0 = ExitStack()
    pti = sb(tmp0, "pti", [128, NSEQ * NPG], I32)
    ptf = sb(tmp0, "ptf", [128, NSEQ * NPG], F32)
    ptf2 = sb(tmp0, "ptf2", [128, NSEQ * NPG], F32)
    dma("sp", pti[:], ptab[0:1, :].to_broadcast([128, NSEQ * NPG]), W=[pti])
    cp("dve", ptf[:], pti[:], [pti], [ptf])
    ts("dve", ptf[:], ptf[:], 128.0, iota[:, 0:1], ALU.mult, ALU.add, R=[ptf, cf], W=[ptf])
    for l_ in range(NL):
        ts("dve", ptf2[:], ptf[:], float(l_ * NPHYS * 128), None, ALU.add, R=[ptf], W=[ptf2])
        cp("dve", pidx[:, l_, :], ptf2[:], [ptf2], [pidx])
    fw.barrier()
    tmp0.close()
    rr = [0]

    def alt(a="act", b="dve"):
        rr[0] += 1
        return a if rr[0] % 2 else b

    import os
    _stop = float(os.environ.get("KSTOP", "99"))

    KCUT = int(os.environ.get("KCUT", "99"))

    def chk(n):
        if n >= _stop:
            raise _Stop()

    try:
      chk(1)
      for l in range(NL):
          xsrc = xin if l == 0 else xs
          lays = ExitStack()
          lay = ExitStack()
          gq = sb(lays, "gq", [128, 64]); gkc = sb(lays, "gkc", [128, 64]); gks = sb(lays, "gks", [128, 64]); gkw = sb(lays, "gkw", [128, 64])
          gmq = sb(lays, "gmq", [128, 128]); gmk = sb(lays, "gmk", [128, 128])
          pemT = sb(lays, "pemT", [128, 2])
          wkbd = sb(lays, "wkbd", [128, 2, 128])
          nsag = sb(lays, "nsag", [128, NTT, 24])
          sampkv = sb(lays, "sampkv", [128, 512])
          kselT = sb(lay, "kselT", [128, 2, S], BF16, NT)
          vsel = sb(lay, "vsel", [128, NT, 2, 65], BF16, NT)
          kwinT = sb(lay, "kwinT", [64, 2, S], BF16, NT)
          vwin = sb(lay, "vwin", [128, NT, 2, 65], BF16, NT)
          kcT2 = sb(lay, "kcT2", [128, 2, 128], BF16)
          vcaug = sb(lay, "vcaug", [128, 2, 65], BF16)
          mkT = sb(lay, "mkT", [128, 4, 256], BF16)
          mva = sb(lay, "mva", [128, 2, 4, 129], BF16)
          for t_, src in ((gq, w_qg), (gkc, w_kcg), (gks, w_ksg), (gkw, w_kwg)):
              dma("sp", t_[:], src[l:l + 1, :].to_broadcast([128, 64]), W=[t_])
          for t_, src in ((gmq, w_mqg), (gmk, w_mkg)):
              dma("sp", t_[:], src[l:l + 1, :].to_broadcast([128, 128]), W=[t_])
          ts("dve", gq[:], gq[:], 0.125, None, ALU.mult, R=[gq], W=[gq])
          ts("dve", gmq[:], gmq[:], 128 ** -0.5, None, ALU.mult, R=[gmq], W=[gmq])
          dma("pool", kselT[64:128, 0, :], e64[:, :], W=kselT.ds + [kselT])
          dma("pool", kselT[64:128, 1, :], e64[:, :], W=kselT.ds + [kselT])
          memset("pool", vsel[:, :, :, 64:65], 1.0, vsel.ds + [vsel])
          memset("pool", vwin[:, :, :, 64:65], 1.0, vwin.ds + [vwin])
          memset("pool", vcaug[:, :, 64:65], 1.0, [vcaug])
          memset("pool", mva[:, :, :, 128:129], 1.0, [mva])
          memset("pool", wkbd[:], 0.0, [wkbd])
          for a_, src in ((0, w_cwk), (1, w_cwv)):
              dma("sp", wkbd[0:64, a_, 0:64], src[l], W=[wkbd])
              dma("sp", wkbd[64:128, a_, 64:128], src[l], W=[wkbd])

          p1 = ExitStack()
          win_sb = sb(p1, "win_sb", [128, 8, 2328], BF16)
          gT = sb(p1, "gT", [128, 8])
          wpl = sb(p1, "wpl", [128, 4, 128], BF16)
          psc = sb(p1, "psc", [128, 4])
          pe2 = sb(p1, "pe2", [32, 2, 128])
          xt_l = [sb(p1, "xt", [128, D]) for _ in range(2)]; xb_l = [sb(p1, "xb", [128, D], BF16) for _ in range(2)]
          hT_l = [sb(p1, "hT", [128, 8, 128], BF16) for _ in range(2)]
          junk_l = [sb(p1, "junk", [128, D], BF16) for _ in range(2)]
          z_l = [sb(p1, "z", [128, 2328]) for _ in range(2)]; a1_l = [sb(p1, "a1", [128, 1536], BF16) for _ in range(2)]
          sq_l = [sb(p1, "sq", [128, 1280]) for _ in range(2)]; rs_l = [sb(p1, "rs", [128, 16]) for _ in range(2)]
          ss_l = [sb(p1, "ss", [128, 1]) for _ in range(2)]
          xt, xb, hT, junk, z, a1, sq, rs, ss = xt_l[0], xb_l[0], hT_l[0], junk_l[0], z_l[0], a1_l[0], sq_l[0], rs_l[0], ss_l[0]
          ksb = sb(p1, "ksb", [128, 256], BF16)
          uprev = sb(p1, "uprev", [128, 512]); dTb = sb(p1, "dTb", [128, 4, 128], BF16)
          stt_ = sb(p1, "stt_", [15, NSEQ, 512])
          pooled = sb(p1, "pooled", [128, 2, 128]); kcf = sb(p1, "kcf", [128, 256]); kcb = sb(p1, "kcb", [128, 2, 128], BF16)
          ptb = pst(p1, "ptb", [128, 8, 128], BF16)
          pz = [pst(p1, f"pz{i}", [128, 512]) for i in range(3)]
          pcmp = pst(p1, "pcmp", [128, 2, 128])
          pd = pst(p1, "pd", [128, 4, 128])
          pk = pst(p1, "pk", [128, 4, 128], BF16)
          pmisc = pst(p1, "pmisc", [128, 512])

          for c in range(8):
              dma("pool", win_sb[:, c, :], w_in[l, c * 128:(c + 1) * 128, 0:2328], W=[win_sb])
          dma_nc("sp", gT[:], w_attn_g[l].rearrange("(c p) -> p c", p=128), W=[gT])
          dma("pool", wpl[:], w_pool[l].rearrange("g c e -> c g e"), W=[wpl])
          dma_nc("sp", psc[:], w_psc[l].rearrange("(g e) -> e g", e=128), W=[psc])
          for a_, src in ((0, w_pek), (1, w_pev)):
              dma("sp", pe2[:, a_, 0:64], src[l], W=[pe2])
              dma("sp", pe2[:, a_, 64:128], src[l], W=[pe2])
          dma("sp", stt_[:], spool[l].rearrange("s j c -> j s c"), W=[stt_])
          for a_ in range(2):
              mm(pmisc[:, a_:a_ + 1], pe2[:, a_, :], pmf[0:32, 0:1], True, True, [pe2, cf], [pmisc])
          cp("dve", pemT[:], pmisc[:, 0:2], [pmisc], [pemT])

          def norm_transpose(xt_, gT_, hT_, rstd_):
              act("act", junk[:], xt_[:], AF.Square, accum=rstd_[:], R=[xt_], W=[junk, rstd_])
              ts("dve", rstd_[:], rstd_[:], 1.0 / D, EPS, ALU.mult, ALU.add, R=[rstd_], W=[rstd_])
              rsqrt_chain(rstd_)
              cp("dve", xb[:], xt_[:], [xt_], [xb])
              for c in range(8):
                  tr(ptb[:, c, :], xb[:, c * 128:(c + 1) * 128], identb, [xb, cb], [ptb])
              for c in range(8):
                  with relaxed():
                      e = alt()
                      if e == "act":
                          act("act", hT_[:, c, :], ptb[:, c, :], AF.Copy, scale=gT_[:, c:c + 1], R=[ptb, gT_], W=[hT_])
                      else:
                          ts("dve", hT_[:, c, :], ptb[:, c, :], gT_[:, c:c + 1], None, ALU.mult, R=[ptb, gT_], W=[hT_])

          chk(1.5)
          zi = [0]
          for i in range(NTT):
              samp = (i == NT)
              _b = i % 2
              xt, xb, hT, junk, z, a1, sq, rs, ss = xt_l[_b], xb_l[_b], hT_l[_b], junk_l[_b], z_l[_b], a1_l[_b], sq_l[_b], rs_l[_b], ss_l[_b]
              if i == 0:
                  dma("sp", xt[:], xsrc[i * 128:(i + 1) * 128, :], R=[xs_d[i]], W=[xt])
              if i + 1 < NTT:
                  dma("sp", xt_l[1 - _b][:], xsrc[(i + 1) * 128:(i + 2) * 128, :], R=[xs_d[i + 1]], W=[xt_l[1 - _b]])
              norm_transpose(xt, gT, hT, ss)
              if KCUT <= 1:
                  chk(1.6 + 0.01 * i); continue
              col = 0
              while col < 2328:
                  wdt = min(512, 2328 - col)
                  p_ = pz[zi[0] % 3]; zi[0] += 1
                  for c in range(8):
                      mm(p_[:, 0:wdt], hT[:, c, :], win_sb[:, c, col:col + wdt], c == 0, c == 7, [hT, win_sb], [p_])
                  if col < 2328:
                      with relaxed():
                          e = alt()
                          if e == "act":
                              act("act", z[:, col:col + wdt], p_[:, 0:wdt], AF.Copy, scale=ss[:, 0:1], R=[p_, ss], W=[z])
                          else:
                              ts("dve", z[:, col:col + wdt], p_[:, 0:wdt], ss[:, 0:1], None, ALU.mult, R=[p_, ss], W=[z])
                  else:
                      a0 = 1536 + (col - 2328)
                      act("act", a1[:, a0:a0 + wdt], p_[:, 0:wdt], AF.Sigmoid, scale=ss[:, 0:1], R=[p_, ss], W=[a1])
                  col += wdt
              if KCUT <= 2:
                  chk(1.6 + 0.01 * i); continue
              with relaxed():
                  tt("dve", sq[:, 0:512], z[:, 0:512], z[:, 0:512], ALU.mult, [z], [sq])
                  tt("dve", sq[:, 512:640], z[:, 768:896], z[:, 768:896], ALU.mult, [z], [sq])
                  tt("dve", sq[:, 640:768], z[:, 1024:1152], z[:, 1024:1152], ALU.mult, [z], [sq])
                  tt("dve", sq[:, 768:1280], z[:, 1816:2328], z[:, 1816:2328], ALU.mult, [z], [sq])
              red(rs[:, 0:12], sq[:, 0:768].rearrange("p (h d) -> p h d", d=64), [sq], [rs])
              red(rs[:, 12:16], sq[:, 768:1280].rearrange("p (h d) -> p h d", d=128), [sq], [rs])
              ts("dve", rs[:, 0:12], rs[:, 0:12], 1.0 / 64, EPS, ALU.mult, ALU.add, R=[rs], W=[rs])
              ts("dve", rs[:, 12:16], rs[:, 12:16], 1.0 / 128, EPS, ALU.mult, ALU.add, R=[rs], W=[rs])
              rsqrt_chain(rs)
              for h in range(8):
                  with relaxed():
                      stt(a1[:, h * 64:(h + 1) * 64], z[:, h * 64:(h + 1) * 64], rs[:, h:h + 1], gq[:], ALU.mult, ALU.mult, [z, rs, gq], [a1])
              for kv in range(2):
                  stt(z[:, 768 + kv * 64:832 + kv * 64], z[:, 768 + kv * 64:832 + kv * 64], rs[:, 8 + kv:9 + kv], gks[:], ALU.mult, ALU.mult, [z, rs, gks], [z])
                  stt(z[:, 1024 + kv * 64:1088 + kv * 64], z[:, 1024 + kv * 64:1088 + kv * 64], rs[:, 10 + kv:11 + kv], gkw[:], ALU.mult, ALU.mult, [z, rs, gkw], [z])
              for h in range(4):
                  with relaxed():
                      stt(a1[:, 512 + h * 128:640 + h * 128], z[:, 1816 + h * 128:1944 + h * 128], rs[:, 12 + h:13 + h], gmq[:], ALU.mult, ALU.mult, [z, rs, gmq], [a1])
              act("act", nsag[:, i, :], z[:, 1280:1304], AF.Sigmoid, R=[z], W=[nsag])
              if KCUT <= 3:
                  chk(1.6 + 0.01 * i); continue
              outdma("sp", o_nsa[l, i * 128:(i + 1) * 128, :], z[:, 512:1024], R=[z])
              if not samp:
                  if i >= NT - 4:
                      outdma("sp", o_winp[l, (i - (NT - 4)) * 128:(i - (NT - 4) + 1) * 128, :], z[:, 1024:1280], R=[z])
                  if i == NT - 1:
                      outdma("sp", o_poolp[l, :, :], z[113:128, 1304:1816], R=[z])
                  if KCUT <= 4:
                      chk(1.6 + 0.01 * i); continue
                  K5 = int(os.environ.get("K5", "9"))
                  cp("act", vsel[:, i, :, 0:64], z[:, 896:1024].rearrange("p (k d) -> p k d", k=2), [z], [vsel.ds[i]])
                  cp("act", vwin[:, i, :, 0:64], z[:, 1152:1280].rearrange("p (k d) -> p k d", k=2), [z], [vwin.ds[i]])
                  if K5 >= 2:
                      cp("act", ksb[:, 0:128], z[:, 768:896], [z], [ksb])
                      cp("act", ksb[:, 128:256], z[:, 1024:1152], [z], [ksb])
                  if K5 >= 3:
                      for q_ in range(4):
                          tr(pk[0:64, q_, :], ksb[:, q_ * 64:(q_ + 1) * 64], identb, [ksb, cb], [pk])
                  if K5 >= 4:
                      cp("act", kselT[0:64, :, i * 128:(i + 1) * 128], pk[0:64, 0:2, :], [pk], [kselT.ds[i]])
                  if K5 >= 5:
                      cp("act", kwinT[0:64, :, i * 128:(i + 1) * 128], pk[0:64, 2:4, :], [pk], [kwinT.ds[i]])
                  if KCUT <= 5:
                      chk(1.6 + 0.01 * i); continue
                  mm(pcmp[:, 0, 4 * i:4 * i + 4], z[:, 512:640], pmf, True, True, [z, cf], [pcmp])
                  mm(pcmp[:, 1, 4 * i:4 * i + 4], z[:, 640:768], pmf, True, True, [z, cf], [pcmp])
                  for g in range(4):
                      if i == 0:
                          mm(pd[:, g, :], z[:, 1304 + g * 128:1432 + g * 128], afirst[:, g, :], True, True, [z, cf], [pd])
                      else:
                          mm(pd[:, g, :], z[:, 1304 + g * 128:1432 + g * 128], acur[:, g, :], True, False, [z, cf], [pd])
                          mm(pd[:, g, :], uprev[:, g * 128:(g + 1) * 128], aprev[:, g, :], False, True, [uprev, cf], [pd])
                  cp("act", dTb[:], pd[:], [pd], [dTb])
                  for g in range(4):
                      mm(pd[:, g, :], wpl[:, g, :], dTb[:, g, :], True, True, [wpl, dTb], [pd])
                  for g in range(4):
                      ts("dve", a1[:, 1024 + g * 128:1152 + g * 128], pd[:, g, :], psc[:, g:g + 1], None, ALU.mult, R=[pd, psc], W=[a1])
                  cp("act", uprev[:], z[:, 1304:1816], [z], [uprev])
              else:
                  cp("dve", sampkv[:], z[:, 768:1280], [z], [sampkv])
                  outdma("sp", o_wins[l, :, 511, :], z[0:NSEQ, 1024:1280], R=[z])
                  outdma("sp", o_pools[l, :, 14, :], z[0:NSEQ, 1304:1816], R=[z])
                  outdma("sp", o_wins[l, :, 0:511, :], cwin[l, :, 1:512, :])
                  outdma("sp", o_pools[l, :, 0:14, :], spool[l, :, 1:15, :])
                  for g in range(4):
                      for s_ in range(NSEQ):
                          mm(pd[:, g, s_:s_ + 1], stt_[0:15, s_, g * 128:(g + 1) * 128], ast[0:15, g:g + 1], True, False, [stt_, cf], [pd])
                          mm(pd[:, g, s_:s_ + 1], z[0:NSEQ, 1304 + g * 128:1432 + g * 128], anew[0:NSEQ, s_, g:g + 1], False, True, [z, cf], [pd])
                  cp("act", dTb[:, :, 0:NSEQ], pd[:, :, 0:NSEQ], [pd], [dTb])
                  for g in range(4):
                      mm(pd[:, g, 0:NSEQ], wpl[:, g, :], dTb[:, g, 0:NSEQ], True, True, [wpl, dTb], [pd])
                  for g in range(4):
                      ts("dve", a1[:, 1024 + g * 128:1024 + g * 128 + NSEQ], pd[:, g, 0:NSEQ], psc[:, g:g + 1], None, ALU.mult, R=[pd, psc], W=[a1])
              dma("sp", a1s[i][:, 0:1536], a1[:], R=[a1], W=[a1s_d[i]])
              chk(1.6 + 0.01 * i)

          chk(2)
          for a_ in range(2):
              ts("dve", pooled[:, a_, 0:NB], pcmp[:, a_, 0:NB], pemT[:, a_:a_ + 1], None, ALU.add, R=[pcmp, pemT], W=[pooled])
          mm(pmisc[0:NB, 0:128], pooled[:, 0, 0:NB], wkbd[:, 0, :], True, True, [pooled, wkbd], [pmisc])
          mm(pmisc[0:NB, 128:256], pooled[:, 1, 0:NB], wkbd[:, 1, :], True, True, [pooled, wkbd], [pmisc])
          cp("act", kcf[0:NB, :], pmisc[0:NB, 0:256], [pmisc], [kcf])
          tt("dve", sq[0:NB, 0:128], kcf[0:NB, 0:128], kcf[0:NB, 0:128], ALU.mult, [kcf], [sq])
          red(rs[0:NB, 0:2], sq[0:NB, 0:128].rearrange("p (h d) -> p h d", d=64), [sq], [rs])
          ts("dve", rs[0:NB, 0:2], rs[0:NB, 0:2], 1.0 / 64, EPS, ALU.mult, ALU.add, R=[rs], W=[rs])
          act("act", rs[0:NB, 0:2], rs[0:NB, 0:2], AF.Sqrt, R=[rs], W=[rs])
          recip(rs[0:NB, 0:2], rs[0:NB, 0:2], [rs], [rs])
          memset("pool", kcb[:], 0.0, [kcb])
          for kv in range(2):
              for dup in range(2):
                  stt(kcb[0:NB, kv, dup * 64:(dup + 1) * 64], kcf[0:NB, kv * 64:(kv + 1) * 64], rs[0:NB, kv:kv + 1], gkc[0:NB, :], ALU.mult, ALU.mult, [kcf, rs, gkc], [kcb])
          for kv in range(2):
              tr(pk[:, kv, 0:NB], kcb[0:NB, kv, :], identb[0:NB, 0:NB], [kcb, cb], [pk])
          memset("pool", kcT2[:], 0.0, [kcT2])
          cp("act", kcT2[:, :, 0:NB], pk[:, 0:2, 0:NB], [pk], [kcT2])
          memset("pool", vcaug[:, :, 0:64], 0.0, [vcaug])
          cp("dve", vcaug[0:NB, :, 0:64], kcf[0:NB, 128:256].rearrange("p (k d) -> p k d", k=2), [kcf], [vcaug])

          fw.barrier()
          p1.close()

          chk(3)
          p1b = ExitStack()
          win_sb = sb(p1b, "win_sb2", [128, 8, 3072], BF16)
          gT = sb(p1b, "gT2", [128, 8])
          xt_l = [sb(p1b, "xt1b", [128, D]) for _ in range(2)]; xb_l = [sb(p1b, "xb1b", [128, D], BF16) for _ in range(2)]
          hT_l = [sb(p1b, "hT1b", [128, 8, 128], BF16) for _ in range(2)]
          junk_l = [sb(p1b, "junk1b", [128, D], BF16) for _ in range(2)]; ss_l = [sb(p1b, "ss1b", [128, 1]) for _ in range(2)]
          a1m = [sb(p1b, f"a1m{k}", [128, 3072], BF16) for k in range(2)]
          ptb = pst(p1b, "ptb1b", [128, 8, 128], BF16)
          pz = [pst(p1b, f"pz1b{i}", [128, 512]) for i in range(3)]
          for c in range(8):
              dma("pool", win_sb[:, c, :], w_in[l, c * 128:(c + 1) * 128, 2328:PW], W=[win_sb])
          dma_nc("sp", gT[:], w_attn_g[l].rearrange("(c p) -> p c", p=128), W=[gT])
          for i in range(NTT):
              _b = i % 2
              xt, xb, hT, junk, ss = xt_l[_b], xb_l[_b], hT_l[_b], junk_l[_b], ss_l[_b]
              if i == 0:
                  dma("sp", xt[:], xsrc[i * 128:(i + 1) * 128, :], R=[xs_d[i]], W=[xt])
              if i + 1 < NTT:
                  dma("sp", xt_l[1 - _b][:], xsrc[(i + 1) * 128:(i + 2) * 128, :], R=[xs_d[i + 1]], W=[xt_l[1 - _b]])
              norm_transpose(xt, gT, hT, ss)
              am = a1m[i % 2]
              for n in range(6):
                  p_ = pz[zi[0] % 3]; zi[0] += 1
                  for c in range(8):
                      mm(p_[:], hT[:, c, :], win_sb[:, c, n * 512:(n + 1) * 512], c == 0, c == 7, [hT, win_sb], [p_])
                  with relaxed():
                      act("act", am[:, n * 512:(n + 1) * 512], p_[:], AF.Sigmoid, scale=ss[:, 0:1], R=[p_, ss], W=[am])
              dma("sp", a1s[i][:, 1536:A1W], am[:], R=[am], W=[a1s_d2[i]])
          fw.barrier()
          p1b.close()

          chk(4)
          p1 = ExitStack()
          xb = sb(p1, "xbm", [128, D], BF16); hT = sb(p1, "hTm", [128, 8, 128], BF16)
          junk = sb(p1, "junkm", [128, D], BF16); ss = sb(p1, "ssm", [128, 1])
          sq = sb(p1, "sqm", [128, 512]); rs = sb(p1, "rsm", [128, 4])
          memx = sb(p1, "memx", [128, D]); wmem = sb(p1, "wmem", [128, 8, 1024], BF16); gmT = sb(p1, "gmT", [128, 8])
          mkvf = sb(p1, "mkvf", [128, 1024]); mkb = sb(p1, "mkb", [128, 512], BF16)
          ptb = pst(p1, "ptbm", [128, 8, 128], BF16)
          pz = [pst(p1, f"pzm{i}", [128, 512]) for i in range(3)]
          pk = pst(p1, "pkm", [128, 4, 128], BF16)
          for c in range(8):
              dma("pool", wmem[:, c, :], w_memkv[l, c * 128:(c + 1) * 128, :], W=[wmem])
          dma_nc("sp", gmT[:], w_memg[l].rearrange("(c p) -> p c", p=128), W=[gmT])
          for mt in range(2):
              dma("sp", memx[:], memin[mt * 128:(mt + 1) * 128, :], W=[memx])
              norm_transpose(memx, gmT, hT, ss)
              for n in range(2):
                  p_ = pz[zi[0] % 3]; zi[0] += 1
                  for c in range(8):
                      mm(p_[:], hT[:, c, :], wmem[:, c, n * 512:(n + 1) * 512], c == 0, c == 7, [hT, wmem], [p_])
                  act("act", mkvf[:, n * 512:(n + 1) * 512], p_[:], AF.Copy, scale=ss[:, 0:1], R=[p_, ss], W=[mkvf])
              tt("dve", sq[:, 0:512], mkvf[:, 0:512], mkvf[:, 0:512], ALU.mult, [mkvf], [sq])
              red(rs[:, 0:4], sq[:, 0:512].rearrange("p (h d) -> p h d", d=128), [sq], [rs])
              ts("dve", rs[:, 0:4], rs[:, 0:4], 1.0 / 128, EPS, ALU.mult, ALU.add, R=[rs], W=[rs])
              rsqrt_chain(rs)
              for h in range(4):
                  stt(mkvf[:, h * 128:(h + 1) * 128], mkvf[:, h * 128:(h + 1) * 128], rs[:, h:h + 1], gmk[:], ALU.mult, ALU.mult, [mkvf, rs, gmk], [mkvf])
              outdma("sp", o_memkv[l, mt * 128:(mt + 1) * 128, :], mkvf[:], R=[mkvf])
              cp("act", mkb[:], mkvf[:, 0:512], [mkvf], [mkb])
              for h in range(4):
                  tr(pk[:, h, :], mkb[:, h * 128:(h + 1) * 128], identb, [mkb, cb], [pk])
              cp("act", mkT[:, :, mt * 128:(mt + 1) * 128], pk[:], [pk], [mkT])
              cp("dve", mva[:, mt, :, 0:128], mkvf[:, 512:1024].rearrange("p (h d) -> p h d", h=4), [mkvf], [mva])
          fw.barrier()
          p1.close()

          chk(5)
          for mode in ("prompt", "sample"):
              PR = (mode == "prompt")
              SM = not PR
              if SM:
                  lay.close()
              p2 = ExitStack()

              def mk(cond, name, shape, dt=F32):
                  return sb(p2, name + mode[0], shape, dt) if cond else None

              wup = sb(p2, "wup" + mode[0], [128, 3, 4, D], BF16)
              wo = sb(p2, "wo" + mode[0], [128, 8, D], BF16)
              nb2 = 2 if PR else 1
              a1_l = [sb(p2, "a1b" + mode[0], [128, A1W], BF16) for _ in range(nb2)]
              xt_l = [sb(p2, "xt2" + mode[0], [128, D]) for _ in range(nb2)]
              a1, xt = a1_l[0], xt_l[0]
              den = mk(True, "den", [128, 8]); rg = mk(True, "rg", [128, 8])
              onsa = mk(True, "onsa", [128, 8, 64]); otmp = mk(True, "otmp", [128, 8, 64]); onsab = mk(True, "onsab", [128, 512], BF16)
              brT = mk(True, "brT", [128, 2, 4, 128], BF16)
              memo = mk(True, "memo", [128, 4, 128], BF16)
              hsum = mk(True, "hsum", [128, D]); htmp = mk(True, "htmp", [128, 512]); hb = mk(True, "hb", [128, D], BF16)
              hT2 = mk(True, "hT2", [128, 8, 128], BF16)
              qtp = mk(PR, "qtp", [128, 8, 128], BF16)
              ec = mk(PR, "ec", [128, 8, 128]); pc = mk(PR, "pc", [128, 8, 128]); pcb = mk(PR, "pcb", [128, 8, 128], BF16)
              pcT = mk(PR, "pcT", [128, 8, 128], BF16)
              imp = mk(PR, "imp", [128, 2, 128]); sc = mk(PR, "sc", [128, 2, 64]); sc2 = mk(PR, "sc2", [128, 2, 64])
              m8 = mk(PR, "m8", [128, 2, 16]); selm = mk(PR, "selm", [128, 2, 64])
              qa = mk(PR, "qa", [128, 8, 128], BF16); qta = mk(PR, "qta", [128, 8, 128], BF16)
              pT = [mk(PR, f"pT{i}", [128, 512], BF16) for i in range(3)]
              mqT = mk(PR, "mqT", [128, 4, 128], BF16)
              pmT = mk(PR, "pmT", [128, 8, 128], BF16)
              pg = [mk(SM, f"pg{i}", [128, 512]) for i in range(3)]
              qbf = mk(SM, "qbf", [128, 512]); mqbf = mk(SM, "mqbf", [128, 512])
              prod = [mk(SM, "prod0", [128, 1024]), mk(SM, "prod1", [128, 512])]
              ssel = mk(SM, "ssel", [128, NPG, 8]); esel = mk(SM, "esel", [128, NPG, 8]); emb = mk(SM, "emb", [128, NPG, 8], BF16)
              vsall = mk(SM, "vsall", [128, NPG + 1, 2, 65], BF16)
              pooleds = mk(SM, "pooleds", [128, 2, NBS])
              kcs = mk(SM, "kcs", [128, 256]); kcsb = mk(SM, "kcsb", [128, 128], BF16); kcTs = mk(SM, "kcTs", [128, NBS], BF16)
              vcs = mk(SM, "vcs", [128, NCK, 2, 65], BF16)
              qbt = mk(SM, "qbt", [128, 8, 128], BF16); qbm = mk(SM, "qbm", [128, 8, 128], BF16)
              pe8 = mk(SM, "pe8", [8, NBS]); pn8 = mk(SM, "pn8", [8, NBS]); d8 = mk(SM, "d8", [8, 4])
              scs = mk(SM, "scs", [2, NBS // 2]); scs2 = mk(SM, "scs2", [2, NBS // 2]); sels = mk(SM, "sels", [2, NBS // 2]); m8s = mk(SM, "m8s", [2, 16])
              pTs = mk(SM, "pTs", [128, NCK, 8], BF16)
              pnew = mk(SM, "pnew", [128, 2, 8], BF16); snew = mk(SM, "snew", [128, 2, 8])
              vnew = mk(SM, "vnew", [128, 2, 2, 65], BF16)
              wc = mk(SM, "wc", [128, 4, 256]); sw = mk(SM, "sw", [128, 4, 8]); pw = mk(SM, "pw", [128, 4, 8], BF16)
              vwa = mk(SM, "vwa", [128, 4, 2, 65], BF16)
              mc = mk(SM, "mc", [128, 2, 1024]); smm = mk(SM, "smm", [128, 8]); pmm = mk(SM, "pmm", [128, 2, 4], BF16)
              vmb = mk(SM, "vmb", [128, 2, 512], BF16)
              oall = mk(SM, "oall", [8, 3, NSEQ, 64]); o8t = mk(SM, "o8t", [8, 2, 65]); o8 = mk(SM, "o8", [8, 65])
              mall = mk(SM, "mall", [4, NSEQ, 128]); m4t = mk(SM, "m4t", [4, 4, 128]); m4 = mk(SM, "m4", [4, 128])
              od = mk(SM, "od", [8, 8, 64]); odm = mk(SM, "odm", [4, 4, 128])
              ob = mk(SM, "ob", [128, 3, 512])
              d8_rs_t = mk(SM, "d8rs", [128, 2])
              d8_rs = d8_rs_t.h if SM else None
              b0 = pst(p2, "b0" + mode[0], [128, 8, 128], BF16)
              b12 = pst(p2, "b12" + mode[0], [128, 8, 128])
              b34 = [pst(p2, f"b3{i}" + mode[0], [128, 512]) for i in range(2)]
              acc = pst(p2, "acc" + mode[0], [128, 2, 512])
              b7 = pst(p2, "b7" + mode[0], [128, 8, 128], BF16)

              for bi, src in enumerate((w_upn, w_upp, w_upm)):
                  dma("pool", wup[:, bi, :, :], src[l].rearrange("(c p) n -> p c n", p=128), W=[wup])
              dma("pool", wo[:], w_out[l].rearrange("(c p) n -> p c n", p=128), W=[wo])
              if PR:
                  memset("pool", sc[:], 0.0, [sc])
              else:
                  memset("pool", vsall[:, :, :, 64:65], 1.0, [vsall])
                  memset("pool", vnew[:, :, :, 64:65], 1.0, [vnew])
                  memset("pool", vwa[:, :, :, 64:65], 1.0, [vwa])
                  memset("pool", vcs[:, :, :, 64:65], 1.0, [vcs])
                  memset("pool", qbm[:], 0.0, [qbm])

              acc4 = acc[:, :, 0:260].rearrange("p k (g e) -> p k g e", e=65)

              def branch_out(br, i, first):
                  ts("dve", den[:].rearrange("p (k g) -> p k g", k=2), acc4[:, :, :, 64], 1e-30, None, ALU.max, R=[acc], W=[den])
                  if br == 2 and i < 4:
                      ts("dve", den[:], den[:], npad[:, i:i + 1], None, ALU.add, R=[den, cf], W=[den])
                  recip(den[:], den[:], [den], [den])
                  gv = nsag[:, i, :].rearrange("p (h b) -> p h b", b=3)[:, :, br]
                  tt("dve", rg[:], den[:], gv, ALU.mult, [den, nsag], [rg])
                  if "csw"[br] in os.environ.get("KOFF", ""):
                      ts("dve", rg[:], rg[:], 0.0, None, ALU.mult, R=[rg], W=[rg])
                  rgb = rg[:].rearrange("p (k g) -> p k g", k=2).unsqueeze(3).to_broadcast([128, 2, 4, 64])
                  dst = onsa if first else otmp
                  tt("dve", dst[:].rearrange("p (k g) d -> p k g d", k=2), acc4[:, :, :, 0:64], rgb, ALU.mult, [acc, rg], [dst])
                  if not first:
                      tt("dve", onsa[:], onsa[:], otmp[:], ALU.add, [onsa, otmp], [onsa])

              pti_ = [0]

              def prompt_attention(i):
                  for h in range(8):
                      tr(b0[0:64, h, :], a1[:, h * 64:(h + 1) * 64], identb, [a1, cb], [b0])
                  cp("act", qtp[0:64, :, :], b0[0:64, :, :], [b0], [qtp])
                  for h in range(8):
                      mm(b12[:, h, :], qtp[0:64, h, :], kcT2[0:64, h // 4, :], True, True, [qtp, kcT2], [b12])
                  act("act", ec[:], b12[:], AF.Exp, R=[b12], W=[ec])
                  tm = tcmp[:, 124 - 4 * i:124 - 4 * i + 128]
                  for h in range(8):
                      with relaxed():
                          ttr(pc[:, h, :], ec[:, h, :], tm, den[:, h:h + 1], [ec, cf], [pc, den])
                  ts("dve", den[:], den[:], 1e-30, None, ALU.max, R=[den], W=[den])
                  recip(den[:], den[:], [den], [den])
                  for kv in range(2):
                      for g in range(4):
                          h = kv * 4 + g
                          if g == 0:
                              ts("dve", imp[:, kv, :], pc[:, h, :], den[:, h:h + 1], None, ALU.mult, R=[pc, den], W=[imp])
                          else:
                              stt(imp[:, kv, :], pc[:, h, :], den[:, h:h + 1], imp[:, kv, :], ALU.mult, ALU.add, [pc, den, imp], [imp])
                  iv = imp[:, :, 0:NB].rearrange("p k (b r) -> p k b r", r=2)
                  tt("dve", sc[:, :, 0:NB // 2], iv[:, :, :, 0], iv[:, :, :, 1], ALU.add, [imp], [sc])
                  bs = bsel[:, 62 - 2 * i:126 - 2 * i]
                  tt("dve", sc2[:], sc[:], bs.unsqueeze(1).to_broadcast([128, 2, 64]), ALU.add, [sc, cf], [sc2])
                  memset("dve", sc2[:, :, 0:1], 1e9, [sc2], st=True)
                  for kv in range(2):
                      vmax(m8[:, kv, 0:8], sc2[:, kv, :], [sc2], [m8])
                      mrep(selm[:, kv, :], m8[:, kv, 0:8], sc2[:, kv, :], -2e9, [sc2, m8], [selm])
                      vmax(m8[:, kv, 8:16], selm[:, kv, :], [selm], [m8])
                      ts("dve", selm[:, kv, :], sc2[:, kv, :], m8[:, kv, 15:16], None, ALU.is_ge, R=[sc2, m8], W=[selm])
                      ts("dve", qa[:, 4 * kv:4 * kv + 4, 64:128], selm[:, kv, :].unsqueeze(1).to_broadcast([128, 4, 64]), 1.0, BIGM, ALU.subtract, ALU.mult, R=[selm], W=[qa])
                  cp("act", qa[:, :, 0:64], a1[:, 0:512].rearrange("p (h d) -> p h d", h=8), [a1], [qa])
                  for h in range(8):
                      tr(b0[:, h, :], qa[:, h, :], identb, [qa, cb], [b0])
                  cp("act", qta[:], b0[:], [b0], [qta])
                  cp("act", pcb[:], pc[:], [pc], [pcb])
                  for h in range(8):
                      tr(b7[:, h, :], pcb[:, h, :], identb, [pcb, cb], [b7])
                  cp("dve", pcT[:], b7[:], [b7], [pcT])
                  for h in range(8):
                      mm(acc[:, h // 4, (h % 4) * 65:(h % 4) * 65 + 65], pcT[:, h, :], vcaug[:, h // 4, :], True, True, [pcT, vcaug], [acc])
                  branch_out(0, i, True)
                  if i == 0:
                      chk(5.1)
                  for br in (1, 2):
                      jlo = 0 if br == 1 else max(0, i - 4)
                      for kv in range(2):
                          mm(acc[:, kv, 0:260], zerob[:, 0:128], zerob[:, 0:260], True, False, [zerob], [acc])
                          for j in range(jlo, i + 1):
                              ps_ = b34[pti_[0] % 2]
                              pt_ = pT[pti_[0] % 3]
                              pti_[0] += 1
                              if br == 1:
                                  mm(ps_[:], kselT[:, kv, j * 128:(j + 1) * 128], qta[:, 4 * kv:4 * kv + 4, :], True, True, [kselT.ds[j], kselT, qta], [ps_])
                              else:
                                  mm(ps_[:], kwinT[0:64, kv, j * 128:(j + 1) * 128], qta[0:64, 4 * kv:4 * kv + 4, :], True, True, [kwinT.ds[j], qta], [ps_])
                              act("act", pt_[:], ps_[:], AF.Exp, R=[ps_], W=[pt_])
                              if j == i or (br == 2 and j == i - 4):
                                  msk = tric_b if j == i else trib_b
                                  tt("dve", pt_[:].rearrange("p (g t) -> p g t", g=4), pt_[:].rearrange("p (g t) -> p g t", g=4),
                                     msk.unsqueeze(1).to_broadcast([128, 4, 128]), ALU.mult, [pt_, cb], [pt_])
                              vv = vsel if br == 1 else vwin
                              for g in range(4):
                                  mm(acc[:, kv, g * 65:g * 65 + 65], pt_[:, g * 128:(g + 1) * 128], vv[:, j, kv, :], False, (j == i and g == 3), [pt_, vv.ds[j], vv], [acc])
                      branch_out(br, i, False)
                      if i == 0:
                          chk(5.2 + 0.01 * br)
                  for h in range(4):
                      tr(b0[:, h, :], a1[:, 512 + h * 128:640 + h * 128], identb, [a1, cb], [b0])
                  cp("act", mqT[:], b0[:, 0:4, :], [b0], [mqT])
                  for h in range(4):
                      for mt in range(2):
                          mm(b12[:, h * 2 + mt, :], mkT[:, h, mt * 128:(mt + 1) * 128], mqT[:, h, :], True, True, [mkT, mqT], [b12])
                  act("act", pmT[:], b12[:], AF.Exp, R=[b12], W=[pmT])
                  for h in range(4):
                      for mt in range(2):
                          mm(acc[:, h // 2, (h % 2) * 256:(h % 2) * 256 + 129], pmT[:, h * 2 + mt, :], mva[:, mt, h, :], mt == 0, mt == 1, [pmT, mva], [acc])
                  accm = acc[:, :, :].rearrange("p k (g e) -> p (k g) e", e=256)
                  cp("dve", den[:, 0:4], accm[:, :, 128], [acc], [den])
                  recip(den[:, 0:4], den[:, 0:4], [den], [den])
                  tt("dve", memo[:], accm[:, :, 0:128], den[:, 0:4].unsqueeze(2).to_broadcast([128, 4, 128]), ALU.mult, [acc, den], [memo])
                  if i == 0:
                      chk(5.3)

              def sample_attention():
                  i = NT
                  for kv in range(2):
                      cp("pool", qbm[:, 4 * kv:4 * kv + 4, 64 * kv:64 * kv + 64], a1[:, 256 * kv:256 * kv + 256].rearrange("p (g d) -> p g d", g=4), [a1], [qbm])
                  for h in range(8):
                      tr(b0[:, h, :], qbm[:, h, :], identb, [qbm, cb], [b0])
                  cp("act", qbt[:], b0[:], [b0], [qbt])
                  cp("act", vnew[:, 0, :, 0:64], sampkv[:, 128:256].rearrange("p (k d) -> p k d", k=2), [sampkv], [vnew])
                  cp("act", vnew[:, 1, :, 0:64], sampkv[:, 384:512].rearrange("p (k d) -> p k d", k=2), [sampkv], [vnew])
                  pcs = b12
                  pcs_v = b12[:].rearrange("p a b -> p (a b)")[:, 0:2 * NBS].rearrange("p (a b) -> p a b", a=2)
                  for s_ in range(NSEQ):
                      mm(b34[0][:], selall_b[:, s_, :], a1[:, 0:512], True, True, [cb, a1], [b34[0]])
                      cp("act", qbf[:], b34[0][:], [b34[0]], [qbf])
                      mm(b34[1][:], selall_b[:, s_, :], a1[:, 512:1024], True, True, [cb, a1], [b34[1]])
                      cp("act", mqbf[:], b34[1][:], [b34[1]], [mqbf])
                      qv = qbf[:].rearrange("p (k g d) -> p k g d", k=2, g=4)

                      def dots(dst, ksrc, pi):
                          pr = prod[pi % 2]
                          e = "dve" if pi % 2 == 0 else "pool"
                          tt(e, pr[:, 0:512].rearrange("p (k g d) -> p k g d", k=2, g=4),
                             ksrc.rearrange("p (k d) -> p k d", k=2).unsqueeze(2).to_broadcast([128, 2, 4, 64]), qv, ALU.mult, [qbf] + dots_R, [pr])
                          red(dst, pr[:, 0:512].rearrange("p (h d) -> p h d", d=64), [pr], dots_W)

                      for j in range(NPG):
                          pgt = pg[j % 3]
                          col = s_ * NPG + j
                          fw.dma("pool", lambda E, o_=pgt[:], i_=cnsa_flat, x_=pidx[:, l, col:col + 1]: E.indirect_dma_start(
                              out=o_, out_offset=None, in_=i_,
                              in_offset=bass.IndirectOffsetOnAxis(ap=x_, axis=0)), [pidx.d], [pgt.d])
                          mm(pcs_v[:, 0, 4 * j:4 * j + 4], pgt[:, 0:128], pmf, True, True, [pgt, cf], [b12])
                          mm(pcs_v[:, 1, 4 * j:4 * j + 4], pgt[:, 128:256], pmf, True, True, [pgt, cf], [b12])
                          dots_R = [pgt]; dots_W = [ssel]
                          dots(ssel[:, j, :], pgt[:, 256:384], j)
                          cp("act", vsall[:, j, :, 0:64], pgt[:, 384:512].rearrange("p (k d) -> p k d", k=2), [pgt], [vsall])
                      act("act", esel[:], ssel[:], AF.Exp, R=[ssel], W=[esel])
                      for a_ in range(2):
                          ts("dve", pooleds[:, a_, :], pcs_v[:, a_, :], pemT[:, a_:a_ + 1], None, ALU.add, R=[b12, pemT], W=[pooleds])
                      for ck in range(NCK):
                          mm(b34[0][0:CB, 0:128], pooleds[:, 0, ck * CB:(ck + 1) * CB], wkbd[:, 0, :], True, True, [pooleds, wkbd], [b34[0]])
                          mm(b34[0][0:CB, 128:256], pooleds[:, 1, ck * CB:(ck + 1) * CB], wkbd[:, 1, :], True, True, [pooleds, wkbd], [b34[0]])
                          cp("act", kcs[0:CB, :], b34[0][0:CB, 0:256], [b34[0]], [kcs])
                          tt("dve", prod[0][0:CB, 0:128], kcs[0:CB, 0:128], kcs[0:CB, 0:128], ALU.mult, [kcs], [prod[0]])
                          red(d8_rs[0:CB, 0:2], prod[0][0:CB, 0:128].rearrange("p (h d) -> p h d", d=64), [prod[0]], [d8_rs_t])
                          ts("dve", d8_rs[0:CB, 0:2], d8_rs[0:CB, 0:2], 1.0 / 64, EPS, ALU.mult, ALU.add, R=[d8_rs_t], W=[d8_rs_t])
                          act("act", d8_rs[0:CB, 0:2], d8_rs[0:CB, 0:2], AF.Sqrt, R=[d8_rs_t], W=[d8_rs_t])
                          recip(d8_rs[0:CB, 0:2], d8_rs[0:CB, 0:2], [d8_rs_t], [d8_rs_t])
                          for kv in range(2):
                              stt(kcsb[0:CB, kv * 64:(kv + 1) * 64], kcs[0:CB, kv * 64:(kv + 1) * 64], d8_rs[0:CB, kv:kv + 1], gkc[0:CB, :], ALU.mult, ALU.mult, [kcs, d8_rs_t, gkc], [kcsb])
                          tr(b7[:, 0, 0:CB], kcsb[0:CB, :], identb[0:CB, 0:CB], [kcsb, cb], [b7])
                          cp("act", kcTs[:, ck * CB:(ck + 1) * CB], b7[:, 0, 0:CB], [b7], [kcTs])
                          cp("dve", vcs[0:CB, ck, :, 0:64], kcs[0:CB, 128:256].rearrange("p (k d) -> p k d", k=2), [kcs], [vcs])
                      mm(b34[1][0:8, 0:NBS], qbt[:, :, s_], kcTs[:], True, True, [qbt, kcTs], [b34[1]])
                      act("act", pe8[:], b34[1][0:8, 0:NBS], AF.Exp, accum=d8[:, 0:1], R=[b34[1]], W=[pe8, d8])
                      recip(d8[:, 1:2], d8[:, 0:1], [d8], [d8])
                      ts("dve", pn8[:], pe8[:], d8[:, 1:2], None, ALU.mult, R=[pe8, d8], W=[pn8])
                      mm(b34[0][0:2, 0:NBS], g8[0:8, :], pn8[:], True, True, [cf, pn8], [b34[0]])
                      iv = b34[0][0:2, 0:NBS].rearrange("p (b r) -> p b r", r=2)
                      cp("act", scs2[:], iv[:, :, 0], [b34[0]], [scs2])
                      tt("dve", scs[:], scs2[:], iv[:, :, 1], ALU.add, [scs2, b34[0]], [scs])
                      memset("dve", scs[:, 0:1], 1e9, [scs])
                      memset("dve", scs[:, NBS // 2 - 1:NBS // 2], 1e9, [scs])
                      vmax(m8s[:, 0:8], scs[:], [scs], [m8s])
                      mrep(scs2[:], m8s[:, 0:8], scs[:], -2e9, [scs, m8s], [scs2])
                      vmax(m8s[:, 8:16], scs2[:], [scs2], [m8s])
                      ts("dve", sels[:], scs[:], m8s[:, 14:15], None, ALU.is_ge, R=[scs, m8s], W=[sels])
                      for ck in range(NCK):
                          tr(b34[1][0:CB, 256 + ck * 8:264 + ck * 8], pn8[:, ck * CB:(ck + 1) * CB], identf[0:8, 0:8], [pn8, cf], [b34[1]])
                          cp("act", pTs[0:CB, ck, :], b34[1][0:CB, 256 + ck * 8:264 + ck * 8], [b34[1]], [pTs])
                      for ck in range(NCK):
                          mm(acc[0:8, 0, 0:130], pTs[0:CB, ck, :], vcs[0:CB, ck, :, :].rearrange("p k e -> p (k e)"), ck == 0, ck == NCK - 1, [pTs, vcs], [acc])
                      pick8(0, s_, False)
                      sv = sels[:].rearrange("p (j r) -> p j r", r=2)
                      for kv in range(2):
                          mm(b34[1][:, kv * NPG:(kv + 1) * NPG], inde[0:2, kv, 0, :], sv[:, :, 0], True, False, [cf, sels], [b34[1]])
                          mm(b34[1][:, kv * NPG:(kv + 1) * NPG], inde[0:2, kv, 1, :], sv[:, :, 1], False, True, [cf, sels], [b34[1]])
                      mv = b34[1][:, 0:2 * NPG].rearrange("p (k j) -> p j k", k=2).unsqueeze(3).to_broadcast([128, NPG, 2, 4])
                      tt("dve", emb[:].rearrange("p j (k g) -> p j k g", k=2), esel[:].rearrange("p j (k g) -> p j k g", k=2), mv, ALU.mult, [esel, b34[1]], [emb])
                      dots_R = [sampkv]; dots_W = [snew]
                      dots(snew[:, 0, :], sampkv[:, 0:128], 0)
                      dots(snew[:, 1, :], sampkv[:, 256:384], 1)
                      act("act", snew[:], snew[:], AF.Exp, R=[snew], W=[snew])
                      ts("dve", pnew[:], snew[:], rowsel[:, s_:s_ + 1], None, ALU.mult, R=[snew, cf], W=[pnew])
                      for j in range(NPG):
                          mm(acc[0:8, 0, 0:130], emb[:, j, :], vsall[:, j, :, :].rearrange("p k e -> p (k e)"), j == 0, False, [emb, vsall], [acc])
                      mm(acc[0:8, 0, 0:130], pnew[:, 0, :], vnew[:, 0, :, :].rearrange("p k e -> p (k e)"), False, True, [pnew, vnew], [acc])
                      pick8(1, s_, True)
                      dma("sp", wc[:], cwin[l, s_].rearrange("(kt p) c -> p kt c", p=128), W=[wc])
                      for kt in range(4):
                          dots_R = [wc]; dots_W = [sw]
                          dots(sw[:, kt, :], wc[:, kt, 0:128], kt)
                      act("act", sw[:], sw[:], AF.Exp, R=[sw], W=[sw])
                      ts("dve", pw[:, 0, :], sw[:, 0, :], wmask[:, 0:1], None, ALU.mult, R=[sw, cf], W=[pw])
                      cp("dve", pw[:, 1:4, :], sw[:, 1:4, :], [sw], [pw])
                      cp("act", vwa[:, :, :, 0:64], wc[:, :, 128:256].rearrange("p t (k d) -> p t k d", k=2), [wc], [vwa])
                      for kt in range(4):
                          mm(acc[0:8, 0, 0:130], pw[:, kt, :], vwa[:, kt, :, :].rearrange("p k e -> p (k e)"), kt == 0, False, [pw, vwa], [acc])
                      mm(acc[0:8, 0, 0:130], pnew[:, 1, :], vnew[:, 1, :, :].rearrange("p k e -> p (k e)"), False, True, [pnew, vnew], [acc])
                      pick8(2, s_, True)
                      dma("sp", mc[:], cmem[l, s_].rearrange("(mt p) c -> p mt c", p=128), W=[mc])
                      tt("dve", prod[0][:].rearrange("p (t c) -> p t c", t=2), mc[:, :, 0:512], mqbf[:].unsqueeze(1).to_broadcast([128, 2, 512]), ALU.mult, [mc, mqbf], [prod[0]])
                      red(smm[:], prod[0][:].rearrange("p (h d) -> p h d", d=128), [prod[0]], [smm])
                      act("act", pmm[:].rearrange("p t h -> p (t h)"), smm[:], AF.Exp, R=[smm], W=[pmm])
                      cp("pool", vmb[:], mc[:, :, 512:1024], [mc], [vmb])
                      for mt in range(2):
                          mm(acc[0:4, 1, :], pmm[:, mt, :], vmb[:, mt, :], mt == 0, mt == 1, [pmm, vmb], [acc])
                      for mt in range(2):
                          mm(b34[0][0:4, 0:1], pmm[:, mt, :], onesb[:, 0:1], mt == 0, mt == 1, [pmm, cb], [b34[0]])
                      tt("dve", m4t[:], acc[0:4, 1, :].rearrange("p (a d) -> p a d", a=4), identf[0:4, 0:4].unsqueeze(2).to_broadcast([4, 4, 128]), ALU.mult, [acc, cf], [m4t])
                      red(m4[:], m4t[:].rearrange("p a d -> p d a"), [m4t], [m4])
                      recip(d8[0:4, 2:3], b34[0][0:4, 0:1], [b34[0].d], [d8.d])
                      ts("dve", mall[:, s_, :], m4[:], d8[0:4, 2:3], None, ALU.mult, R=[m4, d8], W=[mall])
                  for br in range(3):
                      for s_ in range(NSEQ):
                          tt("dve", od[:], oall[:, br, s_, :].unsqueeze(1).to_broadcast([8, 8, 64]), identf[0:8, 0:8].unsqueeze(2).to_broadcast([8, 8, 64]), ALU.mult, [oall, cf], [od])
                          mm(b12[:, 0:4, :].rearrange("p a b -> p (a b)"), sr_f[0:8, s_, :], od[:].rearrange("p a b -> p (a b)"), True, True, [cf, od], [b12])
                          if s_ == 0:
                              cp("act", ob[:, br, :], b12[:, 0:4, :].rearrange("p a b -> p (a b)"), [b12], [ob])
                          else:
                              tt("dve", ob[:, br, :], ob[:, br, :], b12[:, 0:4, :].rearrange("p a b -> p (a b)"), ALU.add, [ob, b12], [ob])
                  for br in range(3):
                      gv = nsag[:, i, :].rearrange("p (h b) -> p h b", b=3)[:, :, br].unsqueeze(2).to_broadcast([128, 8, 64])
                      dst = onsa if br == 0 else otmp
                      tt("dve", dst[:], ob[:, br, :].rearrange("p (h d) -> p h d", h=8), gv, ALU.mult, [ob, nsag], [dst])
                      if br > 0:
                          tt("dve", onsa[:], onsa[:], otmp[:], ALU.add, [onsa, otmp], [onsa])
                  for s_ in range(NSEQ):
                      tt("dve", odm[:], mall[:, s_, :].unsqueeze(1).to_broadcast([4, 4, 128]), identf[0:4, 0:4].unsqueeze(2).to_broadcast([4, 4, 128]), ALU.mult, [mall, cf], [odm])
                      mm(b12[:, 4:8, :].rearrange("p a b -> p (a b)"), sr_f[0:4, s_, :], odm[:].rearrange("p a b -> p (a b)"), True, True, [cf, odm], [b12])
                      if s_ == 0:
                          cp("act", hsum[:, 0:512], b12[:, 4:8, :].rearrange("p a b -> p (a b)"), [b12], [hsum])
                      else:
                          tt("dve", hsum[:, 0:512], hsum[:, 0:512], b12[:, 4:8, :].rearrange("p a b -> p (a b)"), ALU.add, [hsum, b12], [hsum])
                  cp("dve", memo[:].rearrange("p h d -> p (h d)"), hsum[:, 0:512], [hsum], [memo])

              def pick8(br, s_, normalize):
                  tt("dve", o8t[:], acc[0:8, 0, 0:130].rearrange("p (k e) -> p k e", k=2), g8[0:8, :].unsqueeze(2).to_broadcast([8, 2, 65]), ALU.mult, [acc, cf], [o8t])
                  tt("dve", o8[:], o8t[:, 0, :], o8t[:, 1, :], ALU.add, [o8t], [o8])
                  if normalize:
                      recip(d8[:, 3:4], o8[:, 64:65], [o8.d], [d8.d])
                      ts("dve", oall[:, br, s_, :], o8[:, 0:64], d8[:, 3:4], None, ALU.mult, R=[o8, d8], W=[oall])
                  else:
                      cp("dve", oall[:, br, s_, :], o8[:, 0:64], [o8], [oall])

              for i in (range(NT) if PR else [NT]):
                  _b = i % nb2
                  a1, xt = a1_l[_b], xt_l[_b]
                  if i == 0 or not PR:
                      dma("sp", a1[:], a1s[i], R=[a1s_d[i], a1s_d2[i]], W=[a1])
                      dma("sp", xt[:], xsrc[i * 128:(i + 1) * 128, :], R=[xs_d[i]], W=[xt])
                  if PR and i + 1 < NT:
                      dma("sp", a1_l[1 - _b][:], a1s[i + 1], R=[a1s_d[i + 1], a1s_d2[i + 1]], W=[a1_l[1 - _b]])
                      dma("sp", xt_l[1 - _b][:], xsrc[(i + 1) * 128:(i + 2) * 128, :], R=[xs_d[i + 1]], W=[xt_l[1 - _b]])
                  if i < NT:
                      prompt_attention(i)
                  else:
                      sample_attention()
                  cp("act", onsab[:], onsa[:].rearrange("p h d -> p (h d)"), [onsa], [onsab])
                  for c in range(4):
                      tr(b0[:, c, :], onsab[:, c * 128:(c + 1) * 128], identb, [onsab, cb], [b0])
                      tr(b0[:, 4 + c, :], memo[:, c, :], identb, [memo, cb], [b0])
                  cp("act", brT[:].rearrange("p a c t -> p (a c) t"), b0[:], [b0], [brT])
                  pyT = a1[:, 1024:1536].rearrange("p (g t) -> p g t", g=4)
                  for n in range(2):
                      for bi in range(3):
                          p_ = b34[bi % 2]
                          for c in range(4):
                              lt = brT[:, 0, c, :] if bi == 0 else (pyT[:, c, :] if bi == 1 else brT[:, 1, c, :])
                              mm(p_[:], lt, wup[:, bi, c, n * 512:(n + 1) * 512], c == 0, c == 3, [brT, a1, wup], [p_])
                          mgv = a1[:, 1536 + bi * 1024 + n * 512:1536 + bi * 1024 + (n + 1) * 512]
                          if bi == 0:
                              tt("dve", hsum[:, n * 512:(n + 1) * 512], p_[:], mgv, ALU.mult, [p_, a1], [hsum])
                          else:
                              tt("dve", htmp[:], p_[:], mgv, ALU.mult, [p_, a1], [htmp])
                              if "xpm"[bi] in os.environ.get("KOFF", ""):
                                  ts("dve", htmp[:], htmp[:], 0.0, None, ALU.mult, R=[htmp], W=[htmp])
                              if bi == 1:
                                  tt("dve", hsum[:, n * 512:(n + 1) * 512], hsum[:, n * 512:(n + 1) * 512], htmp[:], ALU.add, [hsum, htmp], [hsum])
                              else:
                                  tt("dve", hb[:, n * 512:(n + 1) * 512], hsum[:, n * 512:(n + 1) * 512], htmp[:], ALU.add, [hsum, htmp], [hb])
                  for c in range(8):
                      tr(b7[:, c, :], hb[:, c * 128:(c + 1) * 128], identb, [hb, cb], [b7])
                  cp("act", hT2[:], b7[:], [b7], [hT2])
                  for n in range(2):
                      p_ = b34[n]
                      for c in range(8):
                          mm(p_[:], hT2[:, c, :], wo[:, c, n * 512:(n + 1) * 512], c == 0, c == 7, [hT2, wo], [p_])
                      tt("dve", xt[:, n * 512:(n + 1) * 512], xt[:, n * 512:(n + 1) * 512], p_[:], ALU.add, [xt, p_], [xt])
                  dma("sp", xs[i * 128:(i + 1) * 128, :], xt[:], R=[xt], W=[xs_d[i]])
                  chk(5.4 + 0.01 * i)
              fw.barrier()
              p2.close()
          lays.close()

          chk(8)
          p3 = ExitStack()
          wgu = sb(p3, "wgu", [128, 8, 2 * DFF], BF16)
          wdn = sb(p3, "wdn", [128, 22, D], BF16)
          gfT = sb(p3, "gfT", [128, 8])
          x4 = [sb(p3, f"x4_{k}", [128, D]) for k in range(4)]
          xb3 = sb(p3, "xb3", [128, D], BF16); junk3 = xb3; ss3 = sb(p3, "ss3", [128, 1])
          hT3 = sb(p3, "hT3", [128, 8, 512], BF16)
          aT = sb(p3, "aT", [128, 22, 512], BF16)
          sg = [sb(p3, f"sg{k}", [128, 512], BF16) for k in range(2)]
          ptb3 = pst(p3, "ptb3", [128, 8, 128], BF16)
          pgk = [pst(p3, f"pgk{k}", [128, 512]) for k in range(2)]
          puk = [pst(p3, f"puk{k}", [128, 512]) for k in range(2)]
          pdn = [pst(p3, f"pdn{k}", [128, 512]) for k in range(2)]
          for c in range(8):
              dma("pool", wgu[:, c, :], w_gu[l, c * 128:(c + 1) * 128, :], W=[wgu])
          dma("pool", wdn[:], w_dn[l].rearrange("(c p) n -> p c n", p=128), W=[wdn])
          dma_nc("sp", gfT[:], w_ffng[l].rearrange("(c p) -> p c", p=128), W=[gfT])
          st = 0
          kk = 0
          while st < NTT:
              nt_ = min(4, NT - st) if st < NT else 1
              TW = nt_ * 128
              for k in range(nt_):
                  i = st + k
                  dma("sp", x4[k][:], xs[i * 128:(i + 1) * 128, :], R=[xs_d[i]], W=[x4[k]])
                  act("act", junk3[:], x4[k][:], AF.Square, accum=ss3[:], R=[x4[k]], W=[junk3, ss3])
                  ts("dve", ss3[:], ss3[:], 1.0 / D, EPS, ALU.mult, ALU.add, R=[ss3], W=[ss3])
                  rsqrt_chain(ss3)
                  act("act", xb3[:], x4[k][:], AF.Copy, scale=ss3[:, 0:1], R=[x4[k], ss3], W=[xb3])
                  for c in range(8):
                      tr(ptb3[:, c, :], xb3[:, c * 128:(c + 1) * 128], identb, [xb3, cb], [ptb3])
                  for c in range(8):
                      with relaxed():
                          ts("dve", hT3[:, c, k * 128:(k + 1) * 128], ptb3[:, c, :], gfT[:, c:c + 1], None, ALU.mult, R=[ptb3, gfT], W=[hT3])
              for fc in range(22):
                  pg_, pu_, sg_ = pgk[kk % 2], puk[kk % 2], sg[kk % 2]
                  kk += 1
                  for c in range(8):
                      mm(pg_[:, 0:TW], wgu[:, c, fc * 128:(fc + 1) * 128], hT3[:, c, 0:TW], c == 0, c == 7, [wgu, hT3], [pg_])
                  for c in range(8):
                      mm(pu_[:, 0:TW], wgu[:, c, DFF + fc * 128:DFF + (fc + 1) * 128], hT3[:, c, 0:TW], c == 0, c == 7, [wgu, hT3], [pu_])
                  act("act", sg_[:, 0:TW], pg_[:, 0:TW], AF.Silu, R=[pg_], W=[sg_])
                  with relaxed():
                      tt("dve", aT[:, fc, 0:TW], pu_[:, 0:TW], sg_[:, 0:TW], ALU.mult, [pu_, sg_], [aT])
              for k in range(nt_):
                  i = st + k
                  for n in range(2):
                      p_ = pdn[n]
                      for fc in range(22):
                          mm(p_[:], aT[:, fc, k * 128:(k + 1) * 128], wdn[:, fc, n * 512:(n + 1) * 512], fc == 0, fc == 21, [aT, wdn], [p_])
                      tt("dve", x4[k][:, n * 512:(n + 1) * 512], x4[k][:, n * 512:(n + 1) * 512], p_[:], ALU.add, [x4[k], p_], [x4[k]])
                  if l == NL - 1:
                      outdma("sp", o_y[i * 128:(i + 1) * 128, :], x4[k][:], R=[x4[k]])
                  else:
                      dma("sp", xs[i * 128:(i + 1) * 128, :], x4[k][:], R=[x4[k]], W=[xs_d[i]])
              st += nt_
          fw.barrier()
          p3.close()

    except _Stop:
        pass
    print("OPCOUNT", _opc[0])
    fw.finish_deps = outdeps
    for d in outdeps:
        for o in [d.w] + list(d.r):
            w = fw._need("sp", o)
            if w:
                fw.ops["sp"].append(("wait", w))
    fw.emit()
    return nc


WNAMES = ["attn_norm_g", "w_in", "nsa_q_g", "nsa_kc_g", "nsa_ks_g", "nsa_kw_g", "cmp_pe_k", "cmp_pe_v", "cmp_wk", "cmp_wv",
          "w_pool", "pool_scale", "mem_norm_g", "w_mem_kv", "mem_q_g", "mem_k_g", "w_up_nsa", "w_up_pool", "w_up_mem",
          "w_out", "ffn_norm_g", "w_gate_up", "w_down"]


def kernel(x_prompt, x_sample, mem_prompt, cache_nsa_kv, cache_win_kv, state_pool, cache_mem_kv, page_table, **w):
    x_prompt = np.asarray(x_prompt); x_sample = np.asarray(x_sample)
    B, S, _ = x_prompt.shape
    NL = cache_nsa_kv.shape[0]
    NPHYS = cache_nsa_kv.shape[1]
    NPG = page_table.shape[1]
    NT = S // 128
    NTT = NT + 1
    ncore = 8
    consts = make_consts(NT)
    CW = consts.shape[1]
    e64 = (np.arange(64)[:, None] == (np.arange(S)[None, :] // 64) % 64).astype(np.float32)
    nc = build(NL, NT, NPG, NPHYS, CW)
    cn = np.ascontiguousarray(np.asarray(cache_nsa_kv)).reshape(NL, NPHYS * 128, 512)
    in_maps = []
    for c in range(ncore):
        b = c % B
        xin = np.zeros((NTT * 128, D), np.float32)
        xin[:S] = x_prompt[b]
        xin[S:S + NSEQ] = x_sample[NSEQ * c:NSEQ * c + NSEQ, 0]
        m = {"xin": xin, "memin": np.ascontiguousarray(mem_prompt[b]), "cnsa": cn,
             "cwin": np.ascontiguousarray(np.asarray(cache_win_kv)[:, NSEQ * c:NSEQ * c + NSEQ]).reshape(NL, NSEQ, 512, 256),
             "spool": np.ascontiguousarray(np.asarray(state_pool)[:, NSEQ * c:NSEQ * c + NSEQ]),
             "cmem": np.ascontiguousarray(np.asarray(cache_mem_kv)[:, NSEQ * c:NSEQ * c + NSEQ]).reshape(NL, NSEQ, 256, 1024),
             "ptab": np.ascontiguousarray(np.asarray(page_table)[NSEQ * c:NSEQ * c + NSEQ]).reshape(1, NSEQ * NPG).astype(np.int32),
             "consts": consts, "e64": e64}
        for n in WNAMES:
            m[n] = np.ascontiguousarray(np.asarray(w[n], dtype=np.float32))
        in_maps.append(m)
    res = run_bass_kernel_spmd(nc, in_maps, core_ids=list(range(ncore)))
    R = res.results
    DB = x_sample.shape[0]
    y_p = np.stack([R[b]["o_y"][:S] for b in range(B)])
    y_s = np.concatenate([R[c]["o_y"][S:S + NSEQ] for c in range(ncore)])[:, None, :]
    nsa_p = np.stack([R[b]["o_nsa"][:, :S] for b in range(B)], axis=1).reshape(NL, B, S, 4, 2, 64)
    nsa_s = np.concatenate([R[c]["o_nsa"][:, S:S + NSEQ] for c in range(ncore)], axis=1).reshape(NL, DB, 1, 4, 2, 64)
    win_p = np.stack([R[b]["o_winp"] for b in range(B)], axis=1).reshape(NL, B, 512, 2, 2, 64)
    win_s = np.concatenate([R[c]["o_wins"] for c in range(ncore)], axis=1).reshape(NL, DB, 512, 2, 2, 64)
    pool_p = np.stack([R[b]["o_poolp"] for b in range(B)], axis=1)
    pool_s = np.concatenate([R[c]["o_pools"] for c in range(ncore)], axis=1)
    mem_p = np.stack([R[b]["o_memkv"] for b in range(B)], axis=1).reshape(NL, B, 256, 2, 4, 128)
    return (y_p.astype(np.float32), y_s.astype(np.float32), nsa_p, nsa_s, win_p, win_s, pool_p, pool_s, mem_p)
```
